# Optimizing a Trainium2 kernel written in Bass

```python
import math
import jax, jax.numpy as jnp
from jax import lax
import numpy as np

D_MODEL = 1024
BATCH = 2
SEQ = 8192
DEPTH = 2

D_SSM = 384
SSM_GROUP = 16
N_SSM_GROUPS = D_SSM // SSM_GROUP
SSM_STATE = 64
POOL_WINDOWS = (2, 4, 8, 16)
N_POOL_GROUPS = len(POOL_WINDOWS)
POOL_GROUP = 64
D_POOL = N_POOL_GROUPS * POOL_GROUP
MAX_WINDOW = max(POOL_WINDOWS)
SGU_HEADS = 6
SGU_HEAD_DIM = 64
D_SGU = SGU_HEADS * SGU_HEAD_DIM
CHUNK = 128
D_MIX = D_SSM + D_POOL + D_SGU
D_IN = D_SSM + D_POOL + 2 * D_SGU
D_FF = ((8 * D_MODEL // 3 + 255) // 256) * 256
EPS = 1e-6

kernel_name = "hybrid_s5_pool_sgu_trunk"


def rms_norm(x, g):
    xf = x.astype(jnp.float32)
    y = xf * lax.rsqrt(jnp.mean(xf * xf, axis=-1, keepdims=True) + EPS)
    return (y * g.astype(jnp.float32)).astype(x.dtype)


def s5_mixer(u, A_re, A_im, log_dt, B_re, B_im, C_re, C_im, D_skip, w_glu, b_glu):
    f32 = jnp.float32
    bsz, seq, _ = u.shape
    uf = u.astype(f32).reshape(bsz, seq, N_SSM_GROUPS, SSM_GROUP)
    A_re = A_re.astype(f32); A_im = A_im.astype(f32)
    dt = jnp.exp(log_dt.astype(f32))[:, None]
    mag = jnp.exp(A_re * dt)
    ar = mag * jnp.cos(A_im * dt)
    ai = mag * jnp.sin(A_im * dt)
    den = A_re * A_re + A_im * A_im
    f_re = ((ar - 1.0) * A_re + ai * A_im) / den
    f_im = (ai * A_re - (ar - 1.0) * A_im) / den
    B_re = B_re.astype(f32); B_im = B_im.astype(f32)
    Bb_re = f_re[..., None] * B_re - f_im[..., None] * B_im
    Bb_im = f_re[..., None] * B_im + f_im[..., None] * B_re
    bu_re = jnp.einsum('bsgc,gnc->bsgn', uf, Bb_re)
    bu_im = jnp.einsum('bsgc,gnc->bsgn', uf, Bb_im)
    a_re = jnp.broadcast_to(ar, bu_re.shape)
    a_im = jnp.broadcast_to(ai, bu_re.shape)

    def combine(left, right):
        a1r, a1i, b1r, b1i = left
        a2r, a2i, b2r, b2i = right
        return (a1r * a2r - a1i * a2i,
                a1r * a2i + a1i * a2r,
                a2r * b1r - a2i * b1i + b2r,
                a2r * b1i + a2i * b1r + b2i)

    _, _, h_re, h_im = lax.associative_scan(combine, (a_re, a_im, bu_re, bu_im), axis=1)
    y = (jnp.einsum('bsgn,gcn->bsgc', h_re, C_re.astype(f32))
         - jnp.einsum('bsgn,gcn->bsgc', h_im, C_im.astype(f32)))
    y = y.reshape(bsz, seq, D_SSM) + D_skip.astype(f32) * uf.reshape(bsz, seq, D_SSM)
    g = jax.nn.gelu(y)
    out = g * jax.nn.sigmoid(g @ w_glu.astype(f32) + b_glu.astype(f32))
    return out.astype(u.dtype)


def pool_mixer(u, w_pool, pool_scale):
    f32 = jnp.float32
    bsz, seq, _ = u.shape
    uf = u.astype(f32).reshape(bsz, seq, N_POOL_GROUPS, POOL_GROUP)
    csum = jnp.cumsum(uf, axis=1)
    cpad = jnp.pad(csum, ((0, 0), (MAX_WINDOW, 0), (0, 0), (0, 0)))
    pos = jnp.arange(1, seq + 1)
    means = []
    for g, w in enumerate(POOL_WINDOWS):
        lagged = cpad[:, MAX_WINDOW - w:MAX_WINDOW - w + seq, g]
        count = jnp.minimum(pos, w).astype(f32)[None, :, None]
        means.append((csum[:, :, g] - lagged) / count)
    pooled = jnp.stack(means, axis=2) - uf
    mixed = jnp.einsum('bsgc,gcd->bsgd', pooled, w_pool.astype(f32))
    out = mixed.reshape(bsz, seq, D_POOL) * pool_scale.astype(f32)
    return out.astype(u.dtype)


def sgu_mixer(zu, zv, ln_g, ln_b, w_spatial, b_spatial):
    f32 = jnp.float32
    bsz, seq, _ = zu.shape
    n_chunks = seq // CHUNK
    u = jax.nn.gelu(zu.astype(f32))
    v = jax.nn.gelu(zv.astype(f32))
    mu = jnp.mean(v, axis=-1, keepdims=True)
    var = jnp.mean(jnp.square(v - mu), axis=-1, keepdims=True)
    v = (v - mu) * lax.rsqrt(var + EPS) * ln_g.astype(f32) + ln_b.astype(f32)
    vh = v.reshape(bsz, n_chunks, CHUNK, SGU_HEADS, SGU_HEAD_DIM)
    mask = jnp.tril(jnp.ones((CHUNK, CHUNK), dtype=bool))
    ws = jnp.where(mask[None], w_spatial.astype(f32), 0.0)
    mixed = jnp.einsum('hts,bnshd->bnthd', ws, vh)
    mixed = mixed + jnp.transpose(b_spatial.astype(f32))[None, None, :, :, None]
    out = u * mixed.reshape(bsz, seq, D_SGU)
    return out.astype(zu.dtype)


def setup_inputs(seed: int = 0) -> dict:
    key = jax.random.key(seed)
    ks = jax.random.split(key, 24)
    f32 = jnp.float32
    nrm = lambda k, shape, s: (jax.random.normal(k, shape, f32) * s)
    x = jax.random.normal(ks[0], (BATCH, SEQ, D_MODEL), f32)
    g_mix = 1.0 + nrm(ks[1], (DEPTH, D_MODEL), 0.02)
    w_in = nrm(ks[2], (DEPTH, D_MODEL, D_IN), D_MODEL ** -0.5)
    A_re = -0.5 + nrm(ks[3], (DEPTH, N_SSM_GROUPS, SSM_STATE), 0.01)
    A_im = (jnp.pi * jnp.arange(SSM_STATE, dtype=f32))[None, None, :] + nrm(ks[4], (DEPTH, N_SSM_GROUPS, SSM_STATE), 0.01)
    log_dt = jax.random.uniform(ks[5], (DEPTH, N_SSM_GROUPS), f32, math.log(1e-3), math.log(1e-1))
    B_re = nrm(ks[6], (DEPTH, N_SSM_GROUPS, SSM_STATE, SSM_GROUP), (2 * SSM_GROUP) ** -0.5)
    B_im = nrm(ks[7], (DEPTH, N_SSM_GROUPS, SSM_STATE, SSM_GROUP), (2 * SSM_GROUP) ** -0.5)
    C_re = nrm(ks[8], (DEPTH, N_SSM_GROUPS, SSM_GROUP, SSM_STATE), (2 * SSM_STATE) ** -0.5)
    C_im = nrm(ks[9], (DEPTH, N_SSM_GROUPS, SSM_GROUP, SSM_STATE), (2 * SSM_STATE) ** -0.5)
    D_skip = nrm(ks[10], (DEPTH, D_SSM), 1.0)
    w_glu = nrm(ks[11], (DEPTH, D_SSM, D_SSM), D_SSM ** -0.5)
    b_glu = nrm(ks[12], (DEPTH, D_SSM), 0.01)
    w_pool = nrm(ks[13], (DEPTH, N_POOL_GROUPS, POOL_GROUP, POOL_GROUP), POOL_GROUP ** -0.5)
    pool_scale = 1.0 + nrm(ks[14], (DEPTH, D_POOL), 0.02)
    sgu_ln_g = 1.0 + nrm(ks[15], (DEPTH, D_SGU), 0.02)
    sgu_ln_b = nrm(ks[16], (DEPTH, D_SGU), 0.01)
    w_spatial = nrm(ks[17], (DEPTH, SGU_HEADS, CHUNK, CHUNK), CHUNK ** -0.5)
    b_spatial = 1.0 + nrm(ks[18], (DEPTH, SGU_HEADS, CHUNK), 0.02)
    w_out = nrm(ks[19], (DEPTH, D_MIX, D_MODEL), D_MIX ** -0.5)
    g_ffn = 1.0 + nrm(ks[20], (DEPTH, D_MODEL), 0.02)
    kf = jax.random.split(ks[21], 3)
    w_gate = nrm(kf[0], (DEPTH, D_MODEL, D_FF), D_MODEL ** -0.5)
    w_up = nrm(kf[1], (DEPTH, D_MODEL, D_FF), D_MODEL ** -0.5)
    w_down = nrm(kf[2], (DEPTH, D_FF, D_MODEL), D_FF ** -0.5)
    g_final = 1.0 + nrm(ks[22], (D_MODEL,), 0.02)
    return {"x": x, "g_mix": g_mix, "w_in": w_in, "A_re": A_re, "A_im": A_im,
            "log_dt": log_dt, "B_re": B_re, "B_im": B_im, "C_re": C_re, "C_im": C_im,
            "D_skip": D_skip, "w_glu": w_glu, "b_glu": b_glu, "w_pool": w_pool,
            "pool_scale": pool_scale, "sgu_ln_g": sgu_ln_g, "sgu_ln_b": sgu_ln_b,
            "w_spatial": w_spatial, "b_spatial": b_spatial, "w_out": w_out,
            "g_ffn": g_ffn, "w_gate": w_gate, "w_up": w_up, "w_down": w_down,
            "g_final": g_final}


def reference(x, g_mix, w_in, A_re, A_im, log_dt, B_re, B_im, C_re, C_im, D_skip,
              w_glu, b_glu, w_pool, pool_scale, sgu_ln_g, sgu_ln_b, w_spatial, b_spatial,
              w_out, g_ffn, w_gate, w_up, w_down, g_final):
    split_points = (D_SSM, D_SSM + D_POOL, D_SSM + D_POOL + D_SGU)
    for l in range(DEPTH):
        h = rms_norm(x, g_mix[l])
        z = h @ w_in[l]
        z_a, z_b, z_u, z_v = jnp.split(z, split_points, axis=-1)
        y_a = s5_mixer(z_a, A_re[l], A_im[l], log_dt[l], B_re[l], B_im[l], C_re[l], C_im[l],
                       D_skip[l], w_glu[l], b_glu[l])
        y_b = pool_mixer(z_b, w_pool[l], pool_scale[l])
        y_c = sgu_mixer(z_u, z_v, sgu_ln_g[l], sgu_ln_b[l], w_spatial[l], b_spatial[l])
        y = jnp.concatenate([y_a, y_b, y_c], axis=-1) @ w_out[l]
        x = x + y
        h = rms_norm(x, g_ffn[l])
        x = x + (jax.nn.silu(h @ w_gate[l]) * (h @ w_up[l])) @ w_down[l]
    return rms_norm(x, g_final)
```

```python
import math
from contextlib import ExitStack

import numpy as np
import concourse.bass as bass
import concourse.mybir as mybir
from concourse.bass_utils import run_bass_kernel_spmd

F32 = mybir.dt.float32
BF16 = mybir.dt.bfloat16
AF = mybir.ActivationFunctionType
ALU = mybir.AluOpType
AX = mybir.AxisListType

NCORES = 8
T = 2048
DM = 1024
DFF = 2816
NF = DFF // 128
EPS = 1e-6
GW = 48 + 256


class Eng:
    def __init__(self, e, sem, name):
        self.e, self.sem, self.name = e, sem, name
        self.count = 0
        self.waited = {}


ALL_DSEMS = []


class DSem:
    def __init__(self, sem, name):
        self.sem, self.name = sem, name
        self.count = 0
        ALL_DSEMS.append(self)


class Buf:
    __slots__ = ("name", "w", "r")

    def __init__(self, name):
        self.name = name
        self.w = None
        self.r = {}


def _sync(E, reads, writes):
    deps = []
    for b in reads:
        if b.w is not None:
            deps.append(b.w)
    for b in writes:
        if b.w is not None:
            deps.append(b.w)
        deps.extend(b.r.items())
    for src, val in deps:
        if src is E and E.name == "pe":
            continue
        if E.waited.get(src, 0) < val:
            E.e.wait_ge(src.sem, val)
            E.waited[src] = val


def _mark(src, val, reads, writes):
    for b in reads:
        if b.r.get(src, 0) < val:
            b.r[src] = val
    for b in writes:
        b.w = (src, val)
        b.r = {}


def op(E, fn, reads=(), writes=()):
    _sync(E, reads, writes)
    ins = fn()
    E.count += 1
    ins.then_inc(E.sem, 1)
    _mark(E, E.count, reads, writes)


def grp(E, fns, reads=(), writes=()):
    _sync(E, reads, writes)
    ins = None
    for f in fns:
        ins = f()
    E.count += 1
    ins.then_inc(E.sem, 1)
    _mark(E, E.count, reads, writes)


def dma(Q, dsem, out, in_, reads=(), writes=()):
    _sync(Q, reads, writes)
    ins = Q.e.dma_start(out=out, in_=in_)
    dsem.count += 16
    ins.then_inc(dsem.sem, 16)
    _mark(dsem, dsem.count, reads, writes)


def mkap(base, off, pat, p0=None, np_=None):
    b = base if p0 is None else base[p0:p0 + np_]
    pp = list(b.ap[0])
    return bass.AP(tensor=b.tensor, offset=b.offset + off, ap=[pp] + [list(x) for x in pat])


def _pos_to_tok():
    tok = np.zeros(T, np.int64)
    for q in range(16):
        for r in range(8):
            jj = np.arange(16)
            tok[q * 128 + r * 16 + jj] = 128 * q + 8 * jj + r
    return tok


def _chunk_perm():
    m = np.arange(128)
    return 8 * (m % 16) + (m // 16)


_POOL_W = (2, 4, 8, 16)


def _pool_mats(first_chunk_of_seq):
    main = np.zeros((4, 128, 128), np.float32)
    halo = np.zeros((4, 128, 128), np.float32)
    for g, w in enumerate(_POOL_W):
        for t in range(128):
            cnt = min(t + 1, w) if first_chunk_of_seq else w
            for s in range(t - w + 1, t + 1):
                if s >= 0:
                    main[g, t, s] += 1.0 / cnt
                elif not first_chunk_of_seq:
                    halo[g, t, s + 128] += 1.0 / cnt
            main[g, t, t] -= 1.0
    return main, halo


class _Stop(Exception):
    pass


def build_program(dbg=None, stop=None):
    ALL_DSEMS.clear()
    nc = bass.Bass("TRN2", target_bir_lowering=False)
    es = ExitStack()

    def din(name, shape, dt=F32):
        return nc.dram_tensor(name, list(shape), dt, kind="ExternalInput").ap()

    d_x = din("xT", [128, 8, T])
    d_win = din("w_in", [2, DM, 1408])
    d_wout = din("w_out", [2, DM, DM])
    d_wg = din("w_gate", [2, DM, DFF])
    d_wu = din("w_up", [2, DM, DFF])
    d_wd = din("w_down", [2, DFF, DM])
    d_wglu = din("w_glu", [2, 384, 384])
    d_gcat = din("gcat", [128, 5, 8])
    d_ssms = din("ssm_s", [2, 128, 3, 12])
    d_ssmbc = din("ssm_bc", [2, 128, 4, 12, 16])
    d_vecs = din("vecs", [2, 128, 11])
    d_lnb = din("lnb", [2, 128, 384])
    d_wpool = din("wpool", [2, 128, 2, 128])
    d_wsT = din("wsT", [2, 128, 6, 128])
    d_bsp = din("bsp", [2, 1, 6, 128])
    d_cmask = din("cmask", [128, 128])
    d_pmat = din("pmat", [128, 16, 128])
    d_oh = din("oh", [128, 3, 8])
    d_ident = din("ident", [128, 128])
    d_out = nc.dram_tensor("outT", [128, 8, T], F32, kind="ExternalOutput").ap()
    d_agin = [nc.dram_tensor(f"ag_in{l}", [128, GW], BF16) for l in range(2)]
    d_agout = [nc.dram_tensor(f"ag_out{l}", [128 * NCORES, GW], BF16) for l in range(2)]
    d_dbg = {}
    if dbg:
        for name, shape in dbg.items():
            d_dbg[name] = nc.dram_tensor("dbg_" + name, list(shape), F32, kind="ExternalOutput").ap()

    def sem(name):
        return es.enter_context(nc.semaphore(name))

    PE = Eng(nc.tensor, sem("s_pe"), "pe")
    ACT = Eng(nc.scalar, sem("s_act"), "act")
    DVE = Eng(nc.vector, sem("s_dve"), "dve")
    POOL = Eng(nc.gpsimd, sem("s_pool"), "pool")
    SP = Eng(nc.sync, sem("s_sp"), "sp")
    ENGS = [PE, ACT, DVE, POOL, SP]

    def barrier():
        for E in ENGS:
            for E2 in ENGS:
                if E2 is not E and E2.count > 0 and E.waited.get(E2, 0) < E2.count:
                    E.e.wait_ge(E2.sem, E2.count)
                    E.waited[E2] = E2.count

    def sb(name, shape, dt):
        return nc.alloc_sbuf_tensor("sb_" + name, list(shape), dt).ap()

    xT = sb("xT", [128, 8, T], F32)
    xT_b = [Buf(f"xT{t}") for t in range(4)]
    ones_bf = sb("ones_bf", [128, 128], BF16)
    ident = sb("ident", [128, 128], F32)
    gcat = sb("gcat", [128, 5, 8], F32)
    cmask = sb("cmask", [128, 128], BF16)
    pmat = sb("pmat", [128, 16, 128], BF16)
    oh = sb("oh", [128, 3, 8], F32)
    kc = sb("kc", [128, 8], F32)
    ssm_s = sb("ssm_s", [128, 3, 12], F32)
    ssm_bc = sb("ssm_bc", [128, 4, 12, 16], F32)
    vecs = sb("vecs", [128, 11], F32)
    LNB = sb("lnb", [128, 384], BF16)
    WP = sb("wpool", [128, 2, 128], BF16)
    WsT = sb("wsT", [128, 6, 128], BF16)
    bsp = sb("bsp", [1, 6, 128], BF16)
    wglu = sb("wglu", [128, 3, 384], BF16)
    cterm = sb("cterm", [128, 3, 128], F32)
    b_gconst = Buf("gconst")
    b_lconst = Buf("lconst")
    b_cterm = Buf("cterm")
    ds_g = DSem(sem("ds_g"), "ds_g")
    ds_l = [DSem(sem(f"ds_l{l}"), f"ds_l{l}") for l in range(2)]

    AR_BYTES = 122 * 1024
    arena = sb("arena", [128, AR_BYTES // 4], F32)

    def carve(off, shape, dt):
        esz = 2 if dt == BF16 else 4
        n = int(np.prod(shape[1:]))
        assert off % 4 == 0 and (n * esz) % 4 == 0
        assert off + n * esz <= AR_BYTES, (off, n * esz)
        v = arena[:, off // 4: (off + n * esz) // 4]
        if dt != F32:
            v = v.bitcast(dt)
        if len(shape) == 2:
            return v
        names = " ".join(f"d{i}" for i in range(len(shape) - 1))
        kw = {f"d{i}": shape[i + 1] for i in range(len(shape) - 1)}
        return v.rearrange(f"p ({names}) -> p {names}", **kw)

    KB = 1024
    hT = carve(0, [128, 8, 1024], BF16)
    hT_b = [Buf("hT0"), Buf("hT1")]
    sq = carve(16 * KB, [128, 8, 512], BF16)
    sq_b = Buf("sq")
    rs = carve(24 * KB, [128, 512], F32)
    rs_b = Buf("rs")
    TMP = 26 * KB
    yT = carve(32 * KB, [128, 8, T], BF16)
    za_b = [[Buf(f"za{k}_{r}") for r in range(8)] for k in range(3)]
    yb_b = [Buf(f"yb{t}") for t in range(4)]
    yc_b = [Buf(f"yc{t}") for t in range(4)]
    zb = carve(64 * KB, [128, 16, 256], BF16)
    zb_b = [Buf(f"zb{q}") for q in range(16)]
    RA = 72 * KB

    ps_t = [nc.alloc_psum_tensor(f"ps{i}", [128, 512], F32).ap() for i in range(8)]
    ps_b = [Buf(f"ps{i}") for i in range(8)]
    ps_i = [0]

    def PSN():
        i = ps_i[0] % 8
        ps_i[0] += 1
        return ps_t[i], ps_b[i]

    v_ = nc.vector
    a_ = nc.scalar
    pe = nc.tensor

    dma(SP, ds_g, ident, d_ident, writes=[b_gconst])
    dma(SP, ds_g, gcat, d_gcat, writes=[b_gconst])
    dma(SP, ds_g, oh, d_oh, writes=[b_gconst])
    dma(POOL, ds_g, cmask, d_cmask, writes=[b_gconst])
    dma(POOL, ds_g, pmat, d_pmat, writes=[b_gconst])
    b_gconst.w = (ds_g, ds_g.count)
    for t in range(4):
        dsx = DSem(sem(f"ds_x{t}"), f"ds_x{t}")
        dma(SP, dsx, xT[:, :, t * 512:(t + 1) * 512], d_x[:, :, t * 512:(t + 1) * 512], writes=[xT_b[t]])
    b_kc = Buf("kc")
    op(DVE, lambda: v_.memset(ones_bf, 1.0), writes=[b_kc])
    op(DVE, lambda: v_.memset(kc[:, 0:1], EPS), writes=[b_kc])
    op(DVE, lambda: v_.memset(kc[:, 1:2], -math.pi), writes=[b_kc])
    op(DVE, lambda: v_.memset(kc[:, 2:3], 0.0), writes=[b_kc])
    KEPS = kc[:, 0:1]
    KNPI = kc[:, 1:2]

    def norm_tile(gi, tt, outs_fn, reads_extra=()):
        c0 = tt * 512
        op(ACT, lambda: a_.activation(out=sq, in_=xT[:, :, c0:c0 + 512], func=AF.Square),
           reads=[xT_b[tt]], writes=[sq_b])
        pst, psb = PSN()
        grp(PE, [(lambda k=k: pe.matmul(pst, lhsT=ones_bf, rhs=sq[:, k, :], start=(k == 0), stop=(k == 7)))
                 for k in range(8)], reads=[sq_b, b_kc], writes=[psb])
        op(ACT, lambda: a_.activation(out=rs, in_=pst, func=AF.Sqrt, bias=KEPS, scale=1.0 / DM),
           reads=[psb, b_kc], writes=[rs_b])
        op(DVE, lambda: v_.reciprocal(out=rs, in_=rs), reads=[rs_b], writes=[rs_b])
        outs_fn(c0)

    def norm_to_hT(gi, tt, lt):
        def outs(c0):
            for k in range(8):
                op(DVE, lambda k=k: v_.scalar_tensor_tensor(
                    out=hT[:, k, lt * 512:(lt + 1) * 512], in0=xT[:, k, c0:c0 + 512],
                    scalar=gcat[:, gi, k:k + 1], in1=rs, op0=ALU.mult, op1=ALU.mult),
                   reads=[xT_b[tt], rs_b, b_gconst], writes=[hT_b[lt]])
        norm_tile(gi, tt, outs)

    ds_win = DSem(sem("ds_win"), "ds_win")
    b_win = Buf("w_in")
    ds_wo = [DSem(sem(f"ds_wo{i}"), f"ds_wo{i}") for i in range(2)]
    b_wo = [Buf(f"wo{i}") for i in range(2)]
    ds_wgu = [DSem(sem(f"ds_wgu{i}"), f"ds_wgu{i}") for i in range(3)]
    b_wgu = [Buf(f"wgu{i}") for i in range(3)]
    ds_wd = [DSem(sem(f"ds_wd{i}"), f"ds_wd{i}") for i in range(3)]
    b_wd = [Buf(f"wd{i}") for i in range(3)]
    ds_ag = DSem(sem("ds_ag"), "ds_ag")
    ds_out = DSem(sem("ds_out"), "ds_out")
    cc_sem = sem("cc_sem")
    cc_cnt = [0]
    ds_dbg = DSem(sem("ds_dbg"), "ds_dbg")

    def stop_at(i):
        if stop is not None and stop == i:
            raise _Stop()

    def dbg_dump(name, ap, reads):
        if name in d_dbg:
            dma(POOL, ds_dbg, d_dbg[name], ap, reads=reads)

    try:
        for l in range(2):
            stop_at(20 * l)
            dsl = ds_l[l]
            dma(SP, dsl, ssm_s, d_ssms[l], writes=[b_lconst])
            dma(SP, dsl, ssm_bc, d_ssmbc[l], writes=[b_lconst])
            dma(SP, dsl, vecs, d_vecs[l], writes=[b_lconst])
            dma(POOL, dsl, LNB, d_lnb[l], writes=[b_lconst])
            dma(POOL, dsl, WP, d_wpool[l], writes=[b_lconst])
            dma(POOL, dsl, WsT, d_wsT[l], writes=[b_lconst])
            dma(POOL, dsl, bsp, d_bsp[l], writes=[b_lconst])
            dma(POOL, dsl, wglu, d_wglu[l].rearrange("(k p) c -> p k c", p=128), writes=[b_lconst])
            b_lconst.w = (dsl, dsl.count)
            Dsk = vecs[:, 0:3]
            bglu = vecs[:, 3:6]
            pscale = vecs[:, 6:8]
            lng = vecs[:, 8:11]

            w_in = carve(RA, [128, 8, 1408], BF16)
            for k in range(8):
                dma(POOL, ds_win, w_in[:, k, :], d_win[l, k * 128:(k + 1) * 128, :], writes=[b_win])
            uT = carve(RA + 22528, [128, 3, 1024], BF16)
            uT_b = [Buf("uT0"), Buf("uT1")]
            v32 = carve(TMP, [128, 384], F32)
            vhat = carve(TMP + 1536, [128, 384], BF16)
            stt = carve(TMP + 2304, [128, 8], F32)
            tmpc = carve(TMP + 2560, [128, 128], F32)
            b_v32, b_vhat, b_stt, b_tmpc = Buf("v32"), Buf("vhat"), Buf("stt"), Buf("tmpc")

            op(DVE, lambda: v_.tensor_tensor(out=WsT, in0=WsT, in1=mkap(cmask, 0, [[0, 6], [1, 128]]), op=ALU.mult),
               reads=[b_lconst, b_gconst], writes=[b_lconst])
            for hp in range(3):
                pst, psb = PSN()
                fns = []
                for hh in range(2):
                    h = 2 * hp + hh
                    fns.append(lambda h=h, hh=hh: pe.matmul(pst[64 * hh:64 * hh + 64, 0:128], lhsT=LNB[:, 64 * h:64 * h + 64],
                                                           rhs=WsT[:, h, :], start=True, stop=False, tile_position=(0, 64 * hh)))
                    fns.append(lambda h=h, hh=hh: pe.matmul(pst[64 * hh:64 * hh + 64, 0:128], lhsT=ones_bf[0:1, 0:64],
                                                           rhs=bsp[0:1, h, :], start=False, stop=True, tile_position=(0, 64 * hh)))
                grp(PE, fns, reads=[b_lconst, b_kc], writes=[psb])
                op(ACT, lambda hp=hp: a_.copy(out=cterm[:, hp, :], in_=pst[:, 0:128]), reads=[psb], writes=[b_cterm])

            stop_at(20 * l + 1)
            for hf in range(2):
                for lt in range(2):
                    norm_to_hT(2 * l, 2 * hf + lt, lt)
                for k3 in range(3):
                    for lt in range(2):
                        tt = 2 * hf + lt
                        pst, psb = PSN()
                        grp(PE, [(lambda k=k: pe.matmul(pst, lhsT=w_in[:, k, k3 * 128:(k3 + 1) * 128],
                                                        rhs=hT[:, k, lt * 512:(lt + 1) * 512], start=(k == 0), stop=(k == 7)))
                                 for k in range(8)], reads=[b_win, hT_b[lt]], writes=[psb])
                        rb = za_b[k3]
                        op(ACT, lambda: a_.copy(out=yT[:, k3, tt * 512:(tt + 1) * 512], in_=pst), reads=[psb], writes=rb)
                        pst2, psb2 = PSN()
                        grp(PE, [(lambda k=k: pe.matmul(pst2, lhsT=w_in[:, k, 384 + k3 * 128:384 + (k3 + 1) * 128],
                                                        rhs=hT[:, k, lt * 512:(lt + 1) * 512], start=(k == 0), stop=(k == 7)))
                                 for k in range(8)], reads=[b_win, hT_b[lt]], writes=[psb2])
                        op(ACT, lambda: a_.activation(out=uT[:, k3, lt * 512:(lt + 1) * 512], in_=pst2, func=AF.Gelu_apprx_tanh),
                           reads=[psb2], writes=[uT_b[lt]])
                stop_at(20 * l + 2)
                for qq in range(8):
                    q = 8 * hf + qq
                    pB, pBb = PSN()
                    pV, pVb = PSN()
                    lhs = [hT[:, k, qq * 128:(qq + 1) * 128] for k in range(8)]
                    grp(PE, [(lambda k=k: pe.matmul(pB[:, 0:256], lhsT=lhs[k], rhs=w_in[:, k, 768:1024], start=(k == 0), stop=(k == 7)))
                             for k in range(8)], reads=[b_win, hT_b[qq // 4]], writes=[pBb])
                    grp(PE, [(lambda k=k: pe.matmul(pV[:, 0:384], lhsT=lhs[k], rhs=w_in[:, k, 1024:1408], start=(k == 0), stop=(k == 7)))
                             for k in range(8)], reads=[b_win, hT_b[qq // 4]], writes=[pVb])
                    op(DVE, lambda: v_.tensor_copy(out=zb[:, q, :], in_=pB[:, 0:256]), reads=[pBb], writes=[zb_b[q]])
                    op(ACT, lambda: a_.activation(out=v32, in_=pV[:, 0:384], func=AF.Gelu_apprx_tanh), reads=[pVb], writes=[b_v32])
                    op(DVE, lambda: v_.bn_stats(out=stt[:, 0:6], in_=v32), reads=[b_v32], writes=[b_stt])
                    def SV(fn):
                        op(DVE, fn, reads=[b_stt], writes=[b_stt])
                    SV(lambda: v_.tensor_scalar(out=stt[:, 6:7], in0=stt[:, 1:2], scalar1=stt[:, 4:5], scalar2=0.5,
                                                op0=ALU.add, op1=ALU.mult))
                    SV(lambda: v_.tensor_tensor(out=stt[:, 0:1], in0=stt[:, 1:2], in1=stt[:, 4:5], op=ALU.subtract))
                    SV(lambda: v_.tensor_scalar(out=stt[:, 0:1], in0=stt[:, 0:1], scalar1=stt[:, 0:1], scalar2=0.25,
                                                op0=ALU.mult, op1=ALU.mult))
                    SV(lambda: v_.tensor_tensor(out=stt[:, 3:4], in0=stt[:, 2:3], in1=stt[:, 5:6], op=ALU.add))
                    SV(lambda: v_.scalar_tensor_tensor(out=stt[:, 7:8], in0=stt[:, 3:4], scalar=1.0 / 384.0, in1=stt[:, 0:1],
                                                       op0=ALU.mult, op1=ALU.add))
                    op(ACT, lambda: a_.activation(out=stt[:, 7:8], in_=stt[:, 7:8], func=AF.Sqrt, bias=KEPS, scale=1.0),
                       reads=[b_stt, b_kc], writes=[b_stt])
                    op(DVE, lambda: v_.reciprocal(out=stt[:, 7:8], in_=stt[:, 7:8]), reads=[b_stt], writes=[b_stt])
                    op(DVE, lambda: v_.tensor_scalar(out=vhat, in0=v32, scalar1=stt[:, 6:7], scalar2=stt[:, 7:8],
                                                     op0=ALU.subtract, op1=ALU.mult), reads=[b_v32, b_stt], writes=[b_vhat])
                    for hp in range(3):
                        pst, psb = PSN()
                        grp(PE, [(lambda hh=hh: pe.matmul(pst[64 * hh:64 * hh + 64, 0:128],
                                                          lhsT=vhat[:, 64 * (2 * hp + hh):64 * (2 * hp + hh) + 64],
                                                          rhs=WsT[:, 2 * hp + hh, :], start=True, stop=True,
                                                          tile_position=(0, 64 * hh))) for hh in range(2)],
                            reads=[b_vhat, b_lconst], writes=[psb])
                        op(DVE, lambda hp=hp: v_.scalar_tensor_tensor(out=tmpc, in0=pst[:, 0:128], scalar=lng[:, hp:hp + 1],
                                                                      in1=cterm[:, hp, :], op0=ALU.mult, op1=ALU.add),
                           reads=[psb, b_cterm, b_lconst], writes=[b_tmpc])
                        op(DVE, lambda hp=hp: v_.tensor_tensor(
                            out=yT[:, 5 + hp, q * 128:(q + 1) * 128], in0=tmpc,
                            in1=uT[:, hp, qq * 128:(qq + 1) * 128], op=ALU.mult),
                           reads=[b_tmpc, uT_b[qq // 4]], writes=[yc_b[q // 4]])
            barrier()
            dbg_dump(f"yT_p1_{l}", yT, [za_b[k][r] for k in range(3) for r in range(8)] + yc_b)
            dbg_dump(f"zb_{l}", zb, zb_b)

            stop_at(20 * l + 3)
            W1 = carve(RA, [128, 3, 8, 2, 128], BF16)
            Cab = carve(RA + 12288, [128, 12, 9, 2, 32], BF16)
            Bbb = carve(RA + 26112, [128, 12, 2, 32], BF16)
            Ktap = carve(RA + 27648, [128, 3, 8, 128], BF16)
            TB = RA + 33792

            def tab(i):
                return carve(TB + i * 768, [128, 12, 16], F32)
            T1c, T1s, P8r, P8i, T2c, T2s, P128r, P128i, RH1, RH2 = [tab(i) for i in range(10)]
            SM = TB + 7680
            Ere = carve(SM, [128, 12, 16], F32)
            Eim = carve(SM + 768, [128, 12, 16], F32)
            Xre = carve(SM + 1536, [128, 12, 16], F32)
            Xim = carve(SM + 2304, [128, 12, 16], F32)
            Hre = carve(SM + 3072, [128, 12, 16], F32)
            Him = carve(SM + 3840, [128, 12, 16], F32)
            sm2 = carve(SM + 4608, [128, 16, 12], F32)
            A2048r, A2048i, hinr, hini = sm2[:, 0, :], sm2[:, 1, :], sm2[:, 2, :], sm2[:, 3, :]
            pwr = carve(0, [128, 9, 12], F32)
            pwi = carve(432, [128, 9, 12], F32)
            s12 = carve(1024, [128, 24, 12], F32)
            Bbr = carve(4096, [128, 12, 16], F32)
            Bbi = carve(4864, [128, 12, 16], F32)
            tA = carve(5632, [128, 12, 16], F32)
            tB = carve(6400, [128, 12, 16], F32)
            QQ = carve(8192, [128, 8, 2, 4, 32], F32)
            ts16 = carve(16 * KB, [128, 8, 12, 16], F32)
            b_su = Buf("setup")

            def V(fn, r=(), w=()):
                op(DVE, fn, reads=[b_su, b_lconst] + list(r), writes=[b_su] + list(w))

            def A(fn):
                op(ACT, fn, reads=[b_su, b_lconst, b_kc], writes=[b_su])

            def tt_(out, a, b, o):
                V(lambda: v_.tensor_tensor(out=out, in0=a, in1=b, op=o))

            def cmul(outr, outi, ar_, ai_, br_, bi_, t1, t2):
                tt_(t1, ar_, br_, ALU.mult)
                tt_(t2, ai_, bi_, ALU.mult)
                tt_(outr, t1, t2, ALU.subtract)
                tt_(t1, ar_, bi_, ALU.mult)
                tt_(t2, ai_, br_, ALU.mult)
                tt_(outi, t1, t2, ALU.add)

            Are, Aim, Ldt = ssm_s[:, 0, :], ssm_s[:, 1, :], ssm_s[:, 2, :]
            S = [s12[:, i, :] for i in range(24)]
            dtv, x1, th, mag, cs, sn, ar, ai, den, fre, fim, u1, u2 = S[0:13]
            A(lambda: a_.activation(out=dtv, in_=Ldt, func=AF.Exp))
            tt_(x1, Are, dtv, ALU.mult)
            tt_(th, Aim, dtv, ALU.mult)
            A(lambda: a_.activation(out=mag, in_=x1, func=AF.Exp))
            A(lambda: a_.activation(out=sn, in_=th, func=AF.Sin, scale=1.0 / 8.0))
            A(lambda: a_.activation(out=u1, in_=th, func=AF.Sin, scale=1.0 / 16.0))
            tt_(u2, u1, u1, ALU.mult)
            V(lambda: v_.tensor_scalar(out=cs, in0=u2, scalar1=-2.0, scalar2=1.0, op0=ALU.mult, op1=ALU.add))
            for _ in range(3):
                tt_(u1, cs, cs, ALU.mult)
                tt_(u2, sn, sn, ALU.mult)
                tt_(S[13], cs, sn, ALU.mult)
                tt_(cs, u1, u2, ALU.subtract)
                V(lambda: v_.tensor_scalar(out=sn, in0=S[13], scalar1=2.0, scalar2=None, op0=ALU.mult))
            tt_(ar, mag, cs, ALU.mult)
            tt_(ai, mag, sn, ALU.mult)
            tt_(u1, Are, Are, ALU.mult)
            tt_(u2, Aim, Aim, ALU.mult)
            tt_(den, u1, u2, ALU.add)
            V(lambda: v_.reciprocal(out=den, in_=den))
            arm1 = S[13]
            V(lambda: v_.tensor_scalar(out=arm1, in0=ar, scalar1=-1.0, scalar2=None, op0=ALU.add))
            tt_(u1, arm1, Are, ALU.mult)
            tt_(u2, ai, Aim, ALU.mult)
            tt_(u1, u1, u2, ALU.add)
            tt_(fre, u1, den, ALU.mult)
            tt_(u1, ai, Are, ALU.mult)
            tt_(u2, arm1, Aim, ALU.mult)
            tt_(u1, u1, u2, ALU.subtract)
            tt_(fim, u1, den, ALU.mult)
            Bre, Bim, Cre, Cim = ssm_bc[:, 0], ssm_bc[:, 1], ssm_bc[:, 2], ssm_bc[:, 3]

            def bc16(v):
                return mkap(v, 0, [[v.ap[1][0], 12], [0, 16]])
            cmul(Bbr, Bbi, bc16(fre), bc16(fim), Bre, Bim, tA, tB)
            V(lambda: v_.memset(pwr[:, 0, :], 1.0))
            V(lambda: v_.memset(pwi[:, 0, :], 0.0))
            for k in range(1, 9):
                cmul(pwr[:, k, :], pwi[:, k, :], pwr[:, k - 1, :], pwi[:, k - 1, :], ar, ai, S[14], S[15])
            A8r, A8i = pwr[:, 8, :], pwi[:, 8, :]
            rho8, irho8, u8r, u8i, rho128, irho, A128r, A128i, u128r, u128i = S[14:24]
            tt_(u1, mag, mag, ALU.mult)
            tt_(u2, u1, u1, ALU.mult)
            tt_(rho8, u2, u2, ALU.mult)
            V(lambda: v_.reciprocal(out=irho8, in_=rho8))
            tt_(u8r, A8r, irho8, ALU.mult)
            tt_(u8i, A8i, irho8, ALU.mult)

            def col(tb, j):
                return tb[:, :, j]

            def build_table(Tr, Ti, br_, bi_, first_one):
                if first_one:
                    V(lambda: v_.memset(col(Tr, 0), 1.0))
                    V(lambda: v_.memset(col(Ti, 0), 0.0))
                    V(lambda: v_.tensor_copy(out=col(Tr, 1), in_=br_))
                    V(lambda: v_.tensor_copy(out=col(Ti, 1), in_=bi_))
                    start = 2
                else:
                    V(lambda: v_.tensor_copy(out=col(Tr, 0), in_=br_))
                    V(lambda: v_.tensor_copy(out=col(Ti, 0), in_=bi_))
                    start = 1
                for j in range(start, 16):
                    cmul(col(Tr, j), col(Ti, j), col(Tr, j - 1), col(Ti, j - 1), br_, bi_, u1, u2)
            build_table(T1c, T1s, u8r, u8i, True)
            build_table(P8r, P8i, A8r, A8i, False)
            V(lambda: v_.tensor_copy(out=A128r, in_=col(P8r, 15)))
            V(lambda: v_.tensor_copy(out=A128i, in_=col(P8i, 15)))
            tt_(u1, rho8, rho8, ALU.mult)
            tt_(u2, u1, u1, ALU.mult)
            tt_(u1, u2, u2, ALU.mult)
            tt_(rho128, u1, u1, ALU.mult)
            V(lambda: v_.reciprocal(out=irho, in_=rho128))
            tt_(u128r, A128r, irho, ALU.mult)
            tt_(u128i, A128i, irho, ALU.mult)
            build_table(T2c, T2s, u128r, u128i, True)
            build_table(P128r, P128i, A128r, A128i, True)
            cmul(A2048r, A2048i, col(P128r, 15), col(P128i, 15), A128r, A128i, u1, u2)
            V(lambda: v_.tensor_copy(out=RH1, in_=bc16(rho8)))
            V(lambda: v_.memset(col(RH1, 0), 0.0))
            V(lambda: v_.tensor_copy(out=RH2, in_=bc16(rho128)))
            V(lambda: v_.memset(col(RH2, 0), 0.0))

            V(lambda: v_.memset(Cab, 0.0))
            V(lambda: v_.memset(Bbb, 0.0))
            for k in range(9):
                pr, pi_ = mkap(pwr, k * 12, [[1, 12], [0, 16]]), mkap(pwi, k * 12, [[1, 12], [0, 16]])
                tt_(tA, Cre, pr, ALU.mult)
                tt_(tB, Cim, pi_, ALU.mult)
                for g2 in range(2):
                    V(lambda g2=g2, k=k: v_.tensor_tensor(
                        out=Cab[64 * g2:64 * g2 + 64, :, k, 0, 16 * g2:16 * g2 + 16],
                        in0=tA[64 * g2:64 * g2 + 64], in1=tB[64 * g2:64 * g2 + 64], op=ALU.subtract))
                tt_(tA, Cre, pi_, ALU.mult)
                tt_(tB, Cim, pr, ALU.mult)
                for g2 in range(2):
                    V(lambda g2=g2, k=k: v_.scalar_tensor_tensor(
                        out=Cab[64 * g2:64 * g2 + 64, :, k, 1, 16 * g2:16 * g2 + 16],
                        in0=tA[64 * g2:64 * g2 + 64], scalar=-1.0, in1=tB[64 * g2:64 * g2 + 64],
                        op0=ALU.mult, op1=ALU.subtract))
            for g2 in range(2):
                V(lambda g2=g2: v_.tensor_copy(out=Bbb[64 * g2:64 * g2 + 64, :, 0, 16 * g2:16 * g2 + 16], in_=Bbr[64 * g2:64 * g2 + 64]))
                V(lambda g2=g2: v_.tensor_copy(out=Bbb[64 * g2:64 * g2 + 64, :, 1, 16 * g2:16 * g2 + 16], in_=Bbi[64 * g2:64 * g2 + 64]))
            V(lambda: v_.memset(Ktap, 0.0))
            for k in range(3):
                pst, psb = PSN()
                pst2, psb2 = PSN()
                fns = []
                for qd in range(4):
                    p = 4 * k + qd
                    for half, pp in ((0, pst), (1, pst2)):
                        for ri in range(2):
                            fns.append(lambda p=p, qd=qd, half=half, pp=pp, ri=ri: pe.matmul(
                                mkap(pp, 32 * qd, [[128, 4], [1, 32]], p0=32 * qd, np_=32),
                                lhsT=Bbb[:, p, ri, :],
                                rhs=mkap(Cab, p * 576 + half * 4 * 64 + ri * 32, [[64, 4], [1, 32]]),
                                start=(ri == 0), stop=(ri == 1), tile_position=(0, 32 * qd)))
                grp(PE, fns, reads=[b_su], writes=[psb, psb2])
                for qd in range(4):
                    for half, pp, pb in ((0, pst, psb), (1, pst2, psb2)):
                        op(DVE, lambda qd=qd, half=half, pp=pp, k=k: v_.tensor_copy(
                            out=Ktap[32 * qd:32 * qd + 32, k, 4 * half:4 * half + 4, 32 * qd:32 * qd + 32],
                            in_=mkap(pp, 32 * qd, [[128, 4], [1, 32]], p0=32 * qd, np_=32)),
                           reads=[pb, b_su], writes=[b_su])
                V(lambda k=k: v_.scalar_tensor_tensor(out=Ktap[:, k, 0, :], in0=ident, scalar=Dsk[:, k:k + 1],
                                                      in1=Ktap[:, k, 0, :], op0=ALU.mult, op1=ALU.add), r=[b_gconst])
            for k in range(3):
                V(lambda: v_.memset(QQ, 0.0))
                for rp in range(8):
                    pw_ = 7 - rp
                    pr, pi_ = mkap(pwr, pw_ * 12 + 4 * k, [[1, 4], [0, 16]]), mkap(pwi, pw_ * 12 + 4 * k, [[1, 4], [0, 16]])
                    br_, bi_ = Bbr[:, 4 * k:4 * k + 4, :], Bbi[:, 4 * k:4 * k + 4, :]
                    tA4, tB4 = tA[:, 0:4, :], tB[:, 0:4, :]
                    tt_(tA4, pr, br_, ALU.mult)
                    tt_(tB4, pi_, bi_, ALU.mult)
                    for g2 in range(2):
                        V(lambda g2=g2, rp=rp: v_.tensor_tensor(
                            out=QQ[64 * g2:64 * g2 + 64, rp, 0, :, 16 * g2:16 * g2 + 16],
                            in0=tA4[64 * g2:64 * g2 + 64], in1=tB4[64 * g2:64 * g2 + 64], op=ALU.subtract))
                    tt_(tA4, pr, bi_, ALU.mult)
                    tt_(tB4, pi_, br_, ALU.mult)
                    for g2 in range(2):
                        V(lambda g2=g2, rp=rp: v_.tensor_tensor(
                            out=QQ[64 * g2:64 * g2 + 64, rp, 1, :, 16 * g2:16 * g2 + 16],
                            in0=tA4[64 * g2:64 * g2 + 64], in1=tB4[64 * g2:64 * g2 + 64], op=ALU.add))
                for rp in range(8):
                    for ri in range(2):
                        pst, psb = PSN()
                        op(PE, lambda rp=rp, ri=ri: pe.transpose(pst[:, 0:128], QQ[:, rp, ri].rearrange("p a b -> p (a b)"), ident),
                           reads=[b_su, b_gconst], writes=[psb])
                        op(ACT, lambda rp=rp, ri=ri, k=k: a_.copy(out=W1[:, k, rp, ri, :], in_=pst[:, 0:128]),
                           reads=[psb, b_su], writes=[b_su])
                if k == 2:
                    dbg_dump(f"tabs_{l}", carve(TB, [128, 1920], F32), [b_su])
                    dbg_dump(f"pw_{l}", carve(0, [128, 216], F32), [b_su])
                    dbg_dump(f"s12_{l}", carve(1024, [128, 288], F32), [b_su])
                    dbg_dump(f"Bb_{l}", carve(4096, [128, 384], F32), [b_su])
                    dbg_dump(f"Ktap_{l}", Ktap.rearrange("p a b c -> p (a b c)"), [b_su])
                    dbg_dump(f"W1_{l}", W1.rearrange("p a b c d -> p (a b c d)"), [b_su])
                    dbg_dump(f"Cab_{l}", Cab.rearrange("p a b c d -> p (a b c d)"), [b_su])
                V(lambda: v_.memset(kc[:, 3:4], 0.0))
            barrier()

            stop_at(20 * l + 4)
            Sb = carve(0, [128, 2, 12, 260], BF16)
            b_Sb = Buf("Sb")
            Gs = carve(16 * KB, [128, 8, GW], BF16)
            b_Gs = Buf("Gs")
            wo = [carve(RA + 46 * KB + i * 2048, [128, 8, 128], BF16) for i in range(2)]
            wr = carve(TMP, [128, 256], F32)
            wi = carve(TMP + 1024, [128, 256], F32)
            t1 = carve(TMP + 2048, [128, 256], F32)
            t2 = carve(TMP + 3072, [128, 256], F32)
            b_scan = Buf("scan")
            rhfull = carve(TMP + 4096, [128, 256], F32)

            def b16x16(tb, p):
                return mkap(tb, p * 16, [[0, 16], [1, 16]])

            def v3(x):
                return x.rearrange("p (a b) -> p a b", a=16)

            for p in range(12):
                k, qd = p // 4, p % 4
                pst, psb = PSN()
                fns = []
                for ri in range(2):
                    for rp in range(8):
                        fns.append(lambda ri=ri, rp=rp: pe.matmul(
                            pst[:, ri * 256:(ri + 1) * 256], lhsT=W1[32 * qd:32 * qd + 32, k, rp, ri, :],
                            rhs=mkap(yT, k * T + rp * 16, [[128, 16], [1, 16]], p0=32 * qd, np_=32),
                            start=(rp == 0), stop=(rp == 7), tile_position=(32 * qd, 0)))
                grp(PE, fns, reads=[b_su] + za_b[k], writes=[psb])
                Lr, Li = v3(pst[:, 0:256]), v3(pst[:, 256:512])
                c_, s_ = b16x16(T1c, p), b16x16(T1s, p)

                def D(fn, extra=()):
                    op(DVE, fn, reads=[psb, b_su, b_scan] + list(extra), writes=[b_scan])
                D(lambda: v_.tensor_tensor(out=v3(t1), in0=Lr, in1=c_, op=ALU.mult))
                D(lambda: v_.tensor_tensor(out=v3(t2), in0=Li, in1=s_, op=ALU.mult))
                D(lambda: v_.tensor_tensor(out=wr, in0=t1, in1=t2, op=ALU.add))
                D(lambda: v_.tensor_tensor(out=v3(t1), in0=Li, in1=c_, op=ALU.mult))
                D(lambda: v_.tensor_tensor(out=v3(t2), in0=Lr, in1=s_, op=ALU.mult))
                D(lambda: v_.tensor_tensor(out=wi, in0=t1, in1=t2, op=ALU.subtract))
                D(lambda: v_.tensor_copy(out=v3(rhfull), in_=mkap(RH1, p * 16, [[0, 16], [1, 16]])))
                D(lambda: v_.tensor_tensor_scan(out=wr, data0=rhfull, data1=wr, initial=0.0, op0=ALU.mult, op1=ALU.add))
                D(lambda: v_.tensor_tensor_scan(out=wi, data0=rhfull, data1=wi, initial=0.0, op0=ALU.mult, op1=ALU.add))
                D(lambda: v_.tensor_tensor(out=v3(t1), in0=v3(wr), in1=c_, op=ALU.mult))
                D(lambda: v_.tensor_tensor(out=v3(t2), in0=v3(wi), in1=s_, op=ALU.mult))
                op(DVE, lambda p=p: v_.tensor_tensor(out=Sb[:, 0, p, 1:257], in0=t1, in1=t2, op=ALU.subtract),
                   reads=[b_scan], writes=[b_Sb, b_scan])
                op(DVE, lambda p=p: v_.tensor_tensor(out=Ere[:, p, :], in0=mkap(t1, 15, [[16, 16]]), in1=mkap(t2, 15, [[16, 16]]),
                                                     op=ALU.subtract), reads=[b_scan, b_su], writes=[b_su, b_scan])
                D(lambda: v_.tensor_tensor(out=v3(t1), in0=v3(wr), in1=s_, op=ALU.mult))
                D(lambda: v_.tensor_tensor(out=v3(t2), in0=v3(wi), in1=c_, op=ALU.mult))
                op(DVE, lambda p=p: v_.tensor_tensor(out=Sb[:, 1, p, 1:257], in0=t1, in1=t2, op=ALU.add),
                   reads=[b_scan], writes=[b_Sb, b_scan])
                op(DVE, lambda p=p: v_.tensor_tensor(out=Eim[:, p, :], in0=mkap(t1, 15, [[16, 16]]), in1=mkap(t2, 15, [[16, 16]]),
                                                     op=ALU.add), reads=[b_scan, b_su], writes=[b_su, b_scan])
            stop_at(20 * l + 5)
            f192 = lambda x: x.rearrange("p a b -> p (a b)")
            xa, xb_, xc, xd = [ts16[:, i] for i in range(4)]
            tt_(xa, Ere, T2c, ALU.mult)
            tt_(xb_, Eim, T2s, ALU.mult)
            tt_(xc, xa, xb_, ALU.add)
            tt_(xa, Eim, T2c, ALU.mult)
            tt_(xb_, Ere, T2s, ALU.mult)
            tt_(xd, xa, xb_, ALU.subtract)
            V(lambda: v_.tensor_tensor_scan(out=f192(xc), data0=f192(RH2), data1=f192(xc), initial=0.0, op0=ALU.mult, op1=ALU.add))
            V(lambda: v_.tensor_tensor_scan(out=f192(xd), data0=f192(RH2), data1=f192(xd), initial=0.0, op0=ALU.mult, op1=ALU.add))
            tt_(xa, xc, T2c, ALU.mult)
            tt_(xb_, xd, T2s, ALU.mult)
            tt_(Xre, xa, xb_, ALU.subtract)
            tt_(xa, xc, T2s, ALU.mult)
            tt_(xb_, xd, T2c, ALU.mult)
            tt_(Xim, xa, xb_, ALU.add)
            stop_at(20 * l + 6)
            agst = carve(20 * KB + 768 * 2, [128, 24], F32)
            b_ag = Buf("agst")
            op(DVE, lambda: v_.tensor_copy(out=agst[:, 0:12], in_=col(Xre, 15)), reads=[b_su], writes=[b_ag])
            op(DVE, lambda: v_.tensor_copy(out=agst[:, 12:24], in_=col(Xim, 15)), reads=[b_su], writes=[b_ag])
            b_agd = Buf("agdram")
            dma(POOL, ds_ag, d_agin[l].ap()[:, 0:48], agst.bitcast(BF16), reads=[b_ag], writes=[b_agd])
            dma(POOL, ds_ag, d_agin[l].ap()[:, 48:GW], zb[:, 15, :], reads=[zb_b[15]], writes=[b_agd])
            _sync(POOL, [b_agd], [])
            cc = nc.gpsimd.collective_compute("AllGather", ALU.bypass, replica_groups=[list(range(NCORES))],
                                              ins=[d_agin[l].ap().opt()], outs=[d_agout[l].ap().opt()])
            cc_cnt[0] += 1
            cc.then_inc(cc_sem, 1)
            nc.gpsimd.wait_ge(cc_sem, cc_cnt[0])
            dma(POOL, ds_ag, Gs, d_agout[l].ap().rearrange("(c p) f -> p c f", p=128), writes=[b_Gs])

            stop_at(20 * l + 7)
            pooled = carve(24 * KB, [128, 128], BF16)
            b_pooled = Buf("pooled")

            def pool_chunk(q, prev_ap, prev_bufs, mi, hi):
                hf, qq = q // 8, q % 8
                for gp in range(2):
                    pst, psb = PSN()
                    fns = []
                    for gg in range(2):
                        g = 2 * gp + gg
                        fns.append(lambda g=g, gg=gg: pe.matmul(pst[64 * gg:64 * gg + 64, 0:128], lhsT=zb[:, q, 64 * g:64 * g + 64],
                                                               rhs=pmat[:, 4 * mi + g, :], start=True, stop=False, tile_position=(0, 64 * gg)))
                        fns.append(lambda g=g, gg=gg: pe.matmul(pst[64 * gg:64 * gg + 64, 0:128], lhsT=prev_ap[:, 64 * g:64 * g + 64],
                                                               rhs=pmat[:, 4 * hi + g, :], start=False, stop=True, tile_position=(0, 64 * gg)))
                    grp(PE, fns, reads=[zb_b[q], b_gconst] + prev_bufs, writes=[psb])
                    op(ACT, lambda: a_.copy(out=pooled, in_=pst[:, 0:128]), reads=[psb], writes=[b_pooled])
                    pst2, psb2 = PSN()
                    grp(PE, [lambda gp=gp: pe.matmul(pst2[:, 0:128], lhsT=WP[:, gp, :], rhs=pooled, start=True, stop=True)],
                        reads=[b_pooled, b_lconst], writes=[psb2])
                    op(DVE, lambda gp=gp: v_.tensor_scalar(
                        out=yT[:, 3 + gp, q * 128:(q + 1) * 128],
                        in0=pst2[:, 0:128], scalar1=pscale[:, gp:gp + 1], scalar2=None,
                        op0=ALU.mult),
                       reads=[psb2, b_lconst], writes=[yb_b[q // 4]])

            for q in range(1, 16):
                pool_chunk(q, zb[:, q - 1, :], [zb_b[q - 1]], 0, 1)

            stop_at(20 * l + 8)
            Gs32 = Gs[:, :, 0:48].bitcast(F32)
            g24 = carve(TMP + 4096, [128, 24, 8], F32)
            sel = carve(TMP + 4096 + 768, [128, 3, 24], F32)
            for d in range(3):
                V(lambda d=d: v_.tensor_tensor(out=g24, in0=mkap(Gs32, 0, [[1, 24], [GW // 2, 8]]),
                                               in1=mkap(oh, d * 8, [[0, 24], [1, 8]]), op=ALU.mult), r=[b_Gs, b_gconst])
                V(lambda d=d: v_.tensor_reduce(out=sel[:, d, :], in_=g24, axis=AX.X, op=ALU.add))
            V(lambda: v_.tensor_copy(out=hinr, in_=sel[:, 2, 0:12]))
            V(lambda: v_.tensor_copy(out=hini, in_=sel[:, 2, 12:24]))
            hr2, hi2 = sm2[:, 4, :], sm2[:, 5, :]
            for d in (1, 0):
                cmul(hr2, hi2, A2048r, A2048i, hinr, hini, sm2[:, 6, :], sm2[:, 7, :])
                tt_(hinr, hr2, sel[:, d, 0:12], ALU.add)
                tt_(hini, hi2, sel[:, d, 12:24], ALU.add)
            xs_a = carve(22528, [128, 12, 16], F32)
            xs_b = carve(23296, [128, 12, 16], F32)
            cmul(Hre, Him, P128r, P128i, bc16(hinr), bc16(hini), xs_a, xs_b)
            tt_(Hre[:, :, 1:16], Hre[:, :, 1:16], Xre[:, :, 0:15], ALU.add)
            tt_(Him[:, :, 1:16], Him[:, :, 1:16], Xim[:, :, 0:15], ALU.add)
            for p in range(12):
                pr_ = mkap(P8r, p * 16, [[0, 16], [1, 16]])
                pi_ = mkap(P8i, p * 16, [[0, 16], [1, 16]])
                hr_ = mkap(Hre, p * 16, [[1, 16], [0, 16]])
                hi_ = mkap(Him, p * 16, [[1, 16], [0, 16]])

                def D2(fn):
                    op(DVE, fn, reads=[b_su, b_scan, b_Sb], writes=[b_scan, b_Sb])
                D2(lambda: v_.tensor_tensor(out=v3(t1), in0=pr_, in1=hr_, op=ALU.mult))
                D2(lambda: v_.tensor_tensor(out=v3(t2), in0=pi_, in1=hi_, op=ALU.mult))
                D2(lambda: v_.tensor_tensor(out=wr, in0=t1, in1=t2, op=ALU.subtract))
                D2(lambda p=p: v_.tensor_tensor(out=Sb[:, 0, p, 1:257], in0=Sb[:, 0, p, 1:257], in1=wr, op=ALU.add))
                D2(lambda: v_.tensor_tensor(out=v3(t1), in0=pr_, in1=hi_, op=ALU.mult))
                D2(lambda: v_.tensor_tensor(out=v3(t2), in0=pi_, in1=hr_, op=ALU.mult))
                D2(lambda: v_.tensor_tensor(out=wi, in0=t1, in1=t2, op=ALU.add))
                D2(lambda p=p: v_.tensor_tensor(out=Sb[:, 1, p, 1:257], in0=Sb[:, 1, p, 1:257], in1=wi, op=ALU.add))
            op(DVE, lambda: v_.tensor_copy(out=Sb[:, 0, :, 0], in_=hinr), reads=[b_su], writes=[b_Sb])
            op(DVE, lambda: v_.tensor_copy(out=Sb[:, 1, :, 0], in_=hini), reads=[b_su], writes=[b_Sb])

            stop_at(20 * l + 9)
            dbg_dump(f"Sb_{l}", Sb.rearrange("p a b c -> p (a b c)"), [b_Sb])
            dbg_dump(f"sm_{l}", carve(SM, [128, 1344], F32), [b_su])
            for k in range(3):
                for r in range(7, -1, -1):
                    pst, psb = PSN()
                    fns = []
                    for tau in range(r + 1):
                        fns.append(lambda tau=tau: pe.matmul(
                            pst[:, 0:256], lhsT=Ktap[:, k, tau, :],
                            rhs=mkap(yT, k * T + (r - tau) * 16, [[128, 16], [1, 16]]),
                            start=(tau == 0), stop=False))
                    for qd in range(4):
                        p = 4 * k + qd
                        for ri in range(2):
                            fns.append(lambda qd=qd, p=p, ri=ri: pe.matmul(
                                pst[32 * qd:32 * qd + 32, 0:256], lhsT=Cab[:, p, r + 1, ri, :],
                                rhs=Sb[:, ri, p, 0:256], start=False, stop=(ri == 1), tile_position=(0, 32 * qd)))
                    grp(PE, fns, reads=[b_su, b_Sb] + za_b[k][0:r + 1], writes=[psb])
                    op(ACT, lambda: a_.activation(
                        out=mkap(yT, k * T + r * 16, [[128, 16], [1, 16]]),
                        in_=pst[:, 0:256].rearrange("p (a b) -> p a b", a=16), func=AF.Gelu_apprx_tanh),
                       reads=[psb], writes=[za_b[k][r]])
            dbg_dump(f"gT_{l}", yT[:, 0:3, :], [za_b[k][r] for k in range(3) for r in range(8)])
            stop_at(20 * l + 10)
            sig = carve(24 * KB + 256, [128, 512], BF16)
            b_sig = Buf("sig")
            for tt in range(4):
                rbs = [za_b[k][i] for k in range(3) for i in range(8)]
                pss = []
                for m3 in range(3):
                    pst, psb = PSN()
                    grp(PE, [(lambda kk=kk: pe.matmul(pst, lhsT=wglu[:, kk, m3 * 128:(m3 + 1) * 128],
                                                      rhs=yT[:, kk, tt * 512:(tt + 1) * 512], start=(kk == 0), stop=(kk == 2)))
                             for kk in range(3)], reads=rbs + [b_lconst], writes=[psb])
                    pss.append((pst, psb))
                for m3 in range(3):
                    pst, psb = pss[m3]
                    op(ACT, lambda: a_.activation(out=sig, in_=pst, func=AF.Sigmoid, bias=bglu[:, m3:m3 + 1], scale=1.0),
                       reads=[psb, b_lconst], writes=[b_sig])
                    op(DVE, lambda: v_.tensor_tensor(out=yT[:, m3, tt * 512:(tt + 1) * 512], in0=yT[:, m3, tt * 512:(tt + 1) * 512],
                                                     in1=sig, op=ALU.mult),
                       reads=[b_sig] + rbs, writes=za_b[m3])
            stop_at(20 * l + 11)
            zprev = carve(TMP + 4096 + 1536, [128, 256], BF16)
            zt8 = carve(0 + 13 * KB, [128, 64, 8], F32)
            b_zprev = Buf("zprev")
            for c4 in range(4):
                op(DVE, lambda c4=c4: v_.tensor_tensor(out=zt8, in0=mkap(Gs, 48 + 64 * c4, [[1, 64], [GW, 8]]),
                                                       in1=mkap(oh, 0, [[0, 64], [1, 8]]), op=ALU.mult),
                   reads=[b_Gs, b_gconst, b_Sb], writes=[b_zprev])
                with nc.allow_low_precision("one-hot select: exactly one non-zero term"):
                    op(DVE, lambda c4=c4: v_.tensor_reduce(out=zprev[:, 64 * c4:64 * c4 + 64], in_=zt8, axis=AX.X, op=ALU.add),
                       reads=[b_zprev], writes=[b_zprev])
            pool_chunk(0, zprev, [b_zprev], 2, 3)
            dbg_dump(f"yT_p2_{l}", yT, [za_b[k][r] for k in range(3) for r in range(8)] + yc_b + yb_b)

            stop_at(20 * l + 12)
            for m in range(8):
                s = m % 2
                dma(POOL, ds_wo[s], wo[s], d_wout[l, :, m * 128:(m + 1) * 128].rearrange("(k p) c -> p k c", p=128), writes=[b_wo[s]])
                for tt in range(4):
                    pst, psb = PSN()
                    rbs = [za_b[k][i] for k in range(3) for i in range(8)] + [yb_b[tt], yc_b[tt]]
                    grp(PE, [(lambda k=k: pe.matmul(pst, lhsT=wo[s][:, k, :], rhs=yT[:, k, tt * 512:(tt + 1) * 512],
                                                    start=(k == 0), stop=(k == 7))) for k in range(8)],
                        reads=rbs + [b_wo[s]], writes=[psb])
                    op(DVE, lambda: v_.tensor_tensor(out=xT[:, m, tt * 512:(tt + 1) * 512], in0=pst, in1=xT[:, m, tt * 512:(tt + 1) * 512],
                                                     op=ALU.add), reads=[psb], writes=[xT_b[tt]])
            barrier()
            dbg_dump(f"x_mix_{l}", xT, xT_b)

            stop_at(20 * l + 13)
            aT = carve(32 * KB, [128, NF, 1024], BF16)
            b_aT = [Buf("aT0"), Buf("aT1")]
            wgu = [carve(76 * KB + i * 4096, [128, 2, 8, 128], BF16) for i in range(3)]
            wd = [carve(88 * KB + i * 5632, [128, NF, 128], BF16) for i in range(3)]
            sg = carve(TMP, [128, 512], BF16)
            b_sg = Buf("sg")
            ci = [0, 0]
            for t2_ in range(2):
                for lt in range(2):
                    norm_to_hT(2 * l + 1, 2 * t2_ + lt, lt)
                for f in range(NF):
                    s = ci[0] % 3
                    ci[0] += 1
                    dma(POOL, ds_wgu[s], wgu[s][:, 0], d_wg[l, :, f * 128:(f + 1) * 128].rearrange("(k p) c -> p k c", p=128), writes=[b_wgu[s]])
                    dma(POOL, ds_wgu[s], wgu[s][:, 1], d_wu[l, :, f * 128:(f + 1) * 128].rearrange("(k p) c -> p k c", p=128), writes=[b_wgu[s]])
                    for lt in range(2):
                        pg, pgb = PSN()
                        pu, pub = PSN()
                        grp(PE, [(lambda k=k: pe.matmul(pg, lhsT=wgu[s][:, 0, k, :], rhs=hT[:, k, lt * 512:(lt + 1) * 512],
                                                        start=(k == 0), stop=(k == 7))) for k in range(8)],
                            reads=[b_wgu[s], hT_b[lt]], writes=[pgb])
                        grp(PE, [(lambda k=k: pe.matmul(pu, lhsT=wgu[s][:, 1, k, :], rhs=hT[:, k, lt * 512:(lt + 1) * 512],
                                                        start=(k == 0), stop=(k == 7))) for k in range(8)],
                            reads=[b_wgu[s], hT_b[lt]], writes=[pub])
                        op(ACT, lambda: a_.activation(out=sg, in_=pg, func=AF.Silu), reads=[pgb], writes=[b_sg])
                        op(DVE, lambda: v_.tensor_tensor(out=aT[:, f, lt * 512:(lt + 1) * 512], in0=sg, in1=pu, op=ALU.mult),
                           reads=[b_sg, pub], writes=[b_aT[lt]])
                for m in range(8):
                    s = ci[1] % 3
                    ci[1] += 1
                    dma(POOL, ds_wd[s], wd[s], d_wd[l, :, m * 128:(m + 1) * 128].rearrange("(f p) c -> p f c", p=128), writes=[b_wd[s]])
                    for lt in range(2):
                        tt = 2 * t2_ + lt
                        pst, psb = PSN()
                        grp(PE, [(lambda f=f: pe.matmul(pst, lhsT=wd[s][:, f, :], rhs=aT[:, f, lt * 512:(lt + 1) * 512],
                                                        start=(f == 0), stop=(f == NF - 1))) for f in range(NF)],
                            reads=[b_wd[s], b_aT[lt]], writes=[psb])
                        op(DVE, lambda: v_.tensor_tensor(out=xT[:, m, tt * 512:(tt + 1) * 512], in0=pst,
                                                         in1=xT[:, m, tt * 512:(tt + 1) * 512], op=ALU.add),
                           reads=[psb], writes=[xT_b[tt]])
            barrier()
            dbg_dump(f"x_ffn_{l}", xT, xT_b)

        ost = carve(0, [128, 8, 512], F32)
        b_ost = Buf("ost")
        for tt in range(4):
            def outs(c0):
                for k in range(8):
                    op(DVE, lambda k=k: v_.scalar_tensor_tensor(out=ost[:, k, :], in0=xT[:, k, c0:c0 + 512], scalar=gcat[:, 4, k:k + 1],
                                                                in1=rs, op0=ALU.mult, op1=ALU.mult),
                       reads=[xT_b[tt], rs_b, b_gconst], writes=[b_ost])
            norm_tile(4, tt, outs)
            dma(SP, ds_out, d_out[:, :, tt * 512:(tt + 1) * 512], ost, reads=[b_ost])
    except _Stop:
        pass
    for dsm in ALL_DSEMS:
        if dsm.count:
            nc.sync.wait_ge(dsm.sem, dsm.count)
    if ds_dbg.count:
        nc.sync.wait_ge(ds_dbg.sem, ds_dbg.count)
    barrier()
    es.close()
    return nc


def _prep_inputs(x, g_mix, w_in, A_re, A_im, log_dt, B_re, B_im, C_re, C_im, D_skip, w_glu, b_glu, w_pool,
                 pool_scale, sgu_ln_g, sgu_ln_b, w_spatial, b_spatial, w_out, g_ffn, w_gate, w_up, w_down, g_final):
    f = np.float32
    tok = _pos_to_tok()
    sperm = _chunk_perm()
    shared = {}
    colperm = np.concatenate([np.arange(0, 384), np.arange(640, 1024), np.arange(384, 640), np.arange(1024, 1408)])
    shared["w_in"] = np.ascontiguousarray(np.asarray(w_in, f)[:, :, colperm])
    shared["w_out"] = np.ascontiguousarray(np.asarray(w_out, f))
    shared["w_gate"] = np.ascontiguousarray(np.asarray(w_gate, f))
    shared["w_up"] = np.ascontiguousarray(np.asarray(w_up, f))
    shared["w_down"] = np.ascontiguousarray(np.asarray(w_down, f))
    shared["w_glu"] = np.ascontiguousarray(np.asarray(w_glu, f))
    gs = [np.asarray(g_mix, f)[0], np.asarray(g_ffn, f)[0], np.asarray(g_mix, f)[1], np.asarray(g_ffn, f)[1], np.asarray(g_final, f)]
    shared["gcat"] = np.ascontiguousarray(np.stack([g.reshape(8, 128).T for g in gs], axis=1))

    def gn(a):
        a = np.asarray(a, f).reshape(2, 12, 2, 64)
        return a.transpose(0, 2, 3, 1).reshape(2, 128, 12)
    ldt = np.broadcast_to(np.asarray(log_dt, f)[:, :, None], (2, 24, 64))
    shared["ssm_s"] = np.ascontiguousarray(np.stack([gn(A_re), gn(A_im), gn(ldt)], axis=2))

    def gB(a):
        a = np.asarray(a, f).reshape(2, 12, 2, 64, 16)
        return a.transpose(0, 2, 3, 1, 4).reshape(2, 128, 12, 16)

    def gC(a):
        a = np.asarray(a, f).reshape(2, 12, 2, 16, 64)
        return a.transpose(0, 2, 4, 1, 3).reshape(2, 128, 12, 16)
    shared["ssm_bc"] = np.ascontiguousarray(np.stack([gB(B_re), gB(B_im), gC(C_re), gC(C_im)], axis=2))
    vec = np.concatenate([np.asarray(D_skip, f).reshape(2, 3, 128).transpose(0, 2, 1),
                          np.asarray(b_glu, f).reshape(2, 3, 128).transpose(0, 2, 1),
                          np.asarray(pool_scale, f).reshape(2, 2, 128).transpose(0, 2, 1),
                          np.asarray(sgu_ln_g, f).reshape(2, 3, 128).transpose(0, 2, 1)], axis=2)
    shared["vecs"] = np.ascontiguousarray(vec)
    shared["lnb"] = np.ascontiguousarray(np.broadcast_to(np.asarray(sgu_ln_b, f)[:, None, :], (2, 128, 384)))
    wp = np.zeros((2, 128, 2, 128), f)
    wpn = np.asarray(w_pool, f)
    for gp in range(2):
        for gg in range(2):
            wp[:, 64 * gg:64 * gg + 64, gp, 64 * gg:64 * gg + 64] = wpn[:, 2 * gp + gg]
    shared["wpool"] = wp
    ws = np.asarray(w_spatial, f)
    wsp = ws[:, :, sperm][:, :, :, sperm]
    shared["wsT"] = np.ascontiguousarray(wsp.transpose(0, 3, 1, 2))
    shared["bsp"] = np.ascontiguousarray(np.asarray(b_spatial, f)[:, :, sperm][:, None])
    shared["cmask"] = (sperm[:, None] <= sperm[None, :]).astype(f)
    shared["ident"] = np.eye(128, dtype=f)

    def permT(mat):
        return mat[:, sperm][:, :, sperm].transpose(2, 0, 1)
    gen_m, gen_h = _pool_mats(False)
    fst_m, fst_h = _pool_mats(True)
    xs = np.asarray(x, f)
    in_maps = []
    for c in range(NCORES):
        b, p = c // 4, c % 4
        xc = xs[b, p * T:(p + 1) * T][tok]
        xT = np.ascontiguousarray(xc.T.reshape(8, 128, T).transpose(1, 0, 2))
        m0, h0 = (fst_m, fst_h) if p == 0 else (gen_m, gen_h)
        pm = np.concatenate([permT(gen_m), permT(gen_h), permT(m0), permT(h0)], axis=1)
        ohv = np.zeros((128, 3, 8), f)
        for d in range(1, 4):
            if p - d >= 0:
                ohv[:, d - 1, c - d] = 1.0
        mp = dict(shared)
        mp["xT"] = xT
        mp["pmat"] = np.ascontiguousarray(pm.astype(f))
        mp["oh"] = ohv
        in_maps.append(mp)
    return in_maps, tok


def _assemble(results, tok, key="outT"):
    out = np.zeros((2, 4 * T, DM), np.float32)
    inv = np.empty_like(tok)
    inv[tok] = np.arange(T)
    for c in range(NCORES):
        b, p = c // 4, c % 4
        oT = np.asarray(results[c][key])
        xc = oT.transpose(1, 0, 2).reshape(DM, T).T
        out[b, p * T:(p + 1) * T] = xc[inv]
    return out


def kernel(**inputs):
    in_maps, tok = _prep_inputs(**inputs)
    nc = build_program()
    res = run_bass_kernel_spmd(nc, in_maps, core_ids=list(range(NCORES)))
    return _assemble(res.results, tok)
```

```python
import math
from contextlib import ExitStack

import numpy as np
import concourse.bass as bass
import concourse.mybir as mybir
from concourse.bass_utils import run_bass_kernel_spmd

F32 = mybir.dt.float32
BF16 = mybir.dt.bfloat16
AF = mybir.ActivationFunctionType
ALU = mybir.AluOpType
AX = mybir.AxisListType

NCORES = 8
T = 2048
DM = 1024
DFF = 2816
NF = DFF // 128
EPS = 1e-6
GW = 48 + 256


class Eng:
    def __init__(self, e, sem, name):
        self.e, self.sem, self.name = e, sem, name
        self.count = 0
        self.waited = {}


ALL_DSEMS = []


class DSem:
    def __init__(self, sem, name):
        self.sem, self.name = sem, name
        self.count = 0
        ALL_DSEMS.append(self)


class Buf:
    __slots__ = ("name", "w", "r")

    def __init__(self, name):
        self.name = name
        self.w = None
        self.r = {}


def _sync(E, reads, writes):
    deps = []
    for b in reads:
        if b.w is not None:
            deps.append(b.w)
    for b in writes:
        if b.w is not None:
            deps.append(b.w)
        deps.extend(b.r.items())
    for src, val in deps:
        if src is E and E.name == "pe":
            continue
        if E.waited.get(src, 0) < val:
            E.e.wait_ge(src.sem, val)
            E.waited[src] = val


def _mark(src, val, reads, writes):
    for b in reads:
        if b.r.get(src, 0) < val:
            b.r[src] = val
    for b in writes:
        b.w = (src, val)
        b.r = {}


def op(E, fn, reads=(), writes=()):
    _sync(E, reads, writes)
    ins = fn()
    E.count += 1
    ins.then_inc(E.sem, 1)
    _mark(E, E.count, reads, writes)


def grp(E, fns, reads=(), writes=()):
    _sync(E, reads, writes)
    ins = None
    for f in fns:
        ins = f()
    E.count += 1
    ins.then_inc(E.sem, 1)
    _mark(E, E.count, reads, writes)


def dma(Q, dsem, out, in_, reads=(), writes=()):
    _sync(Q, reads, writes)
    ins = Q.e.dma_start(out=out, in_=in_)
    dsem.count += 16
    ins.then_inc(dsem.sem, 16)
    _mark(dsem, dsem.count, reads, writes)


def mkap(base, off, pat, p0=None, np_=None):
    b = base if p0 is None else base[p0:p0 + np_]
    pp = list(b.ap[0])
    return bass.AP(tensor=b.tensor, offset=b.offset + off, ap=[pp] + [list(x) for x in pat])


def _pos_to_tok():
    tok = np.zeros(T, np.int64)
    for q in range(16):
        for r in range(8):
            jj = np.arange(16)
            tok[q * 128 + r * 16 + jj] = 128 * q + 8 * jj + r
    return tok


def _chunk_perm():
    m = np.arange(128)
    return 8 * (m % 16) + (m // 16)


_POOL_W = (2, 4, 8, 16)


def _pool_mats(first_chunk_of_seq):
    main = np.zeros((4, 128, 128), np.float32)
    halo = np.zeros((4, 128, 128), np.float32)
    for g, w in enumerate(_POOL_W):
        for t in range(128):
            cnt = min(t + 1, w) if first_chunk_of_seq else w
            for s in range(t - w + 1, t + 1):
                if s >= 0:
                    main[g, t, s] += 1.0 / cnt
                elif not first_chunk_of_seq:
                    halo[g, t, s + 128] += 1.0 / cnt
            main[g, t, t] -= 1.0
    return main, halo


class _Stop(Exception):
    pass


def build_program(dbg=None, stop=None):
    ALL_DSEMS.clear()
    nc = bass.Bass("TRN2", target_bir_lowering=False)
    es = ExitStack()

    def din(name, shape, dt=F32):
        return nc.dram_tensor(name, list(shape), dt, kind="ExternalInput").ap()

    d_x = din("xT", [128, 8, T])
    d_win = din("w_in", [2, DM, 1408])
    d_wout = din("w_out", [2, DM, DM])
    d_wg = din("w_gate", [2, DM, DFF])
    d_wu = din("w_up", [2, DM, DFF])
    d_wd = din("w_down", [2, DFF, DM])
    d_wglu = din("w_glu", [2, 384, 384])
    d_gcat = din("gcat", [128, 5, 8])
    d_ssms = din("ssm_s", [2, 128, 3, 12])
    d_ssmbc = din("ssm_bc", [2, 128, 4, 12, 16])
    d_vecs = din("vecs", [2, 128, 11])
    d_lnb = din("lnb", [2, 128, 384])
    d_wpool = din("wpool", [2, 128, 2, 128])
    d_wsT = din("wsT", [2, 128, 6, 128])
    d_bsp = din("bsp", [2, 1, 6, 128])
    d_cmask = din("cmask", [128, 128])
    d_pmat = din("pmat", [128, 16, 128])
    d_oh = din("oh", [128, 3, 8])
    d_ident = din("ident", [128, 128])
    d_out = nc.dram_tensor("outT", [128, 8, T], F32, kind="ExternalOutput").ap()
    d_agin = [nc.dram_tensor(f"ag_in{l}", [128, GW], BF16) for l in range(2)]
    d_agout = [nc.dram_tensor(f"ag_out{l}", [128 * NCORES, GW], BF16) for l in range(2)]
    d_dbg = {}
    if dbg:
        for name, shape in dbg.items():
            d_dbg[name] = nc.dram_tensor("dbg_" + name, list(shape), F32, kind="ExternalOutput").ap()

    def sem(name):
        return es.enter_context(nc.semaphore(name))

    PE = Eng(nc.tensor, sem("s_pe"), "pe")
    ACT = Eng(nc.scalar, sem("s_act"), "act")
    DVE = Eng(nc.vector, sem("s_dve"), "dve")
    POOL = Eng(nc.gpsimd, sem("s_pool"), "pool")
    SP = Eng(nc.sync, sem("s_sp"), "sp")
    ENGS = [PE, ACT, DVE, POOL, SP]

    def barrier():
        for E in ENGS:
            for E2 in ENGS:
                if E2 is not E and E2.count > 0 and E.waited.get(E2, 0) < E2.count:
                    E.e.wait_ge(E2.sem, E2.count)
                    E.waited[E2] = E2.count

    def sb(name, shape, dt):
        return nc.alloc_sbuf_tensor("sb_" + name, list(shape), dt).ap()

    xT = sb("xT", [128, 8, T], F32)
    xT_b = [Buf(f"xT{t}") for t in range(4)]
    ones_bf = sb("ones_bf", [128, 128], BF16)
    ident = sb("ident", [128, 128], F32)
    gcat = sb("gcat", [128, 5, 8], F32)
    cmask = sb("cmask", [128, 128], BF16)
    pmat = sb("pmat", [128, 16, 128], BF16)
    oh = sb("oh", [128, 3, 8], F32)
    kc = sb("kc", [128, 8], F32)
    ssm_s = sb("ssm_s", [128, 3, 12], F32)
    ssm_bc = sb("ssm_bc", [128, 4, 12, 16], F32)
    vecs = sb("vecs", [128, 11], F32)
    LNB = sb("lnb", [128, 384], BF16)
    WP = sb("wpool", [128, 2, 128], BF16)
    WsT = sb("wsT", [128, 6, 128], BF16)
    bsp = sb("bsp", [1, 6, 128], BF16)
    wglu = sb("wglu", [128, 3, 384], BF16)
    cterm = sb("cterm", [128, 3, 128], F32)
    b_gconst = Buf("gconst")
    b_lconst = Buf("lconst")
    b_cterm = Buf("cterm")
    ds_g = DSem(sem("ds_g"), "ds_g")
    ds_l = [DSem(sem(f"ds_l{l}"), f"ds_l{l}") for l in range(2)]

    AR_BYTES = 122 * 1024
    arena = sb("arena", [128, AR_BYTES // 4], F32)

    def carve(off, shape, dt):
        esz = 2 if dt == BF16 else 4
        n = int(np.prod(shape[1:]))
        assert off % 4 == 0 and (n * esz) % 4 == 0
        assert off + n * esz <= AR_BYTES, (off, n * esz)
        v = arena[:, off // 4: (off + n * esz) // 4]
        if dt != F32:
            v = v.bitcast(dt)
        if len(shape) == 2:
            return v
        names = " ".join(f"d{i}" for i in range(len(shape) - 1))
        kw = {f"d{i}": shape[i + 1] for i in range(len(shape) - 1)}
        return v.rearrange(f"p ({names}) -> p {names}", **kw)

    KB = 1024
    hT = carve(0, [128, 8, 1024], BF16)
    hT_b = [Buf("hT0"), Buf("hT1")]
    sq = carve(16 * KB, [128, 8, 512], BF16)
    sq_b = Buf("sq")
    rs = carve(24 * KB, [128, 512], F32)
    rs_b = Buf("rs")
    TMP = 26 * KB
    yT = carve(32 * KB, [128, 8, T], BF16)
    za_b = [[Buf(f"za{k}_{r}") for r in range(8)] for k in range(3)]
    yb_b = [Buf(f"yb{t}") for t in range(4)]
    yc_b = [Buf(f"yc{t}") for t in range(4)]
    zb = carve(64 * KB, [128, 16, 256], BF16)
    zb_b = [Buf(f"zb{q}") for q in range(16)]
    RA = 72 * KB

    ps_t = [nc.alloc_psum_tensor(f"ps{i}", [128, 512], F32).ap() for i in range(8)]
    ps_b = [Buf(f"ps{i}") for i in range(8)]
    ps_i = [0]

    def PSN():
        i = ps_i[0] % 8
        ps_i[0] += 1
        return ps_t[i], ps_b[i]

    v_ = nc.vector
    a_ = nc.scalar
    pe = nc.tensor

    dma(SP, ds_g, ident, d_ident, writes=[b_gconst])
    dma(SP, ds_g, gcat, d_gcat, writes=[b_gconst])
    dma(SP, ds_g, oh, d_oh, writes=[b_gconst])
    dma(POOL, ds_g, cmask, d_cmask, writes=[b_gconst])
    dma(POOL, ds_g, pmat, d_pmat, writes=[b_gconst])
    b_gconst.w = (ds_g, ds_g.count)
    for t in range(4):
        dsx = DSem(sem(f"ds_x{t}"), f"ds_x{t}")
        dma(SP, dsx, xT[:, :, t * 512:(t + 1) * 512], d_x[:, :, t * 512:(t + 1) * 512], writes=[xT_b[t]])
    b_kc = Buf("kc")
    op(DVE, lambda: v_.memset(ones_bf, 1.0), writes=[b_kc])
    op(DVE, lambda: v_.memset(kc[:, 0:1], EPS), writes=[b_kc])
    op(DVE, lambda: v_.memset(kc[:, 1:2], -math.pi), writes=[b_kc])
    op(DVE, lambda: v_.memset(kc[:, 2:3], 0.0), writes=[b_kc])
    KEPS = kc[:, 0:1]
    KNPI = kc[:, 1:2]

    def norm_tile(gi, tt, outs_fn, reads_extra=()):
        c0 = tt * 512
        op(ACT, lambda: a_.activation(out=sq, in_=xT[:, :, c0:c0 + 512], func=AF.Square),
           reads=[xT_b[tt]], writes=[sq_b])
        pst, psb = PSN()
        grp(PE, [(lambda k=k: pe.matmul(pst, lhsT=ones_bf, rhs=sq[:, k, :], start=(k == 0), stop=(k == 7)))
                 for k in range(8)], reads=[sq_b, b_kc], writes=[psb])
        op(ACT, lambda: a_.activation(out=rs, in_=pst, func=AF.Sqrt, bias=KEPS, scale=1.0 / DM),
           reads=[psb, b_kc], writes=[rs_b])
        op(DVE, lambda: v_.reciprocal(out=rs, in_=rs), reads=[rs_b], writes=[rs_b])
        outs_fn(c0)

    def norm_to_hT(gi, tt, lt):
        def outs(c0):
            for k in range(8):
                op(DVE, lambda k=k: v_.scalar_tensor_tensor(
                    out=hT[:, k, lt * 512:(lt + 1) * 512], in0=xT[:, k, c0:c0 + 512],
                    scalar=gcat[:, gi, k:k + 1], in1=rs, op0=ALU.mult, op1=ALU.mult),
                   reads=[xT_b[tt], rs_b, b_gconst], writes=[hT_b[lt]])
        norm_tile(gi, tt, outs)

    ds_win = DSem(sem("ds_win"), "ds_win")
    b_win = Buf("w_in")
    ds_wo = [DSem(sem(f"ds_wo{i}"), f"ds_wo{i}") for i in range(2)]
    b_wo = [Buf(f"wo{i}") for i in range(2)]
    ds_wgu = [DSem(sem(f"ds_wgu{i}"), f"ds_wgu{i}") for i in range(3)]
    b_wgu = [Buf(f"wgu{i}") for i in range(3)]
    ds_wd = [DSem(sem(f"ds_wd{i}"), f"ds_wd{i}") for i in range(3)]
    b_wd = [Buf(f"wd{i}") for i in range(3)]
    ds_ag = DSem(sem("ds_ag"), "ds_ag")
    ds_out = DSem(sem("ds_out"), "ds_out")
    cc_sem = sem("cc_sem")
    cc_cnt = [0]
    ds_dbg = DSem(sem("ds_dbg"), "ds_dbg")

    def stop_at(i):
        if stop is not None and stop == i:
            raise _Stop()

    def dbg_dump(name, ap, reads):
        if name in d_dbg:
            dma(POOL, ds_dbg, d_dbg[name], ap, reads=reads)

    try:
        for l in range(2):
            stop_at(20 * l)
            dsl = ds_l[l]
            dma(SP, dsl, ssm_s, d_ssms[l], writes=[b_lconst])
            dma(SP, dsl, ssm_bc, d_ssmbc[l], writes=[b_lconst])
            dma(SP, dsl, vecs, d_vecs[l], writes=[b_lconst])
            dma(POOL, dsl, LNB, d_lnb[l], writes=[b_lconst])
            dma(POOL, dsl, WP, d_wpool[l], writes=[b_lconst])
            dma(POOL, dsl, WsT, d_wsT[l], writes=[b_lconst])
            dma(POOL, dsl, bsp, d_bsp[l], writes=[b_lconst])
            dma(POOL, dsl, wglu, d_wglu[l].rearrange("(k p) c -> p k c", p=128), writes=[b_lconst])
            b_lconst.w = (dsl, dsl.count)
            Dsk = vecs[:, 0:3]
            bglu = vecs[:, 3:6]
            pscale = vecs[:, 6:8]
            lng = vecs[:, 8:11]

            w_in = carve(RA, [128, 8, 1408], BF16)
            for k in range(8):
                dma(POOL, ds_win, w_in[:, k, :], d_win[l, k * 128:(k + 1) * 128, :], writes=[b_win])
            uT = carve(RA + 22528, [128, 3, 1024], BF16)
            uT_b = [Buf("uT0"), Buf("uT1")]
            v32s = [carve(TMP + i * 1536, [128, 384], F32) for i in range(2)]
            vhats = [carve(TMP + 3072 + i * 768, [128, 384], BF16) for i in range(2)]
            stts = [carve(TMP + 4608 + i * 32, [128, 8], F32) for i in range(2)]
            tmpcs = [carve(TMP + 4672 + i * 512, [128, 128], F32) for i in range(2)]
            b_v32s, b_vhats = [Buf("v32a"), Buf("v32b")], [Buf("vha"), Buf("vhb")]
            b_stts, b_tmpcs = [Buf("sta"), Buf("stb")], [Buf("tca"), Buf("tcb")]

            op(DVE, lambda: v_.tensor_tensor(out=WsT, in0=WsT, in1=mkap(cmask, 0, [[0, 6], [1, 128]]), op=ALU.mult),
               reads=[b_lconst, b_gconst], writes=[b_lconst])
            for hp in range(3):
                pst, psb = PSN()
                fns = []
                for hh in range(2):
                    h = 2 * hp + hh
                    fns.append(lambda h=h, hh=hh: pe.matmul(pst[64 * hh:64 * hh + 64, 0:128], lhsT=LNB[:, 64 * h:64 * h + 64],
                                                           rhs=WsT[:, h, :], start=True, stop=False, tile_position=(0, 64 * hh)))
                    fns.append(lambda h=h, hh=hh: pe.matmul(pst[64 * hh:64 * hh + 64, 0:128], lhsT=ones_bf[0:1, 0:64],
                                                           rhs=bsp[0:1, h, :], start=False, stop=True, tile_position=(0, 64 * hh)))
                grp(PE, fns, reads=[b_lconst, b_kc], writes=[psb])
                op(ACT, lambda hp=hp: a_.copy(out=cterm[:, hp, :], in_=pst[:, 0:128]), reads=[psb], writes=[b_cterm])

            stop_at(20 * l + 1)
            for hf in range(2):
                for lt in range(2):
                    norm_to_hT(2 * l, 2 * hf + lt, lt)
                for k3 in range(3):
                    for lt in range(2):
                        tt = 2 * hf + lt
                        pst, psb = PSN()
                        grp(PE, [(lambda k=k: pe.matmul(pst, lhsT=w_in[:, k, k3 * 128:(k3 + 1) * 128],
                                                        rhs=hT[:, k, lt * 512:(lt + 1) * 512], start=(k == 0), stop=(k == 7)))
                                 for k in range(8)], reads=[b_win, hT_b[lt]], writes=[psb])
                        rb = za_b[k3]
                        op(ACT, lambda: a_.copy(out=yT[:, k3, tt * 512:(tt + 1) * 512], in_=pst), reads=[psb], writes=rb)
                        pst2, psb2 = PSN()
                        grp(PE, [(lambda k=k: pe.matmul(pst2, lhsT=w_in[:, k, 384 + k3 * 128:384 + (k3 + 1) * 128],
                                                        rhs=hT[:, k, lt * 512:(lt + 1) * 512], start=(k == 0), stop=(k == 7)))
                                 for k in range(8)], reads=[b_win, hT_b[lt]], writes=[psb2])
                        op(ACT, lambda: a_.activation(out=uT[:, k3, lt * 512:(lt + 1) * 512], in_=pst2, func=AF.Gelu_apprx_tanh),
                           reads=[psb2], writes=[uT_b[lt]])
                stop_at(20 * l + 2)
                def stageA(qq):
                    q = 8 * hf + qq
                    i2 = q % 2
                    v32_, vhat_, stt_ = v32s[i2], vhats[i2], stts[i2]
                    bv, bh, bs = b_v32s[i2], b_vhats[i2], b_stts[i2]
                    pB, pBb = PSN()
                    pV, pVb = PSN()
                    lhs = [hT[:, k, qq * 128:(qq + 1) * 128] for k in range(8)]
                    grp(PE, [(lambda k=k: pe.matmul(pB[:, 0:256], lhsT=lhs[k], rhs=w_in[:, k, 768:1024], start=(k == 0), stop=(k == 7)))
                             for k in range(8)], reads=[b_win, hT_b[qq // 4]], writes=[pBb])
                    grp(PE, [(lambda k=k: pe.matmul(pV[:, 0:384], lhsT=lhs[k], rhs=w_in[:, k, 1024:1408], start=(k == 0), stop=(k == 7)))
                             for k in range(8)], reads=[b_win, hT_b[qq // 4]], writes=[pVb])
                    op(ACT, lambda: a_.copy(out=zb[:, q, :], in_=pB[:, 0:256]), reads=[pBb], writes=[zb_b[q]])
                    op(ACT, lambda: a_.activation(out=v32_, in_=pV[:, 0:384], func=AF.Gelu_apprx_tanh), reads=[pVb], writes=[bv])
                    op(DVE, lambda: v_.bn_stats(out=stt_[:, 0:6], in_=v32_), reads=[bv], writes=[bs])

                    def SV(fn):
                        op(DVE, fn, reads=[bs], writes=[bs])
                    SV(lambda: v_.tensor_scalar(out=stt_[:, 6:7], in0=stt_[:, 1:2], scalar1=stt_[:, 4:5], scalar2=0.5,
                                                op0=ALU.add, op1=ALU.mult))
                    SV(lambda: v_.tensor_tensor(out=stt_[:, 0:1], in0=stt_[:, 1:2], in1=stt_[:, 4:5], op=ALU.subtract))
                    SV(lambda: v_.tensor_scalar(out=stt_[:, 0:1], in0=stt_[:, 0:1], scalar1=stt_[:, 0:1], scalar2=0.25,
                                                op0=ALU.mult, op1=ALU.mult))
                    SV(lambda: v_.tensor_tensor(out=stt_[:, 3:4], in0=stt_[:, 2:3], in1=stt_[:, 5:6], op=ALU.add))
                    SV(lambda: v_.scalar_tensor_tensor(out=stt_[:, 7:8], in0=stt_[:, 3:4], scalar=1.0 / 384.0, in1=stt_[:, 0:1],
                                                       op0=ALU.mult, op1=ALU.add))
                    op(ACT, lambda: a_.activation(out=stt_[:, 7:8], in_=stt_[:, 7:8], func=AF.Sqrt, bias=KEPS, scale=1.0),
                       reads=[bs, b_kc], writes=[bs])
                    op(DVE, lambda: v_.reciprocal(out=stt_[:, 7:8], in_=stt_[:, 7:8]), reads=[bs], writes=[bs])
                    op(DVE, lambda: v_.tensor_scalar(out=vhat_, in0=v32_, scalar1=stt_[:, 6:7], scalar2=stt_[:, 7:8],
                                                     op0=ALU.subtract, op1=ALU.mult), reads=[bv, bs], writes=[bh])

                def stageB(qq):
                    q = 8 * hf + qq
                    vhat_, bh = vhats[q % 2], b_vhats[q % 2]
                    for hp in range(3):
                        tmpc_, btc = tmpcs[hp % 2], b_tmpcs[hp % 2]
                        pst, psb = PSN()
                        grp(PE, [(lambda hh=hh: pe.matmul(pst[64 * hh:64 * hh + 64, 0:128],
                                                          lhsT=vhat_[:, 64 * (2 * hp + hh):64 * (2 * hp + hh) + 64],
                                                          rhs=WsT[:, 2 * hp + hh, :], start=True, stop=True,
                                                          tile_position=(0, 64 * hh))) for hh in range(2)],
                            reads=[bh, b_lconst], writes=[psb])
                        op(DVE, lambda hp=hp: v_.scalar_tensor_tensor(out=tmpc_, in0=pst[:, 0:128], scalar=lng[:, hp:hp + 1],
                                                                      in1=cterm[:, hp, :], op0=ALU.mult, op1=ALU.add),
                           reads=[psb, b_cterm, b_lconst], writes=[btc])
                        op(DVE, lambda hp=hp: v_.tensor_tensor(
                            out=yT[:, 5 + hp, q * 128:(q + 1) * 128], in0=tmpc_,
                            in1=uT[:, hp, qq * 128:(qq + 1) * 128], op=ALU.mult),
                           reads=[btc, uT_b[qq // 4]], writes=[yc_b[q // 4]])

                stageA(0)
                for qq in range(1, 8):
                    stageA(qq)
                    stageB(qq - 1)
                stageB(7)
            barrier()
            dbg_dump(f"yT_p1_{l}", yT, [za_b[k][r] for k in range(3) for r in range(8)] + yc_b)
            dbg_dump(f"zb_{l}", zb, zb_b)

            stop_at(20 * l + 3)
            W1 = carve(RA, [128, 3, 8, 2, 128], BF16)
            Cab = carve(RA + 12288, [128, 12, 9, 2, 32], BF16)
            Bbb = carve(RA + 26112, [128, 12, 2, 32], BF16)
            Ktap = carve(RA + 27648, [128, 3, 8, 128], BF16)
            TB = RA + 33792

            def tab(i):
                return carve(TB + i * 768, [128, 12, 16], F32)
            T1c, T1s, P8r, P8i, T2c, T2s, P128r, P128i, RH1, RH2 = [tab(i) for i in range(10)]
            SM = TB + 7680
            Ere = carve(SM, [128, 12, 16], F32)
            Eim = carve(SM + 768, [128, 12, 16], F32)
            Xre = carve(SM + 1536, [128, 12, 16], F32)
            Xim = carve(SM + 2304, [128, 12, 16], F32)
            Hre = carve(SM + 3072, [128, 12, 16], F32)
            Him = carve(SM + 3840, [128, 12, 16], F32)
            sm2 = carve(SM + 4608, [128, 16, 12], F32)
            A2048r, A2048i, hinr, hini = sm2[:, 0, :], sm2[:, 1, :], sm2[:, 2, :], sm2[:, 3, :]
            pwr = carve(0, [128, 9, 12], F32)
            pwi = carve(432, [128, 9, 12], F32)
            s12 = carve(1024, [128, 24, 12], F32)
            Bbr = carve(4096, [128, 12, 16], F32)
            Bbi = carve(4864, [128, 12, 16], F32)
            tA = carve(5632, [128, 12, 16], F32)
            tB = carve(6400, [128, 12, 16], F32)
            QQ = carve(8192, [128, 8, 2, 4, 32], F32)
            ts16 = carve(16 * KB, [128, 8, 12, 16], F32)
            b_su = Buf("setup")

            def V(fn, r=(), w=()):
                op(DVE, fn, reads=[b_su, b_lconst] + list(r), writes=[b_su] + list(w))

            def A(fn):
                op(ACT, fn, reads=[b_su, b_lconst, b_kc], writes=[b_su])

            def tt_(out, a, b, o):
                V(lambda: v_.tensor_tensor(out=out, in0=a, in1=b, op=o))

            def cmul(outr, outi, ar_, ai_, br_, bi_, t1, t2):
                tt_(t1, ar_, br_, ALU.mult)
                tt_(t2, ai_, bi_, ALU.mult)
                tt_(outr, t1, t2, ALU.subtract)
                tt_(t1, ar_, bi_, ALU.mult)
                tt_(t2, ai_, br_, ALU.mult)
                tt_(outi, t1, t2, ALU.add)

            Are, Aim, Ldt = ssm_s[:, 0, :], ssm_s[:, 1, :], ssm_s[:, 2, :]
            S = [s12[:, i, :] for i in range(24)]
            dtv, x1, th, mag, cs, sn, ar, ai, den, fre, fim, u1, u2 = S[0:13]
            A(lambda: a_.activation(out=dtv, in_=Ldt, func=AF.Exp))
            tt_(x1, Are, dtv, ALU.mult)
            tt_(th, Aim, dtv, ALU.mult)
            A(lambda: a_.activation(out=mag, in_=x1, func=AF.Exp))
            A(lambda: a_.activation(out=sn, in_=th, func=AF.Sin, scale=1.0 / 8.0))
            A(lambda: a_.activation(out=u1, in_=th, func=AF.Sin, scale=1.0 / 16.0))
            tt_(u2, u1, u1, ALU.mult)
            V(lambda: v_.tensor_scalar(out=cs, in0=u2, scalar1=-2.0, scalar2=1.0, op0=ALU.mult, op1=ALU.add))
            for _ in range(3):
                tt_(u1, cs, cs, ALU.mult)
                tt_(u2, sn, sn, ALU.mult)
                tt_(S[13], cs, sn, ALU.mult)
                tt_(cs, u1, u2, ALU.subtract)
                V(lambda: v_.tensor_scalar(out=sn, in0=S[13], scalar1=2.0, scalar2=None, op0=ALU.mult))
            tt_(ar, mag, cs, ALU.mult)
            tt_(ai, mag, sn, ALU.mult)
            tt_(u1, Are, Are, ALU.mult)
            tt_(u2, Aim, Aim, ALU.mult)
            tt_(den, u1, u2, ALU.add)
            V(lambda: v_.reciprocal(out=den, in_=den))
            arm1 = S[13]
            V(lambda: v_.tensor_scalar(out=arm1, in0=ar, scalar1=-1.0, scalar2=None, op0=ALU.add))
            tt_(u1, arm1, Are, ALU.mult)
            tt_(u2, ai, Aim, ALU.mult)
            tt_(u1, u1, u2, ALU.add)
            tt_(fre, u1, den, ALU.mult)
            tt_(u1, ai, Are, ALU.mult)
            tt_(u2, arm1, Aim, ALU.mult)
            tt_(u1, u1, u2, ALU.subtract)
            tt_(fim, u1, den, ALU.mult)
            Bre, Bim, Cre, Cim = ssm_bc[:, 0], ssm_bc[:, 1], ssm_bc[:, 2], ssm_bc[:, 3]

            def bc16(v):
                return mkap(v, 0, [[v.ap[1][0], 12], [0, 16]])
            cmul(Bbr, Bbi, bc16(fre), bc16(fim), Bre, Bim, tA, tB)
            V(lambda: v_.memset(pwr[:, 0, :], 1.0))
            V(lambda: v_.memset(pwi[:, 0, :], 0.0))
            for k in range(1, 9):
                cmul(pwr[:, k, :], pwi[:, k, :], pwr[:, k - 1, :], pwi[:, k - 1, :], ar, ai, S[14], S[15])
            A8r, A8i = pwr[:, 8, :], pwi[:, 8, :]
            rho8, irho8, u8r, u8i, rho128, irho, A128r, A128i, u128r, u128i = S[14:24]
            tt_(u1, mag, mag, ALU.mult)
            tt_(u2, u1, u1, ALU.mult)
            tt_(rho8, u2, u2, ALU.mult)
            V(lambda: v_.reciprocal(out=irho8, in_=rho8))
            tt_(u8r, A8r, irho8, ALU.mult)
            tt_(u8i, A8i, irho8, ALU.mult)

            def col(tb, j):
                return tb[:, :, j]

            def build_table(Tr, Ti, br_, bi_, first_one):
                if first_one:
                    V(lambda: v_.memset(col(Tr, 0), 1.0))
                    V(lambda: v_.memset(col(Ti, 0), 0.0))
                    V(lambda: v_.tensor_copy(out=col(Tr, 1), in_=br_))
                    V(lambda: v_.tensor_copy(out=col(Ti, 1), in_=bi_))
                    start = 2
                else:
                    V(lambda: v_.tensor_copy(out=col(Tr, 0), in_=br_))
                    V(lambda: v_.tensor_copy(out=col(Ti, 0), in_=bi_))
                    start = 1
                for j in range(start, 16):
                    cmul(col(Tr, j), col(Ti, j), col(Tr, j - 1), col(Ti, j - 1), br_, bi_, u1, u2)
            build_table(T1c, T1s, u8r, u8i, True)
            build_table(P8r, P8i, A8r, A8i, False)
            V(lambda: v_.tensor_copy(out=A128r, in_=col(P8r, 15)))
            V(lambda: v_.tensor_copy(out=A128i, in_=col(P8i, 15)))
            tt_(u1, rho8, rho8, ALU.mult)
            tt_(u2, u1, u1, ALU.mult)
            tt_(u1, u2, u2, ALU.mult)
            tt_(rho128, u1, u1, ALU.mult)
            V(lambda: v_.reciprocal(out=irho, in_=rho128))
            tt_(u128r, A128r, irho, ALU.mult)
            tt_(u128i, A128i, irho, ALU.mult)
            build_table(T2c, T2s, u128r, u128i, True)
            build_table(P128r, P128i, A128r, A128i, True)
            cmul(A2048r, A2048i, col(P128r, 15), col(P128i, 15), A128r, A128i, u1, u2)
            V(lambda: v_.tensor_copy(out=RH1, in_=bc16(rho8)))
            V(lambda: v_.memset(col(RH1, 0), 0.0))
            V(lambda: v_.tensor_copy(out=RH2, in_=bc16(rho128)))
            V(lambda: v_.memset(col(RH2, 0), 0.0))

            V(lambda: v_.memset(Cab, 0.0))
            V(lambda: v_.memset(Bbb, 0.0))
            for k in range(9):
                pr, pi_ = mkap(pwr, k * 12, [[1, 12], [0, 16]]), mkap(pwi, k * 12, [[1, 12], [0, 16]])
                tt_(tA, Cre, pr, ALU.mult)
                tt_(tB, Cim, pi_, ALU.mult)
                for g2 in range(2):
                    V(lambda g2=g2, k=k: v_.tensor_tensor(
                        out=Cab[64 * g2:64 * g2 + 64, :, k, 0, 16 * g2:16 * g2 + 16],
                        in0=tA[64 * g2:64 * g2 + 64], in1=tB[64 * g2:64 * g2 + 64], op=ALU.subtract))
                tt_(tA, Cre, pi_, ALU.mult)
                tt_(tB, Cim, pr, ALU.mult)
                for g2 in range(2):
                    V(lambda g2=g2, k=k: v_.scalar_tensor_tensor(
                        out=Cab[64 * g2:64 * g2 + 64, :, k, 1, 16 * g2:16 * g2 + 16],
                        in0=tA[64 * g2:64 * g2 + 64], scalar=-1.0, in1=tB[64 * g2:64 * g2 + 64],
                        op0=ALU.mult, op1=ALU.subtract))
            for g2 in range(2):
                V(lambda g2=g2: v_.tensor_copy(out=Bbb[64 * g2:64 * g2 + 64, :, 0, 16 * g2:16 * g2 + 16], in_=Bbr[64 * g2:64 * g2 + 64]))
                V(lambda g2=g2: v_.tensor_copy(out=Bbb[64 * g2:64 * g2 + 64, :, 1, 16 * g2:16 * g2 + 16], in_=Bbi[64 * g2:64 * g2 + 64]))
            V(lambda: v_.memset(Ktap, 0.0))
            for k in range(3):
                pst, psb = PSN()
                pst2, psb2 = PSN()
                fns = []
                for qd in range(4):
                    p = 4 * k + qd
                    for half, pp in ((0, pst), (1, pst2)):
                        for ri in range(2):
                            fns.append(lambda p=p, qd=qd, half=half, pp=pp, ri=ri: pe.matmul(
                                mkap(pp, 32 * qd, [[128, 4], [1, 32]], p0=32 * qd, np_=32),
                                lhsT=Bbb[:, p, ri, :],
                                rhs=mkap(Cab, p * 576 + half * 4 * 64 + ri * 32, [[64, 4], [1, 32]]),
                                start=(ri == 0), stop=(ri == 1), tile_position=(0, 32 * qd)))
                grp(PE, fns, reads=[b_su], writes=[psb, psb2])
                for qd in range(4):
                    for half, pp, pb in ((0, pst, psb), (1, pst2, psb2)):
                        op(DVE, lambda qd=qd, half=half, pp=pp, k=k: v_.tensor_copy(
                            out=Ktap[32 * qd:32 * qd + 32, k, 4 * half:4 * half + 4, 32 * qd:32 * qd + 32],
                            in_=mkap(pp, 32 * qd, [[128, 4], [1, 32]], p0=32 * qd, np_=32)),
                           reads=[pb, b_su], writes=[b_su])
                V(lambda k=k: v_.scalar_tensor_tensor(out=Ktap[:, k, 0, :], in0=ident, scalar=Dsk[:, k:k + 1],
                                                      in1=Ktap[:, k, 0, :], op0=ALU.mult, op1=ALU.add), r=[b_gconst])
            for k in range(3):
                V(lambda: v_.memset(QQ, 0.0))
                for rp in range(8):
                    pw_ = 7 - rp
                    pr, pi_ = mkap(pwr, pw_ * 12 + 4 * k, [[1, 4], [0, 16]]), mkap(pwi, pw_ * 12 + 4 * k, [[1, 4], [0, 16]])
                    br_, bi_ = Bbr[:, 4 * k:4 * k + 4, :], Bbi[:, 4 * k:4 * k + 4, :]
                    tA4, tB4 = tA[:, 0:4, :], tB[:, 0:4, :]
                    tt_(tA4, pr, br_, ALU.mult)
                    tt_(tB4, pi_, bi_, ALU.mult)
                    for g2 in range(2):
                        V(lambda g2=g2, rp=rp: v_.tensor_tensor(
                            out=QQ[64 * g2:64 * g2 + 64, rp, 0, :, 16 * g2:16 * g2 + 16],
                            in0=tA4[64 * g2:64 * g2 + 64], in1=tB4[64 * g2:64 * g2 + 64], op=ALU.subtract))
                    tt_(tA4, pr, bi_, ALU.mult)
                    tt_(tB4, pi_, br_, ALU.mult)
                    for g2 in range(2):
                        V(lambda g2=g2, rp=rp: v_.tensor_tensor(
                            out=QQ[64 * g2:64 * g2 + 64, rp, 1, :, 16 * g2:16 * g2 + 16],
                            in0=tA4[64 * g2:64 * g2 + 64], in1=tB4[64 * g2:64 * g2 + 64], op=ALU.add))
                for rp in range(8):
                    for ri in range(2):
                        pst, psb = PSN()
                        op(PE, lambda rp=rp, ri=ri: pe.transpose(pst[:, 0:128], QQ[:, rp, ri].rearrange("p a b -> p (a b)"), ident),
                           reads=[b_su, b_gconst], writes=[psb])
                        op(ACT, lambda rp=rp, ri=ri, k=k: a_.copy(out=W1[:, k, rp, ri, :], in_=pst[:, 0:128]),
                           reads=[psb, b_su], writes=[b_su])
                if k == 2:
                    dbg_dump(f"tabs_{l}", carve(TB, [128, 1920], F32), [b_su])
                    dbg_dump(f"pw_{l}", carve(0, [128, 216], F32), [b_su])
                    dbg_dump(f"s12_{l}", carve(1024, [128, 288], F32), [b_su])
                    dbg_dump(f"Bb_{l}", carve(4096, [128, 384], F32), [b_su])
                    dbg_dump(f"Ktap_{l}", Ktap.rearrange("p a b c -> p (a b c)"), [b_su])
                    dbg_dump(f"W1_{l}", W1.rearrange("p a b c d -> p (a b c d)"), [b_su])
                    dbg_dump(f"Cab_{l}", Cab.rearrange("p a b c d -> p (a b c d)"), [b_su])
                V(lambda: v_.memset(kc[:, 3:4], 0.0))
            barrier()

            stop_at(20 * l + 4)
            Sb = carve(0, [128, 2, 12, 260], BF16)
            b_Sb = Buf("Sb")
            Gs = carve(16 * KB, [128, 8, GW], BF16)
            b_Gs = Buf("Gs")
            wo = [carve(RA + 46 * KB + i * 2048, [128, 8, 128], BF16) for i in range(2)]
            wr = carve(TMP, [128, 256], F32)
            wi = carve(TMP + 1024, [128, 256], F32)
            t1 = carve(TMP + 2048, [128, 256], F32)
            t2 = carve(TMP + 3072, [128, 256], F32)
            b_scan = Buf("scan")
            rhfull = carve(TMP + 4096, [128, 256], F32)

            def b16x16(tb, p):
                return mkap(tb, p * 16, [[0, 16], [1, 16]])

            def v3(x):
                return x.rearrange("p (a b) -> p a b", a=16)

            pooleds = [carve(24 * KB + i * 256, [128, 128], BF16) for i in range(2)]
            b_pooleds = [Buf("pooled0"), Buf("pooled1")]
            pool_i = [0]

            def pool_chunk(q, prev_ap, prev_bufs, mi, hi):
                hf, qq = q // 8, q % 8
                for gp in range(2):
                    pooled, b_pooled = pooleds[pool_i[0] % 2], b_pooleds[pool_i[0] % 2]
                    pool_i[0] += 1
                    pst, psb = PSN()
                    fns = []
                    for gg in range(2):
                        g = 2 * gp + gg
                        fns.append(lambda g=g, gg=gg: pe.matmul(pst[64 * gg:64 * gg + 64, 0:128], lhsT=zb[:, q, 64 * g:64 * g + 64],
                                                               rhs=pmat[:, 4 * mi + g, :], start=True, stop=False, tile_position=(0, 64 * gg)))
                        fns.append(lambda g=g, gg=gg: pe.matmul(pst[64 * gg:64 * gg + 64, 0:128], lhsT=prev_ap[:, 64 * g:64 * g + 64],
                                                               rhs=pmat[:, 4 * hi + g, :], start=False, stop=True, tile_position=(0, 64 * gg)))
                    grp(PE, fns, reads=[zb_b[q], b_gconst] + prev_bufs, writes=[psb])
                    op(ACT, lambda: a_.copy(out=pooled, in_=pst[:, 0:128]), reads=[psb], writes=[b_pooled])
                    pst2, psb2 = PSN()
                    grp(PE, [lambda gp=gp: pe.matmul(pst2[:, 0:128], lhsT=WP[:, gp, :], rhs=pooled, start=True, stop=True)],
                        reads=[b_pooled, b_lconst], writes=[psb2])
                    op(ACT, lambda gp=gp: a_.activation(out=yT[:, 3 + gp, q * 128:(q + 1) * 128], in_=pst2[:, 0:128],
                                                        func=AF.Copy, scale=pscale[:, gp:gp + 1]),
                       reads=[psb2, b_lconst], writes=[yb_b[q // 4]])

            for p in range(12):
                k, qd = p // 4, p % 4
                pst, psb = PSN()
                fns = []
                for ri in range(2):
                    for rp in range(8):
                        fns.append(lambda ri=ri, rp=rp: pe.matmul(
                            pst[:, ri * 256:(ri + 1) * 256], lhsT=W1[32 * qd:32 * qd + 32, k, rp, ri, :],
                            rhs=mkap(yT, k * T + rp * 16, [[128, 16], [1, 16]], p0=32 * qd, np_=32),
                            start=(rp == 0), stop=(rp == 7), tile_position=(32 * qd, 0)))
                grp(PE, fns, reads=[b_su] + za_b[k], writes=[psb])
                Lr, Li = v3(pst[:, 0:256]), v3(pst[:, 256:512])
                c_, s_ = b16x16(T1c, p), b16x16(T1s, p)

                def D(fn, extra=()):
                    op(DVE, fn, reads=[psb, b_su, b_scan] + list(extra), writes=[b_scan])
                D(lambda: v_.tensor_tensor(out=v3(t1), in0=Lr, in1=c_, op=ALU.mult))
                D(lambda: v_.tensor_tensor(out=v3(t2), in0=Li, in1=s_, op=ALU.mult))
                D(lambda: v_.tensor_tensor(out=wr, in0=t1, in1=t2, op=ALU.add))
                D(lambda: v_.tensor_tensor(out=v3(t1), in0=Li, in1=c_, op=ALU.mult))
                D(lambda: v_.tensor_tensor(out=v3(t2), in0=Lr, in1=s_, op=ALU.mult))
                D(lambda: v_.tensor_tensor(out=wi, in0=t1, in1=t2, op=ALU.subtract))
                D(lambda: v_.tensor_copy(out=v3(rhfull), in_=mkap(RH1, p * 16, [[0, 16], [1, 16]])))
                D(lambda: v_.tensor_tensor_scan(out=wr, data0=rhfull, data1=wr, initial=0.0, op0=ALU.mult, op1=ALU.add))
                D(lambda: v_.tensor_tensor_scan(out=wi, data0=rhfull, data1=wi, initial=0.0, op0=ALU.mult, op1=ALU.add))
                D(lambda: v_.tensor_tensor(out=v3(t1), in0=v3(wr), in1=c_, op=ALU.mult))
                D(lambda: v_.tensor_tensor(out=v3(t2), in0=v3(wi), in1=s_, op=ALU.mult))
                op(DVE, lambda p=p: v_.tensor_tensor(out=Sb[:, 0, p, 1:257], in0=t1, in1=t2, op=ALU.subtract),
                   reads=[b_scan], writes=[b_Sb, b_scan])
                op(DVE, lambda p=p: v_.tensor_tensor(out=Ere[:, p, :], in0=mkap(t1, 15, [[16, 16]]), in1=mkap(t2, 15, [[16, 16]]),
                                                     op=ALU.subtract), reads=[b_scan, b_su], writes=[b_su, b_scan])
                D(lambda: v_.tensor_tensor(out=v3(t1), in0=v3(wr), in1=s_, op=ALU.mult))
                D(lambda: v_.tensor_tensor(out=v3(t2), in0=v3(wi), in1=c_, op=ALU.mult))
                op(DVE, lambda p=p: v_.tensor_tensor(out=Sb[:, 1, p, 1:257], in0=t1, in1=t2, op=ALU.add),
                   reads=[b_scan], writes=[b_Sb, b_scan])
                op(DVE, lambda p=p: v_.tensor_tensor(out=Eim[:, p, :], in0=mkap(t1, 15, [[16, 16]]), in1=mkap(t2, 15, [[16, 16]]),
                                                     op=ALU.add), reads=[b_scan, b_su], writes=[b_su, b_scan])
                for q in range(1 + (15 * p) // 12, 1 + (15 * (p + 1)) // 12):
                    pool_chunk(q, zb[:, q - 1, :], [zb_b[q - 1]], 0, 1)
            stop_at(20 * l + 5)
            f192 = lambda x: x.rearrange("p a b -> p (a b)")
            xa, xb_, xc, xd = [ts16[:, i] for i in range(4)]
            tt_(xa, Ere, T2c, ALU.mult)
            tt_(xb_, Eim, T2s, ALU.mult)
            tt_(xc, xa, xb_, ALU.add)
            tt_(xa, Eim, T2c, ALU.mult)
            tt_(xb_, Ere, T2s, ALU.mult)
            tt_(xd, xa, xb_, ALU.subtract)
            V(lambda: v_.tensor_tensor_scan(out=f192(xc), data0=f192(RH2), data1=f192(xc), initial=0.0, op0=ALU.mult, op1=ALU.add))
            V(lambda: v_.tensor_tensor_scan(out=f192(xd), data0=f192(RH2), data1=f192(xd), initial=0.0, op0=ALU.mult, op1=ALU.add))
            tt_(xa, xc, T2c, ALU.mult)
            tt_(xb_, xd, T2s, ALU.mult)
            tt_(Xre, xa, xb_, ALU.subtract)
            tt_(xa, xc, T2s, ALU.mult)
            tt_(xb_, xd, T2c, ALU.mult)
            tt_(Xim, xa, xb_, ALU.add)
            stop_at(20 * l + 6)
            agst = carve(20 * KB + 768 * 2, [128, 24], F32)
            b_ag = Buf("agst")
            op(DVE, lambda: v_.tensor_copy(out=agst[:, 0:12], in_=col(Xre, 15)), reads=[b_su], writes=[b_ag])
            op(DVE, lambda: v_.tensor_copy(out=agst[:, 12:24], in_=col(Xim, 15)), reads=[b_su], writes=[b_ag])
            b_agd = Buf("agdram")
            dma(POOL, ds_ag, d_agin[l].ap()[:, 0:48], agst.bitcast(BF16), reads=[b_ag], writes=[b_agd])
            dma(POOL, ds_ag, d_agin[l].ap()[:, 48:GW], zb[:, 15, :], reads=[zb_b[15]], writes=[b_agd])
            _sync(POOL, [b_agd], [])
            cc = nc.gpsimd.collective_compute("AllGather", ALU.bypass, replica_groups=[list(range(NCORES))],
                                              ins=[d_agin[l].ap().opt()], outs=[d_agout[l].ap().opt()])
            cc_cnt[0] += 1
            cc.then_inc(cc_sem, 1)
            nc.gpsimd.wait_ge(cc_sem, cc_cnt[0])
            dma(POOL, ds_ag, Gs, d_agout[l].ap().rearrange("(c p) f -> p c f", p=128), writes=[b_Gs])

            stop_at(20 * l + 7)

            stop_at(20 * l + 8)
            Gs32 = Gs[:, :, 0:48].bitcast(F32)
            g24 = carve(TMP + 4096, [128, 24, 8], F32)
            sel = carve(TMP + 4096 + 768, [128, 3, 24], F32)
            for d in range(3):
                V(lambda d=d: v_.tensor_tensor(out=g24, in0=mkap(Gs32, 0, [[1, 24], [GW // 2, 8]]),
                                               in1=mkap(oh, d * 8, [[0, 24], [1, 8]]), op=ALU.mult), r=[b_Gs, b_gconst])
                V(lambda d=d: v_.tensor_reduce(out=sel[:, d, :], in_=g24, axis=AX.X, op=ALU.add))
            V(lambda: v_.tensor_copy(out=hinr, in_=sel[:, 2, 0:12]))
            V(lambda: v_.tensor_copy(out=hini, in_=sel[:, 2, 12:24]))
            hr2, hi2 = sm2[:, 4, :], sm2[:, 5, :]
            for d in (1, 0):
                cmul(hr2, hi2, A2048r, A2048i, hinr, hini, sm2[:, 6, :], sm2[:, 7, :])
                tt_(hinr, hr2, sel[:, d, 0:12], ALU.add)
                tt_(hini, hi2, sel[:, d, 12:24], ALU.add)
            xs_a = carve(22528, [128, 12, 16], F32)
            xs_b = carve(23296, [128, 12, 16], F32)
            cmul(Hre, Him, P128r, P128i, bc16(hinr), bc16(hini), xs_a, xs_b)
            tt_(Hre[:, :, 1:16], Hre[:, :, 1:16], Xre[:, :, 0:15], ALU.add)
            tt_(Him[:, :, 1:16], Him[:, :, 1:16], Xim[:, :, 0:15], ALU.add)
            for p in range(12):
                pr_ = mkap(P8r, p * 16, [[0, 16], [1, 16]])
                pi_ = mkap(P8i, p * 16, [[0, 16], [1, 16]])
                hr_ = mkap(Hre, p * 16, [[1, 16], [0, 16]])
                hi_ = mkap(Him, p * 16, [[1, 16], [0, 16]])

                def D2(fn):
                    op(DVE, fn, reads=[b_su, b_scan, b_Sb], writes=[b_scan, b_Sb])
                D2(lambda: v_.tensor_tensor(out=v3(t1), in0=pr_, in1=hr_, op=ALU.mult))
                D2(lambda: v_.tensor_tensor(out=v3(t2), in0=pi_, in1=hi_, op=ALU.mult))
                D2(lambda: v_.tensor_tensor(out=wr, in0=t1, in1=t2, op=ALU.subtract))
                D2(lambda p=p: v_.tensor_tensor(out=Sb[:, 0, p, 1:257], in0=Sb[:, 0, p, 1:257], in1=wr, op=ALU.add))
                D2(lambda: v_.tensor_tensor(out=v3(t1), in0=pr_, in1=hi_, op=ALU.mult))
                D2(lambda: v_.tensor_tensor(out=v3(t2), in0=pi_, in1=hr_, op=ALU.mult))
                D2(lambda: v_.tensor_tensor(out=wi, in0=t1, in1=t2, op=ALU.add))
                D2(lambda p=p: v_.tensor_tensor(out=Sb[:, 1, p, 1:257], in0=Sb[:, 1, p, 1:257], in1=wi, op=ALU.add))
            op(DVE, lambda: v_.tensor_copy(out=Sb[:, 0, :, 0], in_=hinr), reads=[b_su], writes=[b_Sb])
            op(DVE, lambda: v_.tensor_copy(out=Sb[:, 1, :, 0], in_=hini), reads=[b_su], writes=[b_Sb])

            stop_at(20 * l + 9)
            dbg_dump(f"Sb_{l}", Sb.rearrange("p a b c -> p (a b c)"), [b_Sb])
            dbg_dump(f"sm_{l}", carve(SM, [128, 1344], F32), [b_su])
            for k in range(3):
                for r in range(7, -1, -1):
                    pst, psb = PSN()
                    fns = []
                    for tau in range(r + 1):
                        fns.append(lambda tau=tau: pe.matmul(
                            pst[:, 0:256], lhsT=Ktap[:, k, tau, :],
                            rhs=mkap(yT, k * T + (r - tau) * 16, [[128, 16], [1, 16]]),
                            start=(tau == 0), stop=False))
                    for qd in range(4):
                        p = 4 * k + qd
                        for ri in range(2):
                            fns.append(lambda qd=qd, p=p, ri=ri: pe.matmul(
                                pst[32 * qd:32 * qd + 32, 0:256], lhsT=Cab[:, p, r + 1, ri, :],
                                rhs=Sb[:, ri, p, 0:256], start=False, stop=(ri == 1), tile_position=(0, 32 * qd)))
                    grp(PE, fns, reads=[b_su, b_Sb] + za_b[k][0:r + 1], writes=[psb])
                    op(ACT, lambda: a_.activation(
                        out=mkap(yT, k * T + r * 16, [[128, 16], [1, 16]]),
                        in_=pst[:, 0:256].rearrange("p (a b) -> p a b", a=16), func=AF.Gelu_apprx_tanh),
                       reads=[psb], writes=[za_b[k][r]])
            dbg_dump(f"gT_{l}", yT[:, 0:3, :], [za_b[k][r] for k in range(3) for r in range(8)])
            stop_at(20 * l + 10)
            sig = carve(24 * KB + 512, [128, 512], BF16)
            b_sig = Buf("sig")
            for tt in range(4):
                rbs = [za_b[k][i] for k in range(3) for i in range(8)]
                pss = []
                for m3 in range(3):
                    pst, psb = PSN()
                    grp(PE, [(lambda kk=kk: pe.matmul(pst, lhsT=wglu[:, kk, m3 * 128:(m3 + 1) * 128],
                                                      rhs=yT[:, kk, tt * 512:(tt + 1) * 512], start=(kk == 0), stop=(kk == 2)))
                             for kk in range(3)], reads=rbs + [b_lconst], writes=[psb])
                    pss.append((pst, psb))
                for m3 in range(3):
                    pst, psb = pss[m3]
                    op(ACT, lambda: a_.activation(out=sig, in_=pst, func=AF.Sigmoid, bias=bglu[:, m3:m3 + 1], scale=1.0),
                       reads=[psb, b_lconst], writes=[b_sig])
                    op(DVE, lambda: v_.tensor_tensor(out=yT[:, m3, tt * 512:(tt + 1) * 512], in0=yT[:, m3, tt * 512:(tt + 1) * 512],
                                                     in1=sig, op=ALU.mult),
                       reads=[b_sig] + rbs, writes=za_b[m3])
            stop_at(20 * l + 11)
            zprev = carve(TMP + 4096 + 1536, [128, 256], BF16)
            zt8 = carve(0 + 13 * KB, [128, 64, 8], F32)
            b_zprev = Buf("zprev")
            for c4 in range(4):
                op(DVE, lambda c4=c4: v_.tensor_tensor(out=zt8, in0=mkap(Gs, 48 + 64 * c4, [[1, 64], [GW, 8]]),
                                                       in1=mkap(oh, 0, [[0, 64], [1, 8]]), op=ALU.mult),
                   reads=[b_Gs, b_gconst, b_Sb], writes=[b_zprev])
                with nc.allow_low_precision("one-hot select: exactly one non-zero term"):
                    op(DVE, lambda c4=c4: v_.tensor_reduce(out=zprev[:, 64 * c4:64 * c4 + 64], in_=zt8, axis=AX.X, op=ALU.add),
                       reads=[b_zprev], writes=[b_zprev])
            pool_chunk(0, zprev, [b_zprev], 2, 3)
            dbg_dump(f"yT_p2_{l}", yT, [za_b[k][r] for k in range(3) for r in range(8)] + yc_b + yb_b)

            stop_at(20 * l + 12)
            for m in range(8):
                s = m % 2
                dma(POOL, ds_wo[s], wo[s], d_wout[l, :, m * 128:(m + 1) * 128].rearrange("(k p) c -> p k c", p=128), writes=[b_wo[s]])
                for tt in range(4):
                    pst, psb = PSN()
                    rbs = [za_b[k][i] for k in range(3) for i in range(8)] + [yb_b[tt], yc_b[tt]]
                    grp(PE, [(lambda k=k: pe.matmul(pst, lhsT=wo[s][:, k, :], rhs=yT[:, k, tt * 512:(tt + 1) * 512],
                                                    start=(k == 0), stop=(k == 7))) for k in range(8)],
                        reads=rbs + [b_wo[s]], writes=[psb])
                    op(DVE, lambda: v_.tensor_tensor(out=xT[:, m, tt * 512:(tt + 1) * 512], in0=pst, in1=xT[:, m, tt * 512:(tt + 1) * 512],
                                                     op=ALU.add), reads=[psb], writes=[xT_b[tt]])
            barrier()
            dbg_dump(f"x_mix_{l}", xT, xT_b)

            stop_at(20 * l + 13)
            aT = carve(32 * KB, [128, NF, 1024], BF16)
            b_aT = [Buf("aT0"), Buf("aT1")]
            wgu = [carve(76 * KB + i * 4096, [128, 2, 8, 128], BF16) for i in range(3)]
            wd = [carve(88 * KB + i * 5632, [128, NF, 128], BF16) for i in range(3)]
            sg = carve(TMP, [128, 512], BF16)
            b_sg = Buf("sg")
            ci = [0, 0]
            for t2_ in range(2):
                for lt in range(2):
                    norm_to_hT(2 * l + 1, 2 * t2_ + lt, lt)
                for f in range(NF):
                    s = ci[0] % 3
                    ci[0] += 1
                    dma(POOL, ds_wgu[s], wgu[s][:, 0], d_wg[l, :, f * 128:(f + 1) * 128].rearrange("(k p) c -> p k c", p=128), writes=[b_wgu[s]])
                    dma(POOL, ds_wgu[s], wgu[s][:, 1], d_wu[l, :, f * 128:(f + 1) * 128].rearrange("(k p) c -> p k c", p=128), writes=[b_wgu[s]])
                    for lt in range(2):
                        pg, pgb = PSN()
                        pu, pub = PSN()
                        grp(PE, [(lambda k=k: pe.matmul(pg, lhsT=wgu[s][:, 0, k, :], rhs=hT[:, k, lt * 512:(lt + 1) * 512],
                                                        start=(k == 0), stop=(k == 7))) for k in range(8)],
                            reads=[b_wgu[s], hT_b[lt]], writes=[pgb])
                        grp(PE, [(lambda k=k: pe.matmul(pu, lhsT=wgu[s][:, 1, k, :], rhs=hT[:, k, lt * 512:(lt + 1) * 512],
                                                        start=(k == 0), stop=(k == 7))) for k in range(8)],
                            reads=[b_wgu[s], hT_b[lt]], writes=[pub])
                        op(ACT, lambda: a_.activation(out=sg, in_=pg, func=AF.Silu), reads=[pgb], writes=[b_sg])
                        op(DVE, lambda: v_.tensor_tensor(out=aT[:, f, lt * 512:(lt + 1) * 512], in0=sg, in1=pu, op=ALU.mult),
                           reads=[b_sg, pub], writes=[b_aT[lt]])
                for m in range(8):
                    s = ci[1] % 3
                    ci[1] += 1
                    dma(POOL, ds_wd[s], wd[s], d_wd[l, :, m * 128:(m + 1) * 128].rearrange("(f p) c -> p f c", p=128), writes=[b_wd[s]])
                    for lt in range(2):
                        tt = 2 * t2_ + lt
                        pst, psb = PSN()
                        grp(PE, [(lambda f=f: pe.matmul(pst, lhsT=wd[s][:, f, :], rhs=aT[:, f, lt * 512:(lt + 1) * 512],
                                                        start=(f == 0), stop=(f == NF - 1))) for f in range(NF)],
                            reads=[b_wd[s], b_aT[lt]], writes=[psb])
                        op(DVE, lambda: v_.tensor_tensor(out=xT[:, m, tt * 512:(tt + 1) * 512], in0=pst,
                                                         in1=xT[:, m, tt * 512:(tt + 1) * 512], op=ALU.add),
                           reads=[psb], writes=[xT_b[tt]])
            barrier()
            dbg_dump(f"x_ffn_{l}", xT, xT_b)

        ost = carve(0, [128, 8, 512], F32)
        b_ost = Buf("ost")
        for tt in range(4):
            def outs(c0):
                for k in range(8):
                    op(DVE, lambda k=k: v_.scalar_tensor_tensor(out=ost[:, k, :], in0=xT[:, k, c0:c0 + 512], scalar=gcat[:, 4, k:k + 1],
                                                                in1=rs, op0=ALU.mult, op1=ALU.mult),
                       reads=[xT_b[tt], rs_b, b_gconst], writes=[b_ost])
            norm_tile(4, tt, outs)
            dma(SP, ds_out, d_out[:, :, tt * 512:(tt + 1) * 512], ost, reads=[b_ost])
    except _Stop:
        pass
    for dsm in ALL_DSEMS:
        if dsm.count:
            nc.sync.wait_ge(dsm.sem, dsm.count)
    if ds_dbg.count:
        nc.sync.wait_ge(ds_dbg.sem, ds_dbg.count)
    barrier()
    es.close()
    return nc


def _prep_inputs(x, g_mix, w_in, A_re, A_im, log_dt, B_re, B_im, C_re, C_im, D_skip, w_glu, b_glu, w_pool,
                 pool_scale, sgu_ln_g, sgu_ln_b, w_spatial, b_spatial, w_out, g_ffn, w_gate, w_up, w_down, g_final):
    f = np.float32
    tok = _pos_to_tok()
    sperm = _chunk_perm()
    shared = {}
    colperm = np.concatenate([np.arange(0, 384), np.arange(640, 1024), np.arange(384, 640), np.arange(1024, 1408)])
    shared["w_in"] = np.ascontiguousarray(np.asarray(w_in, f)[:, :, colperm])
    shared["w_out"] = np.ascontiguousarray(np.asarray(w_out, f))
    shared["w_gate"] = np.ascontiguousarray(np.asarray(w_gate, f))
    shared["w_up"] = np.ascontiguousarray(np.asarray(w_up, f))
    shared["w_down"] = np.ascontiguousarray(np.asarray(w_down, f))
    shared["w_glu"] = np.ascontiguousarray(np.asarray(w_glu, f))
    gs = [np.asarray(g_mix, f)[0], np.asarray(g_ffn, f)[0], np.asarray(g_mix, f)[1], np.asarray(g_ffn, f)[1], np.asarray(g_final, f)]
    shared["gcat"] = np.ascontiguousarray(np.stack([g.reshape(8, 128).T for g in gs], axis=1))

    def gn(a):
        a = np.asarray(a, f).reshape(2, 12, 2, 64)
        return a.transpose(0, 2, 3, 1).reshape(2, 128, 12)
    ldt = np.broadcast_to(np.asarray(log_dt, f)[:, :, None], (2, 24, 64))
    shared["ssm_s"] = np.ascontiguousarray(np.stack([gn(A_re), gn(A_im), gn(ldt)], axis=2))

    def gB(a):
        a = np.asarray(a, f).reshape(2, 12, 2, 64, 16)
        return a.transpose(0, 2, 3, 1, 4).reshape(2, 128, 12, 16)

    def gC(a):
        a = np.asarray(a, f).reshape(2, 12, 2, 16, 64)
        return a.transpose(0, 2, 4, 1, 3).reshape(2, 128, 12, 16)
    shared["ssm_bc"] = np.ascontiguousarray(np.stack([gB(B_re), gB(B_im), gC(C_re), gC(C_im)], axis=2))
    vec = np.concatenate([np.asarray(D_skip, f).reshape(2, 3, 128).transpose(0, 2, 1),
                          np.asarray(b_glu, f).reshape(2, 3, 128).transpose(0, 2, 1),
                          np.asarray(pool_scale, f).reshape(2, 2, 128).transpose(0, 2, 1),
                          np.asarray(sgu_ln_g, f).reshape(2, 3, 128).transpose(0, 2, 1)], axis=2)
    shared["vecs"] = np.ascontiguousarray(vec)
    shared["lnb"] = np.ascontiguousarray(np.broadcast_to(np.asarray(sgu_ln_b, f)[:, None, :], (2, 128, 384)))
    wp = np.zeros((2, 128, 2, 128), f)
    wpn = np.asarray(w_pool, f)
    for gp in range(2):
        for gg in range(2):
            wp[:, 64 * gg:64 * gg + 64, gp, 64 * gg:64 * gg + 64] = wpn[:, 2 * gp + gg]
    shared["wpool"] = wp
    ws = np.asarray(w_spatial, f)
    wsp = ws[:, :, sperm][:, :, :, sperm]
    shared["wsT"] = np.ascontiguousarray(wsp.transpose(0, 3, 1, 2))
    shared["bsp"] = np.ascontiguousarray(np.asarray(b_spatial, f)[:, :, sperm][:, None])
    shared["cmask"] = (sperm[:, None] <= sperm[None, :]).astype(f)
    shared["ident"] = np.eye(128, dtype=f)

    def permT(mat):
        return mat[:, sperm][:, :, sperm].transpose(2, 0, 1)
    gen_m, gen_h = _pool_mats(False)
    fst_m, fst_h = _pool_mats(True)
    xs = np.asarray(x, f)
    in_maps = []
    for c in range(NCORES):
        b, p = c // 4, c % 4
        xc = xs[b, p * T:(p + 1) * T][tok]
        xT = np.ascontiguousarray(xc.T.reshape(8, 128, T).transpose(1, 0, 2))
        m0, h0 = (fst_m, fst_h) if p == 0 else (gen_m, gen_h)
        pm = np.concatenate([permT(gen_m), permT(gen_h), permT(m0), permT(h0)], axis=1)
        ohv = np.zeros((128, 3, 8), f)
        for d in range(1, 4):
            if p - d >= 0:
                ohv[:, d - 1, c - d] = 1.0
        mp = dict(shared)
        mp["xT"] = xT
        mp["pmat"] = np.ascontiguousarray(pm.astype(f))
        mp["oh"] = ohv
        in_maps.append(mp)
    return in_maps, tok


def _assemble(results, tok, key="outT"):
    out = np.zeros((2, 4 * T, DM), np.float32)
    inv = np.empty_like(tok)
    inv[tok] = np.arange(T)
    for c in range(NCORES):
        b, p = c // 4, c % 4
        oT = np.asarray(results[c][key])
        xc = oT.transpose(1, 0, 2).reshape(DM, T).T
        out[b, p * T:(p + 1) * T] = xc[inv]
    return out


def kernel(**inputs):
    in_maps, tok = _prep_inputs(**inputs)
    nc = build_program()
    res = run_bass_kernel_spmd(nc, in_maps, core_ids=list(range(NCORES)))
    return _assemble(res.results, tok)
```

```python
import math
from contextlib import ExitStack

import numpy as np
import concourse.bass as bass
import concourse.mybir as mybir
from concourse.bass_utils import run_bass_kernel_spmd

F32 = mybir.dt.float32
BF16 = mybir.dt.bfloat16
AF = mybir.ActivationFunctionType
ALU = mybir.AluOpType
AX = mybir.AxisListType

NCORES = 8
T = 2048
DM = 1024
DFF = 2816
NF = DFF // 128
EPS = 1e-6
GW = 48 + 256


class Eng:
    def __init__(self, e, sem, name):
        self.e, self.sem, self.name = e, sem, name
        self.count = 0
        self.waited = {}


ALL_DSEMS = []


class DSem:
    def __init__(self, sem, name):
        self.sem, self.name = sem, name
        self.count = 0
        ALL_DSEMS.append(self)


class Buf:
    __slots__ = ("name", "w", "r")

    def __init__(self, name):
        self.name = name
        self.w = None
        self.r = {}


def _sync(E, reads, writes):
    deps = []
    for b in reads:
        if b.w is not None:
            deps.append(b.w)
    for b in writes:
        if b.w is not None:
            deps.append(b.w)
        deps.extend(b.r.items())
    for src, val in deps:
        if src is E and E.name == "pe":
            continue
        if E.waited.get(src, 0) < val:
            E.e.wait_ge(src.sem, val)
            E.waited[src] = val


def _mark(src, val, reads, writes):
    for b in reads:
        if b.r.get(src, 0) < val:
            b.r[src] = val
    for b in writes:
        b.w = (src, val)
        b.r = {}


def op(E, fn, reads=(), writes=()):
    _sync(E, reads, writes)
    ins = fn()
    E.count += 1
    ins.then_inc(E.sem, 1)
    _mark(E, E.count, reads, writes)


def grp(E, fns, reads=(), writes=()):
    _sync(E, reads, writes)
    ins = None
    for f in fns:
        ins = f()
    E.count += 1
    ins.then_inc(E.sem, 1)
    _mark(E, E.count, reads, writes)


def dma(Q, dsem, out, in_, reads=(), writes=()):
    _sync(Q, reads, writes)
    ins = Q.e.dma_start(out=out, in_=in_)
    dsem.count += 16
    ins.then_inc(dsem.sem, 16)
    _mark(dsem, dsem.count, reads, writes)


def mkap(base, off, pat, p0=None, np_=None):
    b = base if p0 is None else base[p0:p0 + np_]
    pp = list(b.ap[0])
    return bass.AP(tensor=b.tensor, offset=b.offset + off, ap=[pp] + [list(x) for x in pat])


def _pos_to_tok():
    tok = np.zeros(T, np.int64)
    for q in range(16):
        for r in range(8):
            jj = np.arange(16)
            tok[q * 128 + r * 16 + jj] = 128 * q + 8 * jj + r
    return tok


def _chunk_perm():
    m = np.arange(128)
    return 8 * (m % 16) + (m // 16)


_POOL_W = (2, 4, 8, 16)


def _pool_mats(first_chunk_of_seq):
    main = np.zeros((4, 128, 128), np.float32)
    halo = np.zeros((4, 128, 128), np.float32)
    for g, w in enumerate(_POOL_W):
        for t in range(128):
            cnt = min(t + 1, w) if first_chunk_of_seq else w
            for s in range(t - w + 1, t + 1):
                if s >= 0:
                    main[g, t, s] += 1.0 / cnt
                elif not first_chunk_of_seq:
                    halo[g, t, s + 128] += 1.0 / cnt
            main[g, t, t] -= 1.0
    return main, halo


class _Stop(Exception):
    pass


def build_program(dbg=None, stop=None):
    ALL_DSEMS.clear()
    nc = bass.Bass("TRN2", target_bir_lowering=False)
    es = ExitStack()

    def din(name, shape, dt=F32):
        return nc.dram_tensor(name, list(shape), dt, kind="ExternalInput").ap()

    d_x = din("xT", [128, 8, T])
    d_win = din("w_in", [2, DM, 1408])
    d_wout = din("w_out", [2, DM, DM])
    d_wg = din("w_gate", [2, DM, DFF])
    d_wu = din("w_up", [2, DM, DFF])
    d_wd = din("w_down", [2, DFF, DM])
    d_wglu = din("w_glu", [2, 384, 384])
    d_gcat = din("gcat", [128, 5, 8])
    d_ssms = din("ssm_s", [2, 128, 3, 12])
    d_ssmbc = din("ssm_bc", [2, 128, 4, 12, 16])
    d_vecs = din("vecs", [2, 128, 11])
    d_lnb = din("lnb", [2, 128, 384])
    d_wpool = din("wpool", [2, 128, 2, 128])
    d_wsT = din("wsT", [2, 128, 6, 128])
    d_bsp = din("bsp", [2, 1, 6, 128])
    d_cmask = din("cmask", [128, 128])
    d_pmat = din("pmat", [128, 16, 128])
    d_oh = din("oh", [128, 3, 8])
    d_ident = din("ident", [128, 128])
    d_out = nc.dram_tensor("outT", [128, 8, T], F32, kind="ExternalOutput").ap()
    d_agin = [nc.dram_tensor(f"ag_in{l}", [128, GW], BF16) for l in range(2)]
    d_agout = [nc.dram_tensor(f"ag_out{l}", [128 * NCORES, GW], BF16) for l in range(2)]
    d_dbg = {}
    if dbg:
        for name, shape in dbg.items():
            d_dbg[name] = nc.dram_tensor("dbg_" + name, list(shape), F32, kind="ExternalOutput").ap()

    def sem(name):
        return es.enter_context(nc.semaphore(name))

    PE = Eng(nc.tensor, sem("s_pe"), "pe")
    ACT = Eng(nc.scalar, sem("s_act"), "act")
    DVE = Eng(nc.vector, sem("s_dve"), "dve")
    POOL = Eng(nc.gpsimd, sem("s_pool"), "pool")
    SP = Eng(nc.sync, sem("s_sp"), "sp")
    ENGS = [PE, ACT, DVE, POOL, SP]

    def barrier():
        for E in ENGS:
            for E2 in ENGS:
                if E2 is not E and E2.count > 0 and E.waited.get(E2, 0) < E2.count:
                    E.e.wait_ge(E2.sem, E2.count)
                    E.waited[E2] = E2.count

    def sb(name, shape, dt):
        return nc.alloc_sbuf_tensor("sb_" + name, list(shape), dt).ap()

    xT = sb("xT", [128, 8, T], F32)
    xT_b = [Buf(f"xT{t}") for t in range(4)]
    ones_bf = sb("ones_bf", [128, 128], BF16)
    ident = sb("ident", [128, 128], F32)
    gcat = sb("gcat", [128, 5, 8], F32)
    cmask = sb("cmask", [128, 128], BF16)
    pmat = sb("pmat", [128, 16, 128], BF16)
    oh = sb("oh", [128, 3, 8], F32)
    kc = sb("kc", [128, 8], F32)
    ssm_s = sb("ssm_s", [128, 3, 12], F32)
    ssm_bc = sb("ssm_bc", [128, 4, 12, 16], F32)
    vecs = sb("vecs", [128, 11], F32)
    LNB = sb("lnb", [128, 384], BF16)
    WP = sb("wpool", [128, 2, 128], BF16)
    WsT = sb("wsT", [128, 6, 128], BF16)
    bsp = sb("bsp", [1, 6, 128], BF16)
    wglu = sb("wglu", [128, 3, 384], BF16)
    cterm = sb("cterm", [128, 3, 128], F32)
    b_gconst = Buf("gconst")
    b_lconst = Buf("lconst")
    b_cterm = Buf("cterm")
    ds_g = DSem(sem("ds_g"), "ds_g")
    ds_l = [DSem(sem(f"ds_l{l}"), f"ds_l{l}") for l in range(2)]

    AR_BYTES = 122 * 1024
    arena = sb("arena", [128, AR_BYTES // 4], F32)

    def carve(off, shape, dt):
        esz = 2 if dt == BF16 else 4
        n = int(np.prod(shape[1:]))
        assert off % 4 == 0 and (n * esz) % 4 == 0
        assert off + n * esz <= AR_BYTES, (off, n * esz)
        v = arena[:, off // 4: (off + n * esz) // 4]
        if dt != F32:
            v = v.bitcast(dt)
        if len(shape) == 2:
            return v
        names = " ".join(f"d{i}" for i in range(len(shape) - 1))
        kw = {f"d{i}": shape[i + 1] for i in range(len(shape) - 1)}
        return v.rearrange(f"p ({names}) -> p {names}", **kw)

    KB = 1024
    hT = carve(0, [128, 8, 1024], BF16)
    hT_b = [Buf("hT0"), Buf("hT1")]
    sq = carve(16 * KB, [128, 8, 512], BF16)
    sq_b = Buf("sq")
    rs = carve(24 * KB, [128, 512], F32)
    rs_b = Buf("rs")
    TMP = 26 * KB
    yT = carve(32 * KB, [128, 8, T], BF16)
    za_b = [[Buf(f"za{k}_{r}") for r in range(8)] for k in range(3)]
    yb_b = [Buf(f"yb{t}") for t in range(4)]
    yc_b = [Buf(f"yc{t}") for t in range(4)]
    zb = carve(64 * KB, [128, 16, 256], BF16)
    zb_b = [Buf(f"zb{q}") for q in range(16)]
    RA = 72 * KB

    ps_t = [nc.alloc_psum_tensor(f"ps{i}", [128, 512], F32).ap() for i in range(8)]
    ps_b = [Buf(f"ps{i}") for i in range(8)]
    ps_i = [0]

    def PSN():
        i = ps_i[0] % 8
        ps_i[0] += 1
        return ps_t[i], ps_b[i]

    v_ = nc.vector
    a_ = nc.scalar
    pe = nc.tensor

    dma(SP, ds_g, ident, d_ident, writes=[b_gconst])
    dma(SP, ds_g, gcat, d_gcat, writes=[b_gconst])
    dma(SP, ds_g, oh, d_oh, writes=[b_gconst])
    dma(POOL, ds_g, cmask, d_cmask, writes=[b_gconst])
    dma(POOL, ds_g, pmat, d_pmat, writes=[b_gconst])
    b_gconst.w = (ds_g, ds_g.count)
    for t in range(4):
        dsx = DSem(sem(f"ds_x{t}"), f"ds_x{t}")
        dma(SP, dsx, xT[:, :, t * 512:(t + 1) * 512], d_x[:, :, t * 512:(t + 1) * 512], writes=[xT_b[t]])
    b_kc = Buf("kc")
    op(DVE, lambda: v_.memset(ones_bf, 1.0), writes=[b_kc])
    op(DVE, lambda: v_.memset(kc[:, 0:1], EPS), writes=[b_kc])
    op(DVE, lambda: v_.memset(kc[:, 1:2], -math.pi), writes=[b_kc])
    op(DVE, lambda: v_.memset(kc[:, 2:3], 0.0), writes=[b_kc])
    KEPS = kc[:, 0:1]
    KNPI = kc[:, 1:2]

    def norm_tile(gi, tt, outs_fn, reads_extra=()):
        c0 = tt * 512
        op(ACT, lambda: a_.activation(out=sq, in_=xT[:, :, c0:c0 + 512], func=AF.Square),
           reads=[xT_b[tt]], writes=[sq_b])
        pst, psb = PSN()
        grp(PE, [(lambda k=k: pe.matmul(pst, lhsT=ones_bf, rhs=sq[:, k, :], start=(k == 0), stop=(k == 7)))
                 for k in range(8)], reads=[sq_b, b_kc], writes=[psb])
        op(ACT, lambda: a_.activation(out=rs, in_=pst, func=AF.Sqrt, bias=KEPS, scale=1.0 / DM),
           reads=[psb, b_kc], writes=[rs_b])
        op(DVE, lambda: v_.reciprocal(out=rs, in_=rs), reads=[rs_b], writes=[rs_b])
        outs_fn(c0)

    def norm_to_hT(gi, tt, lt):
        def outs(c0):
            for k in range(8):
                op(DVE, lambda k=k: v_.scalar_tensor_tensor(
                    out=hT[:, k, lt * 512:(lt + 1) * 512], in0=xT[:, k, c0:c0 + 512],
                    scalar=gcat[:, gi, k:k + 1], in1=rs, op0=ALU.mult, op1=ALU.mult),
                   reads=[xT_b[tt], rs_b, b_gconst], writes=[hT_b[lt]])
        norm_tile(gi, tt, outs)

    ds_win = DSem(sem("ds_win"), "ds_win")
    b_win = Buf("w_in")
    ds_wo = [DSem(sem(f"ds_wo{i}"), f"ds_wo{i}") for i in range(2)]
    b_wo = [Buf(f"wo{i}") for i in range(2)]
    ds_wgu = [DSem(sem(f"ds_wgu{i}"), f"ds_wgu{i}") for i in range(3)]
    b_wgu = [Buf(f"wgu{i}") for i in range(3)]
    ds_wd = [DSem(sem(f"ds_wd{i}"), f"ds_wd{i}") for i in range(3)]
    b_wd = [Buf(f"wd{i}") for i in range(3)]
    ds_ag = DSem(sem("ds_ag"), "ds_ag")
    ds_out = DSem(sem("ds_out"), "ds_out")
    cc_sem = sem("cc_sem")
    cc_cnt = [0]
    ds_dbg = DSem(sem("ds_dbg"), "ds_dbg")

    def stop_at(i):
        if stop is not None and stop == i:
            raise _Stop()

    def dbg_dump(name, ap, reads):
        if name in d_dbg:
            dma(POOL, ds_dbg, d_dbg[name], ap, reads=reads)

    try:
        for l in range(2):
            stop_at(20 * l)
            dsl = ds_l[l]
            dma(SP, dsl, ssm_s, d_ssms[l], writes=[b_lconst])
            dma(SP, dsl, ssm_bc, d_ssmbc[l], writes=[b_lconst])
            dma(SP, dsl, vecs, d_vecs[l], writes=[b_lconst])
            dma(POOL, dsl, LNB, d_lnb[l], writes=[b_lconst])
            dma(POOL, dsl, WP, d_wpool[l], writes=[b_lconst])
            dma(POOL, dsl, WsT, d_wsT[l], writes=[b_lconst])
            dma(POOL, dsl, bsp, d_bsp[l], writes=[b_lconst])
            dma(POOL, dsl, wglu, d_wglu[l].rearrange("(k p) c -> p k c", p=128), writes=[b_lconst])
            b_lconst.w = (dsl, dsl.count)
            Dsk = vecs[:, 0:3]
            bglu = vecs[:, 3:6]
            pscale = vecs[:, 6:8]
            lng = vecs[:, 8:11]

            w_in = carve(RA, [128, 8, 1408], BF16)
            for k in range(8):
                dma(POOL, ds_win, w_in[:, k, :], d_win[l, k * 128:(k + 1) * 128, :], writes=[b_win])
            uT = carve(RA + 22528, [128, 3, 1024], BF16)
            uT_b = [Buf("uT0"), Buf("uT1")]
            v32s = [carve(TMP + i * 1536, [128, 384], F32) for i in range(2)]
            vhats = [carve(TMP + 3072 + i * 768, [128, 384], BF16) for i in range(2)]
            stts = [carve(TMP + 4608 + i * 32, [128, 8], F32) for i in range(2)]
            tmpcs = [carve(TMP + 4672 + i * 512, [128, 128], F32) for i in range(2)]
            b_v32s, b_vhats = [Buf("v32a"), Buf("v32b")], [Buf("vha"), Buf("vhb")]
            b_stts, b_tmpcs = [Buf("sta"), Buf("stb")], [Buf("tca"), Buf("tcb")]

            op(DVE, lambda: v_.tensor_tensor(out=WsT, in0=WsT, in1=mkap(cmask, 0, [[0, 6], [1, 128]]), op=ALU.mult),
               reads=[b_lconst, b_gconst], writes=[b_lconst])
            for hp in range(3):
                pst, psb = PSN()
                fns = []
                for hh in range(2):
                    h = 2 * hp + hh
                    fns.append(lambda h=h, hh=hh: pe.matmul(pst[64 * hh:64 * hh + 64, 0:128], lhsT=LNB[:, 64 * h:64 * h + 64],
                                                           rhs=WsT[:, h, :], start=True, stop=False, tile_position=(0, 64 * hh)))
                    fns.append(lambda h=h, hh=hh: pe.matmul(pst[64 * hh:64 * hh + 64, 0:128], lhsT=ones_bf[0:1, 0:64],
                                                           rhs=bsp[0:1, h, :], start=False, stop=True, tile_position=(0, 64 * hh)))
                grp(PE, fns, reads=[b_lconst, b_kc], writes=[psb])
                op(ACT, lambda hp=hp: a_.copy(out=cterm[:, hp, :], in_=pst[:, 0:128]), reads=[psb], writes=[b_cterm])

            stop_at(20 * l + 1)
            for hf in range(2):
                for lt in range(2):
                    norm_to_hT(2 * l, 2 * hf + lt, lt)
                for k3 in range(3):
                    for lt in range(2):
                        tt = 2 * hf + lt
                        pst, psb = PSN()
                        grp(PE, [(lambda k=k: pe.matmul(pst, lhsT=w_in[:, k, k3 * 128:(k3 + 1) * 128],
                                                        rhs=hT[:, k, lt * 512:(lt + 1) * 512], start=(k == 0), stop=(k == 7)))
                                 for k in range(8)], reads=[b_win, hT_b[lt]], writes=[psb])
                        rb = za_b[k3]
                        op(ACT, lambda: a_.copy(out=yT[:, k3, tt * 512:(tt + 1) * 512], in_=pst), reads=[psb], writes=rb)
                        pst2, psb2 = PSN()
                        grp(PE, [(lambda k=k: pe.matmul(pst2, lhsT=w_in[:, k, 384 + k3 * 128:384 + (k3 + 1) * 128],
                                                        rhs=hT[:, k, lt * 512:(lt + 1) * 512], start=(k == 0), stop=(k == 7)))
                                 for k in range(8)], reads=[b_win, hT_b[lt]], writes=[psb2])
                        op(ACT, lambda: a_.activation(out=uT[:, k3, lt * 512:(lt + 1) * 512], in_=pst2, func=AF.Gelu_apprx_tanh),
                           reads=[psb2], writes=[uT_b[lt]])
                stop_at(20 * l + 2)
                def stageA(qq):
                    q = 8 * hf + qq
                    i2 = q % 2
                    v32_, vhat_, stt_ = v32s[i2], vhats[i2], stts[i2]
                    bv, bh, bs = b_v32s[i2], b_vhats[i2], b_stts[i2]
                    pB, pBb = PSN()
                    pV, pVb = PSN()
                    lhs = [hT[:, k, qq * 128:(qq + 1) * 128] for k in range(8)]
                    grp(PE, [(lambda k=k: pe.matmul(pB[:, 0:256], lhsT=lhs[k], rhs=w_in[:, k, 768:1024], start=(k == 0), stop=(k == 7)))
                             for k in range(8)], reads=[b_win, hT_b[qq // 4]], writes=[pBb])
                    grp(PE, [(lambda k=k: pe.matmul(pV[:, 0:384], lhsT=lhs[k], rhs=w_in[:, k, 1024:1408], start=(k == 0), stop=(k == 7)))
                             for k in range(8)], reads=[b_win, hT_b[qq // 4]], writes=[pVb])
                    op(ACT, lambda: a_.copy(out=zb[:, q, :], in_=pB[:, 0:256]), reads=[pBb], writes=[zb_b[q]])
                    op(ACT, lambda: a_.activation(out=v32_, in_=pV[:, 0:384], func=AF.Gelu_apprx_tanh), reads=[pVb], writes=[bv])
                    op(DVE, lambda: v_.bn_stats(out=stt_[:, 0:6], in_=v32_), reads=[bv], writes=[bs])

                    def SV(fn):
                        op(DVE, fn, reads=[bs], writes=[bs])
                    SV(lambda: v_.tensor_scalar(out=stt_[:, 6:7], in0=stt_[:, 1:2], scalar1=stt_[:, 4:5], scalar2=0.5,
                                                op0=ALU.add, op1=ALU.mult))
                    SV(lambda: v_.tensor_tensor(out=stt_[:, 0:1], in0=stt_[:, 1:2], in1=stt_[:, 4:5], op=ALU.subtract))
                    SV(lambda: v_.tensor_scalar(out=stt_[:, 0:1], in0=stt_[:, 0:1], scalar1=stt_[:, 0:1], scalar2=0.25,
                                                op0=ALU.mult, op1=ALU.mult))
                    SV(lambda: v_.tensor_tensor(out=stt_[:, 3:4], in0=stt_[:, 2:3], in1=stt_[:, 5:6], op=ALU.add))
                    SV(lambda: v_.scalar_tensor_tensor(out=stt_[:, 7:8], in0=stt_[:, 3:4], scalar=1.0 / 384.0, in1=stt_[:, 0:1],
                                                       op0=ALU.mult, op1=ALU.add))
                    op(ACT, lambda: a_.activation(out=stt_[:, 7:8], in_=stt_[:, 7:8], func=AF.Sqrt, bias=KEPS, scale=1.0),
                       reads=[bs, b_kc], writes=[bs])
                    op(DVE, lambda: v_.reciprocal(out=stt_[:, 7:8], in_=stt_[:, 7:8]), reads=[bs], writes=[bs])
                    op(DVE, lambda: v_.tensor_scalar(out=vhat_, in0=v32_, scalar1=stt_[:, 6:7], scalar2=stt_[:, 7:8],
                                                     op0=ALU.subtract, op1=ALU.mult), reads=[bv, bs], writes=[bh])

                def stageB(qq):
                    q = 8 * hf + qq
                    vhat_, bh = vhats[q % 2], b_vhats[q % 2]
                    for hp in range(3):
                        tmpc_, btc = tmpcs[hp % 2], b_tmpcs[hp % 2]
                        pst, psb = PSN()
                        grp(PE, [(lambda hh=hh: pe.matmul(pst[64 * hh:64 * hh + 64, 0:128],
                                                          lhsT=vhat_[:, 64 * (2 * hp + hh):64 * (2 * hp + hh) + 64],
                                                          rhs=WsT[:, 2 * hp + hh, :], start=True, stop=True,
                                                          tile_position=(0, 64 * hh))) for hh in range(2)],
                            reads=[bh, b_lconst], writes=[psb])
                        op(DVE, lambda hp=hp: v_.scalar_tensor_tensor(out=tmpc_, in0=pst[:, 0:128], scalar=lng[:, hp:hp + 1],
                                                                      in1=cterm[:, hp, :], op0=ALU.mult, op1=ALU.add),
                           reads=[psb, b_cterm, b_lconst], writes=[btc])
                        op(DVE, lambda hp=hp: v_.tensor_tensor(
                            out=yT[:, 5 + hp, q * 128:(q + 1) * 128], in0=tmpc_,
                            in1=uT[:, hp, qq * 128:(qq + 1) * 128], op=ALU.mult),
                           reads=[btc, uT_b[qq // 4]], writes=[yc_b[q // 4]])

                stageA(0)
                for qq in range(1, 8):
                    stageA(qq)
                    stageB(qq - 1)
                stageB(7)
            barrier()
            dbg_dump(f"yT_p1_{l}", yT, [za_b[k][r] for k in range(3) for r in range(8)] + yc_b)
            dbg_dump(f"zb_{l}", zb, zb_b)

            stop_at(20 * l + 3)
            W1 = carve(RA, [128, 3, 8, 2, 128], BF16)
            Cab = carve(RA + 12288, [128, 12, 9, 2, 32], BF16)
            Bbb = carve(RA + 26112, [128, 12, 2, 32], BF16)
            Ktap = carve(RA + 27648, [128, 3, 8, 128], BF16)
            TB = RA + 33792

            def tab(i):
                return carve(TB + i * 768, [128, 12, 16], F32)
            T1c, T1s, P8r, P8i, T2c, T2s, P128r, P128i, RH1, RH2 = [tab(i) for i in range(10)]
            SM = TB + 7680
            Ere = carve(SM, [128, 12, 16], F32)
            Eim = carve(SM + 768, [128, 12, 16], F32)
            Xre = carve(SM + 1536, [128, 12, 16], F32)
            Xim = carve(SM + 2304, [128, 12, 16], F32)
            Hre = carve(SM + 3072, [128, 12, 16], F32)
            Him = carve(SM + 3840, [128, 12, 16], F32)
            sm2 = carve(SM + 4608, [128, 16, 12], F32)
            A2048r, A2048i, hinr, hini = sm2[:, 0, :], sm2[:, 1, :], sm2[:, 2, :], sm2[:, 3, :]
            pwr = carve(0, [128, 9, 12], F32)
            pwi = carve(432, [128, 9, 12], F32)
            s12 = carve(1024, [128, 24, 12], F32)
            Bbr = carve(4096, [128, 12, 16], F32)
            Bbi = carve(4864, [128, 12, 16], F32)
            tA = carve(5632, [128, 12, 16], F32)
            tB = carve(6400, [128, 12, 16], F32)
            QQ = carve(8192, [128, 8, 2, 4, 32], F32)
            ts16 = carve(16 * KB, [128, 8, 12, 16], F32)
            b_su = Buf("setup")

            def V(fn, r=(), w=()):
                op(DVE, fn, reads=[b_su, b_lconst] + list(r), writes=[b_su] + list(w))

            def A(fn):
                op(ACT, fn, reads=[b_su, b_lconst, b_kc], writes=[b_su])

            def tt_(out, a, b, o):
                V(lambda: v_.tensor_tensor(out=out, in0=a, in1=b, op=o))

            def cmul(outr, outi, ar_, ai_, br_, bi_, t1, t2):
                tt_(t1, ar_, br_, ALU.mult)
                tt_(t2, ai_, bi_, ALU.mult)
                tt_(outr, t1, t2, ALU.subtract)
                tt_(t1, ar_, bi_, ALU.mult)
                tt_(t2, ai_, br_, ALU.mult)
                tt_(outi, t1, t2, ALU.add)

            Are, Aim, Ldt = ssm_s[:, 0, :], ssm_s[:, 1, :], ssm_s[:, 2, :]
            S = [s12[:, i, :] for i in range(24)]
            dtv, x1, th, mag, cs, sn, ar, ai, den, fre, fim, u1, u2 = S[0:13]
            A(lambda: a_.activation(out=dtv, in_=Ldt, func=AF.Exp))
            tt_(x1, Are, dtv, ALU.mult)
            tt_(th, Aim, dtv, ALU.mult)
            A(lambda: a_.activation(out=mag, in_=x1, func=AF.Exp))
            A(lambda: a_.activation(out=sn, in_=th, func=AF.Sin, scale=1.0 / 8.0))
            A(lambda: a_.activation(out=u1, in_=th, func=AF.Sin, scale=1.0 / 16.0))
            tt_(u2, u1, u1, ALU.mult)
            V(lambda: v_.tensor_scalar(out=cs, in0=u2, scalar1=-2.0, scalar2=1.0, op0=ALU.mult, op1=ALU.add))
            for _ in range(3):
                tt_(u1, cs, cs, ALU.mult)
                tt_(u2, sn, sn, ALU.mult)
                tt_(S[13], cs, sn, ALU.mult)
                tt_(cs, u1, u2, ALU.subtract)
                V(lambda: v_.tensor_scalar(out=sn, in0=S[13], scalar1=2.0, scalar2=None, op0=ALU.mult))
            tt_(ar, mag, cs, ALU.mult)
            tt_(ai, mag, sn, ALU.mult)
            tt_(u1, Are, Are, ALU.mult)
            tt_(u2, Aim, Aim, ALU.mult)
            tt_(den, u1, u2, ALU.add)
            V(lambda: v_.reciprocal(out=den, in_=den))
            arm1 = S[13]
            V(lambda: v_.tensor_scalar(out=arm1, in0=ar, scalar1=-1.0, scalar2=None, op0=ALU.add))
            tt_(u1, arm1, Are, ALU.mult)
            tt_(u2, ai, Aim, ALU.mult)
            tt_(u1, u1, u2, ALU.add)
            tt_(fre, u1, den, ALU.mult)
            tt_(u1, ai, Are, ALU.mult)
            tt_(u2, arm1, Aim, ALU.mult)
            tt_(u1, u1, u2, ALU.subtract)
            tt_(fim, u1, den, ALU.mult)
            Bre, Bim, Cre, Cim = ssm_bc[:, 0], ssm_bc[:, 1], ssm_bc[:, 2], ssm_bc[:, 3]

            def bc16(v):
                return mkap(v, 0, [[v.ap[1][0], 12], [0, 16]])
            cmul(Bbr, Bbi, bc16(fre), bc16(fim), Bre, Bim, tA, tB)
            V(lambda: v_.memset(pwr[:, 0, :], 1.0))
            V(lambda: v_.memset(pwi[:, 0, :], 0.0))
            for k in range(1, 9):
                cmul(pwr[:, k, :], pwi[:, k, :], pwr[:, k - 1, :], pwi[:, k - 1, :], ar, ai, S[14], S[15])
            A8r, A8i = pwr[:, 8, :], pwi[:, 8, :]
            rho8, irho8, u8r, u8i, rho128, irho, A128r, A128i, u128r, u128i = S[14:24]
            tt_(u1, mag, mag, ALU.mult)
            tt_(u2, u1, u1, ALU.mult)
            tt_(rho8, u2, u2, ALU.mult)
            V(lambda: v_.reciprocal(out=irho8, in_=rho8))
            tt_(u8r, A8r, irho8, ALU.mult)
            tt_(u8i, A8i, irho8, ALU.mult)

            def col(tb, j):
                return tb[:, :, j]

            def build_table(Tr, Ti, br_, bi_, first_one):
                if first_one:
                    V(lambda: v_.memset(col(Tr, 0), 1.0))
                    V(lambda: v_.memset(col(Ti, 0), 0.0))
                    V(lambda: v_.tensor_copy(out=col(Tr, 1), in_=br_))
                    V(lambda: v_.tensor_copy(out=col(Ti, 1), in_=bi_))
                    start = 2
                else:
                    V(lambda: v_.tensor_copy(out=col(Tr, 0), in_=br_))
                    V(lambda: v_.tensor_copy(out=col(Ti, 0), in_=bi_))
                    start = 1
                for j in range(start, 16):
                    cmul(col(Tr, j), col(Ti, j), col(Tr, j - 1), col(Ti, j - 1), br_, bi_, u1, u2)
            build_table(T1c, T1s, u8r, u8i, True)
            build_table(P8r, P8i, A8r, A8i, False)
            V(lambda: v_.tensor_copy(out=A128r, in_=col(P8r, 15)))
            V(lambda: v_.tensor_copy(out=A128i, in_=col(P8i, 15)))
            tt_(u1, rho8, rho8, ALU.mult)
            tt_(u2, u1, u1, ALU.mult)
            tt_(u1, u2, u2, ALU.mult)
            tt_(rho128, u1, u1, ALU.mult)
            V(lambda: v_.reciprocal(out=irho, in_=rho128))
            tt_(u128r, A128r, irho, ALU.mult)
            tt_(u128i, A128i, irho, ALU.mult)
            build_table(T2c, T2s, u128r, u128i, True)
            build_table(P128r, P128i, A128r, A128i, True)
            cmul(A2048r, A2048i, col(P128r, 15), col(P128i, 15), A128r, A128i, u1, u2)
            V(lambda: v_.tensor_copy(out=RH1, in_=bc16(rho8)))
            V(lambda: v_.memset(col(RH1, 0), 0.0))
            V(lambda: v_.tensor_copy(out=RH2, in_=bc16(rho128)))
            V(lambda: v_.memset(col(RH2, 0), 0.0))

            V(lambda: v_.memset(Cab, 0.0))
            V(lambda: v_.memset(Bbb, 0.0))
            npi = carve(2176, [128, 9, 12], F32)
            V(lambda: v_.tensor_scalar(out=npi, in0=pwi, scalar1=-1.0, scalar2=None, op0=ALU.mult))
            tA9 = carve(16 * KB, [128, 12, 9, 16], F32)
            tB9 = carve(16 * KB + 6912, [128, 12, 9, 16], F32)

            def p9(t):
                return mkap(t, 0, [[1, 12], [12, 9], [0, 16]])

            def c9(t):
                return mkap(t, 0, [[16, 12], [0, 9], [1, 16]])
            tt_(tA9, c9(Cre), p9(pwr), ALU.mult)
            tt_(tB9, c9(Cim), p9(pwi), ALU.mult)
            for g2 in range(2):
                V(lambda g2=g2: v_.tensor_tensor(
                    out=Cab[64 * g2:64 * g2 + 64, :, :, 0, 16 * g2:16 * g2 + 16],
                    in0=tA9[64 * g2:64 * g2 + 64], in1=tB9[64 * g2:64 * g2 + 64], op=ALU.subtract))
            tt_(tA9, c9(Cre), p9(npi), ALU.mult)
            tt_(tB9, c9(Cim), p9(pwr), ALU.mult)
            for g2 in range(2):
                V(lambda g2=g2: v_.tensor_tensor(
                    out=Cab[64 * g2:64 * g2 + 64, :, :, 1, 16 * g2:16 * g2 + 16],
                    in0=tA9[64 * g2:64 * g2 + 64], in1=tB9[64 * g2:64 * g2 + 64], op=ALU.subtract))
            for g2 in range(2):
                V(lambda g2=g2: v_.tensor_copy(out=Bbb[64 * g2:64 * g2 + 64, :, 0, 16 * g2:16 * g2 + 16], in_=Bbr[64 * g2:64 * g2 + 64]))
                V(lambda g2=g2: v_.tensor_copy(out=Bbb[64 * g2:64 * g2 + 64, :, 1, 16 * g2:16 * g2 + 16], in_=Bbi[64 * g2:64 * g2 + 64]))
            V(lambda: v_.memset(Ktap, 0.0))
            for k in range(3):
                pst, psb = PSN()
                pst2, psb2 = PSN()
                fns = []
                for qd in range(4):
                    p = 4 * k + qd
                    for half, pp in ((0, pst), (1, pst2)):
                        for ri in range(2):
                            fns.append(lambda p=p, qd=qd, half=half, pp=pp, ri=ri: pe.matmul(
                                mkap(pp, 32 * qd, [[128, 4], [1, 32]], p0=32 * qd, np_=32),
                                lhsT=Bbb[:, p, ri, :],
                                rhs=mkap(Cab, p * 576 + half * 4 * 64 + ri * 32, [[64, 4], [1, 32]]),
                                start=(ri == 0), stop=(ri == 1), tile_position=(0, 32 * qd)))
                grp(PE, fns, reads=[b_su], writes=[psb, psb2])
                for qd in range(4):
                    for half, pp, pb in ((0, pst, psb), (1, pst2, psb2)):
                        op(DVE, lambda qd=qd, half=half, pp=pp, k=k: v_.tensor_copy(
                            out=Ktap[32 * qd:32 * qd + 32, k, 4 * half:4 * half + 4, 32 * qd:32 * qd + 32],
                            in_=mkap(pp, 32 * qd, [[128, 4], [1, 32]], p0=32 * qd, np_=32)),
                           reads=[pb, b_su], writes=[b_su])
                V(lambda k=k: v_.scalar_tensor_tensor(out=Ktap[:, k, 0, :], in0=ident, scalar=Dsk[:, k:k + 1],
                                                      in1=Ktap[:, k, 0, :], op0=ALU.mult, op1=ALU.add), r=[b_gconst])
            tA8 = carve(16 * KB, [128, 8, 4, 16], F32)
            tB8 = carve(16 * KB + 2048, [128, 8, 4, 16], F32)
            b_W1 = Buf("W1")
            for k in range(3):
                V(lambda: v_.memset(QQ, 0.0))
                pr8 = mkap(pwr, 4 * k, [[12, 8], [1, 4], [0, 16]])
                pi8 = mkap(pwi, 4 * k, [[12, 8], [1, 4], [0, 16]])
                br4 = mkap(Bbr, 4 * k * 16, [[0, 8], [16, 4], [1, 16]])
                bi4 = mkap(Bbi, 4 * k * 16, [[0, 8], [16, 4], [1, 16]])
                tt_(tA8, pr8, br4, ALU.mult)
                tt_(tB8, pi8, bi4, ALU.mult)
                for g2 in range(2):
                    V(lambda g2=g2: v_.tensor_tensor(
                        out=QQ[64 * g2:64 * g2 + 64, :, 0, :, 16 * g2:16 * g2 + 16],
                        in0=tA8[64 * g2:64 * g2 + 64], in1=tB8[64 * g2:64 * g2 + 64], op=ALU.subtract))
                tt_(tA8, pr8, bi4, ALU.mult)
                tt_(tB8, pi8, br4, ALU.mult)
                for g2 in range(2):
                    V(lambda g2=g2: v_.tensor_tensor(
                        out=QQ[64 * g2:64 * g2 + 64, :, 1, :, 16 * g2:16 * g2 + 16],
                        in0=tA8[64 * g2:64 * g2 + 64], in1=tB8[64 * g2:64 * g2 + 64], op=ALU.add))
                for pw_ in range(8):
                    for ri in range(2):
                        pst, psb = PSN()
                        op(PE, lambda pw_=pw_, ri=ri: pe.transpose(pst[:, 0:128], QQ[:, pw_, ri].rearrange("p a b -> p (a b)"), ident),
                           reads=[b_su, b_gconst], writes=[psb])
                        op(ACT, lambda pw_=pw_, ri=ri, k=k: a_.copy(out=W1[:, k, pw_, ri, :], in_=pst[:, 0:128]),
                           reads=[psb], writes=[b_W1])
                if k == 2:
                    dbg_dump(f"tabs_{l}", carve(TB, [128, 1920], F32), [b_su])
                    dbg_dump(f"pw_{l}", carve(0, [128, 216], F32), [b_su])
                    dbg_dump(f"s12_{l}", carve(1024, [128, 288], F32), [b_su])
                    dbg_dump(f"Bb_{l}", carve(4096, [128, 384], F32), [b_su])
                    dbg_dump(f"Ktap_{l}", Ktap.rearrange("p a b c -> p (a b c)"), [b_su])
                    dbg_dump(f"W1_{l}", W1.rearrange("p a b c d -> p (a b c d)"), [b_su, b_W1])
                    dbg_dump(f"Cab_{l}", Cab.rearrange("p a b c d -> p (a b c d)"), [b_su])
                V(lambda: v_.memset(kc[:, 3:4], 0.0))
            barrier()

            stop_at(20 * l + 4)
            Sb = carve(0, [128, 2, 12, 260], BF16)
            b_Sb = Buf("Sb")
            Gs = carve(16 * KB, [128, 8, GW], BF16)
            b_Gs = Buf("Gs")
            wo = [carve(RA + 46 * KB + i * 2048, [128, 8, 128], BF16) for i in range(2)]
            wr = carve(TMP, [128, 256], F32)
            wi = carve(TMP + 1024, [128, 256], F32)
            t1 = carve(TMP + 2048, [128, 256], F32)
            t2 = carve(TMP + 3072, [128, 256], F32)
            b_scan = Buf("scan")
            rhfull = carve(TMP + 4096, [128, 256], F32)

            def b16x16(tb, p):
                return mkap(tb, p * 16, [[0, 16], [1, 16]])

            def v3(x):
                return x.rearrange("p (a b) -> p a b", a=16)

            pooleds = [carve(24 * KB + i * 256, [128, 128], BF16) for i in range(2)]
            b_pooleds = [Buf("pooled0"), Buf("pooled1")]
            pool_i = [0]

            def pool_chunk(q, prev_ap, prev_bufs, mi, hi):
                hf, qq = q // 8, q % 8
                for gp in range(2):
                    pooled, b_pooled = pooleds[pool_i[0] % 2], b_pooleds[pool_i[0] % 2]
                    pool_i[0] += 1
                    pst, psb = PSN()
                    fns = []
                    for gg in range(2):
                        g = 2 * gp + gg
                        fns.append(lambda g=g, gg=gg: pe.matmul(pst[64 * gg:64 * gg + 64, 0:128], lhsT=zb[:, q, 64 * g:64 * g + 64],
                                                               rhs=pmat[:, 4 * mi + g, :], start=True, stop=False, tile_position=(0, 64 * gg)))
                        fns.append(lambda g=g, gg=gg: pe.matmul(pst[64 * gg:64 * gg + 64, 0:128], lhsT=prev_ap[:, 64 * g:64 * g + 64],
                                                               rhs=pmat[:, 4 * hi + g, :], start=False, stop=True, tile_position=(0, 64 * gg)))
                    grp(PE, fns, reads=[zb_b[q], b_gconst] + prev_bufs, writes=[psb])
                    op(ACT, lambda: a_.copy(out=pooled, in_=pst[:, 0:128]), reads=[psb], writes=[b_pooled])
                    pst2, psb2 = PSN()
                    grp(PE, [lambda gp=gp: pe.matmul(pst2[:, 0:128], lhsT=WP[:, gp, :], rhs=pooled, start=True, stop=True)],
                        reads=[b_pooled, b_lconst], writes=[psb2])
                    op(ACT, lambda gp=gp: a_.activation(out=yT[:, 3 + gp, q * 128:(q + 1) * 128], in_=pst2[:, 0:128],
                                                        func=AF.Copy, scale=pscale[:, gp:gp + 1]),
                       reads=[psb2, b_lconst], writes=[yb_b[q // 4]])

            for p in range(12):
                k, qd = p // 4, p % 4
                pst, psb = PSN()
                fns = []
                for ri in range(2):
                    for rp in range(8):
                        fns.append(lambda ri=ri, rp=rp: pe.matmul(
                            pst[:, ri * 256:(ri + 1) * 256], lhsT=W1[32 * qd:32 * qd + 32, k, 7 - rp, ri, :],
                            rhs=mkap(yT, k * T + rp * 16, [[128, 16], [1, 16]], p0=32 * qd, np_=32),
                            start=(rp == 0), stop=(rp == 7), tile_position=(32 * qd, 0)))
                grp(PE, fns, reads=[b_su, b_W1] + za_b[k], writes=[psb])
                Lr, Li = v3(pst[:, 0:256]), v3(pst[:, 256:512])
                c_, s_ = b16x16(T1c, p), b16x16(T1s, p)

                def D(fn, extra=()):
                    op(DVE, fn, reads=[psb, b_su, b_scan] + list(extra), writes=[b_scan])
                D(lambda: v_.tensor_tensor(out=v3(t1), in0=Lr, in1=c_, op=ALU.mult))
                D(lambda: v_.tensor_tensor(out=v3(t2), in0=Li, in1=s_, op=ALU.mult))
                D(lambda: v_.tensor_tensor(out=wr, in0=t1, in1=t2, op=ALU.add))
                D(lambda: v_.tensor_tensor(out=v3(t1), in0=Li, in1=c_, op=ALU.mult))
                D(lambda: v_.tensor_tensor(out=v3(t2), in0=Lr, in1=s_, op=ALU.mult))
                D(lambda: v_.tensor_tensor(out=wi, in0=t1, in1=t2, op=ALU.subtract))
                D(lambda: v_.tensor_copy(out=v3(rhfull), in_=mkap(RH1, p * 16, [[0, 16], [1, 16]])))
                D(lambda: v_.tensor_tensor_scan(out=wr, data0=rhfull, data1=wr, initial=0.0, op0=ALU.mult, op1=ALU.add))
                D(lambda: v_.tensor_tensor_scan(out=wi, data0=rhfull, data1=wi, initial=0.0, op0=ALU.mult, op1=ALU.add))
                D(lambda: v_.tensor_tensor(out=v3(t1), in0=v3(wr), in1=c_, op=ALU.mult))
                D(lambda: v_.tensor_tensor(out=v3(t2), in0=v3(wi), in1=s_, op=ALU.mult))
                op(DVE, lambda p=p: v_.tensor_tensor(out=Sb[:, 0, p, 1:257], in0=t1, in1=t2, op=ALU.subtract),
                   reads=[b_scan], writes=[b_Sb, b_scan])
                op(DVE, lambda p=p: v_.tensor_tensor(out=Ere[:, p, :], in0=mkap(t1, 15, [[16, 16]]), in1=mkap(t2, 15, [[16, 16]]),
                                                     op=ALU.subtract), reads=[b_scan, b_su], writes=[b_su, b_scan])
                D(lambda: v_.tensor_tensor(out=v3(t1), in0=v3(wr), in1=s_, op=ALU.mult))
                D(lambda: v_.tensor_tensor(out=v3(t2), in0=v3(wi), in1=c_, op=ALU.mult))
                op(DVE, lambda p=p: v_.tensor_tensor(out=Sb[:, 1, p, 1:257], in0=t1, in1=t2, op=ALU.add),
                   reads=[b_scan], writes=[b_Sb, b_scan])
                op(DVE, lambda p=p: v_.tensor_tensor(out=Eim[:, p, :], in0=mkap(t1, 15, [[16, 16]]), in1=mkap(t2, 15, [[16, 16]]),
                                                     op=ALU.add), reads=[b_scan, b_su], writes=[b_su, b_scan])
                for q in range(1 + (15 * p) // 12, 1 + (15 * (p + 1)) // 12):
                    pool_chunk(q, zb[:, q - 1, :], [zb_b[q - 1]], 0, 1)
            stop_at(20 * l + 5)
            f192 = lambda x: x.rearrange("p a b -> p (a b)")
            xa, xb_, xc, xd = [ts16[:, i] for i in range(4)]
            tt_(xa, Ere, T2c, ALU.mult)
            tt_(xb_, Eim, T2s, ALU.mult)
            tt_(xc, xa, xb_, ALU.add)
            tt_(xa, Eim, T2c, ALU.mult)
            tt_(xb_, Ere, T2s, ALU.mult)
            tt_(xd, xa, xb_, ALU.subtract)
            V(lambda: v_.tensor_tensor_scan(out=f192(xc), data0=f192(RH2), data1=f192(xc), initial=0.0, op0=ALU.mult, op1=ALU.add))
            V(lambda: v_.tensor_tensor_scan(out=f192(xd), data0=f192(RH2), data1=f192(xd), initial=0.0, op0=ALU.mult, op1=ALU.add))
            tt_(xa, xc, T2c, ALU.mult)
            tt_(xb_, xd, T2s, ALU.mult)
            tt_(Xre, xa, xb_, ALU.subtract)
            tt_(xa, xc, T2s, ALU.mult)
            tt_(xb_, xd, T2c, ALU.mult)
            tt_(Xim, xa, xb_, ALU.add)
            stop_at(20 * l + 6)
            agst = carve(20 * KB + 768 * 2, [128, 24], F32)
            b_ag = Buf("agst")
            op(DVE, lambda: v_.tensor_copy(out=agst[:, 0:12], in_=col(Xre, 15)), reads=[b_su], writes=[b_ag])
            op(DVE, lambda: v_.tensor_copy(out=agst[:, 12:24], in_=col(Xim, 15)), reads=[b_su], writes=[b_ag])
            b_agd = Buf("agdram")
            dma(POOL, ds_ag, d_agin[l].ap()[:, 0:48], agst.bitcast(BF16), reads=[b_ag], writes=[b_agd])
            dma(POOL, ds_ag, d_agin[l].ap()[:, 48:GW], zb[:, 15, :], reads=[zb_b[15]], writes=[b_agd])
            _sync(POOL, [b_agd], [])
            cc = nc.gpsimd.collective_compute("AllGather", ALU.bypass, replica_groups=[list(range(NCORES))],
                                              ins=[d_agin[l].ap().opt()], outs=[d_agout[l].ap().opt()])
            cc_cnt[0] += 1
            cc.then_inc(cc_sem, 1)
            nc.gpsimd.wait_ge(cc_sem, cc_cnt[0])
            dma(POOL, ds_ag, Gs, d_agout[l].ap().rearrange("(c p) f -> p c f", p=128), writes=[b_Gs])

            stop_at(20 * l + 7)

            stop_at(20 * l + 8)
            Gs32 = Gs[:, :, 0:48].bitcast(F32)
            g24 = carve(TMP + 4096, [128, 24, 8], F32)
            sel = carve(TMP + 4096 + 768, [128, 3, 24], F32)
            for d in range(3):
                V(lambda d=d: v_.tensor_tensor(out=g24, in0=mkap(Gs32, 0, [[1, 24], [GW // 2, 8]]),
                                               in1=mkap(oh, d * 8, [[0, 24], [1, 8]]), op=ALU.mult), r=[b_Gs, b_gconst])
                V(lambda d=d: v_.tensor_reduce(out=sel[:, d, :], in_=g24, axis=AX.X, op=ALU.add))
            V(lambda: v_.tensor_copy(out=hinr, in_=sel[:, 2, 0:12]))
            V(lambda: v_.tensor_copy(out=hini, in_=sel[:, 2, 12:24]))
            hr2, hi2 = sm2[:, 4, :], sm2[:, 5, :]
            for d in (1, 0):
                cmul(hr2, hi2, A2048r, A2048i, hinr, hini, sm2[:, 6, :], sm2[:, 7, :])
                tt_(hinr, hr2, sel[:, d, 0:12], ALU.add)
                tt_(hini, hi2, sel[:, d, 12:24], ALU.add)
            xs_a = carve(22528, [128, 12, 16], F32)
            xs_b = carve(23296, [128, 12, 16], F32)
            cmul(Hre, Him, P128r, P128i, bc16(hinr), bc16(hini), xs_a, xs_b)
            tt_(Hre[:, :, 1:16], Hre[:, :, 1:16], Xre[:, :, 0:15], ALU.add)
            tt_(Him[:, :, 1:16], Him[:, :, 1:16], Xim[:, :, 0:15], ALU.add)
            for p in range(12):
                pr_ = mkap(P8r, p * 16, [[0, 16], [1, 16]])
                pi_ = mkap(P8i, p * 16, [[0, 16], [1, 16]])
                hr_ = mkap(Hre, p * 16, [[1, 16], [0, 16]])
                hi_ = mkap(Him, p * 16, [[1, 16], [0, 16]])

                def D2(fn):
                    op(DVE, fn, reads=[b_su, b_scan, b_Sb], writes=[b_scan, b_Sb])
                D2(lambda: v_.tensor_tensor(out=v3(t1), in0=pr_, in1=hr_, op=ALU.mult))
                D2(lambda: v_.tensor_tensor(out=v3(t2), in0=pi_, in1=hi_, op=ALU.mult))
                D2(lambda: v_.tensor_tensor(out=wr, in0=t1, in1=t2, op=ALU.subtract))
                D2(lambda p=p: v_.tensor_tensor(out=Sb[:, 0, p, 1:257], in0=Sb[:, 0, p, 1:257], in1=wr, op=ALU.add))
                D2(lambda: v_.tensor_tensor(out=v3(t1), in0=pr_, in1=hi_, op=ALU.mult))
                D2(lambda: v_.tensor_tensor(out=v3(t2), in0=pi_, in1=hr_, op=ALU.mult))
                D2(lambda: v_.tensor_tensor(out=wi, in0=t1, in1=t2, op=ALU.add))
                D2(lambda p=p: v_.tensor_tensor(out=Sb[:, 1, p, 1:257], in0=Sb[:, 1, p, 1:257], in1=wi, op=ALU.add))
            op(DVE, lambda: v_.tensor_copy(out=Sb[:, 0, :, 0], in_=hinr), reads=[b_su], writes=[b_Sb])
            op(DVE, lambda: v_.tensor_copy(out=Sb[:, 1, :, 0], in_=hini), reads=[b_su], writes=[b_Sb])

            stop_at(20 * l + 9)
            dbg_dump(f"Sb_{l}", Sb.rearrange("p a b c -> p (a b c)"), [b_Sb])
            dbg_dump(f"sm_{l}", carve(SM, [128, 1344], F32), [b_su])
            for k in range(3):
                for r in range(7, -1, -1):
                    pst, psb = PSN()
                    fns = []
                    for tau in range(r + 1):
                        fns.append(lambda tau=tau: pe.matmul(
                            pst[:, 0:256], lhsT=Ktap[:, k, tau, :],
                            rhs=mkap(yT, k * T + (r - tau) * 16, [[128, 16], [1, 16]]),
                            start=(tau == 0), stop=False))
                    for qd in range(4):
                        p = 4 * k + qd
                        for ri in range(2):
                            fns.append(lambda qd=qd, p=p, ri=ri: pe.matmul(
                                pst[32 * qd:32 * qd + 32, 0:256], lhsT=Cab[:, p, r + 1, ri, :],
                                rhs=Sb[:, ri, p, 0:256], start=False, stop=(ri == 1), tile_position=(0, 32 * qd)))
                    grp(PE, fns, reads=[b_su, b_Sb] + za_b[k][0:r + 1], writes=[psb])
                    op(ACT, lambda: a_.activation(
                        out=mkap(yT, k * T + r * 16, [[128, 16], [1, 16]]),
                        in_=pst[:, 0:256].rearrange("p (a b) -> p a b", a=16), func=AF.Gelu_apprx_tanh),
                       reads=[psb], writes=[za_b[k][r]])
            dbg_dump(f"gT_{l}", yT[:, 0:3, :], [za_b[k][r] for k in range(3) for r in range(8)])
            stop_at(20 * l + 10)
            sig = carve(24 * KB + 512, [128, 512], BF16)
            b_sig = Buf("sig")
            for tt in range(4):
                rbs = [za_b[k][i] for k in range(3) for i in range(8)]
                pss = []
                for m3 in range(3):
                    pst, psb = PSN()
                    grp(PE, [(lambda kk=kk: pe.matmul(pst, lhsT=wglu[:, kk, m3 * 128:(m3 + 1) * 128],
                                                      rhs=yT[:, kk, tt * 512:(tt + 1) * 512], start=(kk == 0), stop=(kk == 2)))
                             for kk in range(3)], reads=rbs + [b_lconst], writes=[psb])
                    pss.append((pst, psb))
                for m3 in range(3):
                    pst, psb = pss[m3]
                    op(ACT, lambda: a_.activation(out=sig, in_=pst, func=AF.Sigmoid, bias=bglu[:, m3:m3 + 1], scale=1.0),
                       reads=[psb, b_lconst], writes=[b_sig])
                    op(DVE, lambda: v_.tensor_tensor(out=yT[:, m3, tt * 512:(tt + 1) * 512], in0=yT[:, m3, tt * 512:(tt + 1) * 512],
                                                     in1=sig, op=ALU.mult),
                       reads=[b_sig] + rbs, writes=za_b[m3])
            stop_at(20 * l + 11)
            zprev = carve(TMP + 4096 + 1536, [128, 256], BF16)
            zt8 = carve(0 + 13 * KB, [128, 64, 8], F32)
            b_zprev = Buf("zprev")
            for c4 in range(4):
                op(DVE, lambda c4=c4: v_.tensor_tensor(out=zt8, in0=mkap(Gs, 48 + 64 * c4, [[1, 64], [GW, 8]]),
                                                       in1=mkap(oh, 0, [[0, 64], [1, 8]]), op=ALU.mult),
                   reads=[b_Gs, b_gconst, b_Sb], writes=[b_zprev])
                with nc.allow_low_precision("one-hot select: exactly one non-zero term"):
                    op(DVE, lambda c4=c4: v_.tensor_reduce(out=zprev[:, 64 * c4:64 * c4 + 64], in_=zt8, axis=AX.X, op=ALU.add),
                       reads=[b_zprev], writes=[b_zprev])
            pool_chunk(0, zprev, [b_zprev], 2, 3)
            dbg_dump(f"yT_p2_{l}", yT, [za_b[k][r] for k in range(3) for r in range(8)] + yc_b + yb_b)

            stop_at(20 * l + 12)
            for m in range(8):
                s = m % 2
                dma(POOL, ds_wo[s], wo[s], d_wout[l, :, m * 128:(m + 1) * 128].rearrange("(k p) c -> p k c", p=128), writes=[b_wo[s]])
                for tt in range(4):
                    pst, psb = PSN()
                    rbs = [za_b[k][i] for k in range(3) for i in range(8)] + [yb_b[tt], yc_b[tt]]
                    grp(PE, [(lambda k=k: pe.matmul(pst, lhsT=wo[s][:, k, :], rhs=yT[:, k, tt * 512:(tt + 1) * 512],
                                                    start=(k == 0), stop=(k == 7))) for k in range(8)],
                        reads=rbs + [b_wo[s]], writes=[psb])
                    op(DVE, lambda: v_.tensor_tensor(out=xT[:, m, tt * 512:(tt + 1) * 512], in0=pst, in1=xT[:, m, tt * 512:(tt + 1) * 512],
                                                     op=ALU.add), reads=[psb], writes=[xT_b[tt]])
            barrier()
            dbg_dump(f"x_mix_{l}", xT, xT_b)

            stop_at(20 * l + 13)
            aT = carve(32 * KB, [128, NF, 1024], BF16)
            b_aT = [Buf("aT0"), Buf("aT1")]
            wgu = [carve(76 * KB + i * 4096, [128, 2, 8, 128], BF16) for i in range(3)]
            wd = [carve(88 * KB + i * 5632, [128, NF, 128], BF16) for i in range(3)]
            sg = carve(TMP, [128, 512], BF16)
            b_sg = Buf("sg")
            ci = [0, 0]
            for t2_ in range(2):
                for lt in range(2):
                    norm_to_hT(2 * l + 1, 2 * t2_ + lt, lt)
                for f in range(NF):
                    s = ci[0] % 3
                    ci[0] += 1
                    dma(POOL, ds_wgu[s], wgu[s][:, 0], d_wg[l, :, f * 128:(f + 1) * 128].rearrange("(k p) c -> p k c", p=128), writes=[b_wgu[s]])
                    dma(POOL, ds_wgu[s], wgu[s][:, 1], d_wu[l, :, f * 128:(f + 1) * 128].rearrange("(k p) c -> p k c", p=128), writes=[b_wgu[s]])
                    for lt in range(2):
                        pg, pgb = PSN()
                        pu, pub = PSN()
                        grp(PE, [(lambda k=k: pe.matmul(pg, lhsT=wgu[s][:, 0, k, :], rhs=hT[:, k, lt * 512:(lt + 1) * 512],
                                                        start=(k == 0), stop=(k == 7))) for k in range(8)],
                            reads=[b_wgu[s], hT_b[lt]], writes=[pgb])
                        grp(PE, [(lambda k=k: pe.matmul(pu, lhsT=wgu[s][:, 1, k, :], rhs=hT[:, k, lt * 512:(lt + 1) * 512],
                                                        start=(k == 0), stop=(k == 7))) for k in range(8)],
                            reads=[b_wgu[s], hT_b[lt]], writes=[pub])
                        op(ACT, lambda: a_.activation(out=sg, in_=pg, func=AF.Silu), reads=[pgb], writes=[b_sg])
                        op(DVE, lambda: v_.tensor_tensor(out=aT[:, f, lt * 512:(lt + 1) * 512], in0=sg, in1=pu, op=ALU.mult),
                           reads=[b_sg, pub], writes=[b_aT[lt]])
                for m in range(8):
                    s = ci[1] % 3
                    ci[1] += 1
                    dma(POOL, ds_wd[s], wd[s], d_wd[l, :, m * 128:(m + 1) * 128].rearrange("(f p) c -> p f c", p=128), writes=[b_wd[s]])
                    for lt in range(2):
                        tt = 2 * t2_ + lt
                        pst, psb = PSN()
                        grp(PE, [(lambda f=f: pe.matmul(pst, lhsT=wd[s][:, f, :], rhs=aT[:, f, lt * 512:(lt + 1) * 512],
                                                        start=(f == 0), stop=(f == NF - 1))) for f in range(NF)],
                            reads=[b_wd[s], b_aT[lt]], writes=[psb])
                        op(DVE, lambda: v_.tensor_tensor(out=xT[:, m, tt * 512:(tt + 1) * 512], in0=pst,
                                                         in1=xT[:, m, tt * 512:(tt + 1) * 512], op=ALU.add),
                           reads=[psb], writes=[xT_b[tt]])
            barrier()
            dbg_dump(f"x_ffn_{l}", xT, xT_b)

        ost = carve(0, [128, 8, 512], F32)
        b_ost = Buf("ost")
        for tt in range(4):
            def outs(c0):
                for k in range(8):
                    op(DVE, lambda k=k: v_.scalar_tensor_tensor(out=ost[:, k, :], in0=xT[:, k, c0:c0 + 512], scalar=gcat[:, 4, k:k + 1],
                                                                in1=rs, op0=ALU.mult, op1=ALU.mult),
                       reads=[xT_b[tt], rs_b, b_gconst], writes=[b_ost])
            norm_tile(4, tt, outs)
            dma(SP, ds_out, d_out[:, :, tt * 512:(tt + 1) * 512], ost, reads=[b_ost])
    except _Stop:
        pass
    for dsm in ALL_DSEMS:
        if dsm.count:
            nc.sync.wait_ge(dsm.sem, dsm.count)
    if ds_dbg.count:
        nc.sync.wait_ge(ds_dbg.sem, ds_dbg.count)
    barrier()
    es.close()
    return nc


def _prep_inputs(x, g_mix, w_in, A_re, A_im, log_dt, B_re, B_im, C_re, C_im, D_skip, w_glu, b_glu, w_pool,
                 pool_scale, sgu_ln_g, sgu_ln_b, w_spatial, b_spatial, w_out, g_ffn, w_gate, w_up, w_down, g_final):
    f = np.float32
    tok = _pos_to_tok()
    sperm = _chunk_perm()
    shared = {}
    colperm = np.concatenate([np.arange(0, 384), np.arange(640, 1024), np.arange(384, 640), np.arange(1024, 1408)])
    shared["w_in"] = np.ascontiguousarray(np.asarray(w_in, f)[:, :, colperm])
    shared["w_out"] = np.ascontiguousarray(np.asarray(w_out, f))
    shared["w_gate"] = np.ascontiguousarray(np.asarray(w_gate, f))
    shared["w_up"] = np.ascontiguousarray(np.asarray(w_up, f))
    shared["w_down"] = np.ascontiguousarray(np.asarray(w_down, f))
    shared["w_glu"] = np.ascontiguousarray(np.asarray(w_glu, f))
    gs = [np.asarray(g_mix, f)[0], np.asarray(g_ffn, f)[0], np.asarray(g_mix, f)[1], np.asarray(g_ffn, f)[1], np.asarray(g_final, f)]
    shared["gcat"] = np.ascontiguousarray(np.stack([g.reshape(8, 128).T for g in gs], axis=1))

    def gn(a):
        a = np.asarray(a, f).reshape(2, 12, 2, 64)
        return a.transpose(0, 2, 3, 1).reshape(2, 128, 12)
    ldt = np.broadcast_to(np.asarray(log_dt, f)[:, :, None], (2, 24, 64))
    shared["ssm_s"] = np.ascontiguousarray(np.stack([gn(A_re), gn(A_im), gn(ldt)], axis=2))

    def gB(a):
        a = np.asarray(a, f).reshape(2, 12, 2, 64, 16)
        return a.transpose(0, 2, 3, 1, 4).reshape(2, 128, 12, 16)

    def gC(a):
        a = np.asarray(a, f).reshape(2, 12, 2, 16, 64)
        return a.transpose(0, 2, 4, 1, 3).reshape(2, 128, 12, 16)
    shared["ssm_bc"] = np.ascontiguousarray(np.stack([gB(B_re), gB(B_im), gC(C_re), gC(C_im)], axis=2))
    vec = np.concatenate([np.asarray(D_skip, f).reshape(2, 3, 128).transpose(0, 2, 1),
                          np.asarray(b_glu, f).reshape(2, 3, 128).transpose(0, 2, 1),
                          np.asarray(pool_scale, f).reshape(2, 2, 128).transpose(0, 2, 1),
                          np.asarray(sgu_ln_g, f).reshape(2, 3, 128).transpose(0, 2, 1)], axis=2)
    shared["vecs"] = np.ascontiguousarray(vec)
    shared["lnb"] = np.ascontiguousarray(np.broadcast_to(np.asarray(sgu_ln_b, f)[:, None, :], (2, 128, 384)))
    wp = np.zeros((2, 128, 2, 128), f)
    wpn = np.asarray(w_pool, f)
    for gp in range(2):
        for gg in range(2):
            wp[:, 64 * gg:64 * gg + 64, gp, 64 * gg:64 * gg + 64] = wpn[:, 2 * gp + gg]
    shared["wpool"] = wp
    ws = np.asarray(w_spatial, f)
    wsp = ws[:, :, sperm][:, :, :, sperm]
    shared["wsT"] = np.ascontiguousarray(wsp.transpose(0, 3, 1, 2))
    shared["bsp"] = np.ascontiguousarray(np.asarray(b_spatial, f)[:, :, sperm][:, None])
    shared["cmask"] = (sperm[:, None] <= sperm[None, :]).astype(f)
    shared["ident"] = np.eye(128, dtype=f)

    def permT(mat):
        return mat[:, sperm][:, :, sperm].transpose(2, 0, 1)
    gen_m, gen_h = _pool_mats(False)
    fst_m, fst_h = _pool_mats(True)
    xs = np.asarray(x, f)
    in_maps = []
    for c in range(NCORES):
        b, p = c // 4, c % 4
        xc = xs[b, p * T:(p + 1) * T][tok]
        xT = np.ascontiguousarray(xc.T.reshape(8, 128, T).transpose(1, 0, 2))
        m0, h0 = (fst_m, fst_h) if p == 0 else (gen_m, gen_h)
        pm = np.concatenate([permT(gen_m), permT(gen_h), permT(m0), permT(h0)], axis=1)
        ohv = np.zeros((128, 3, 8), f)
        for d in range(1, 4):
            if p - d >= 0:
                ohv[:, d - 1, c - d] = 1.0
        mp = dict(shared)
        mp["xT"] = xT
        mp["pmat"] = np.ascontiguousarray(pm.astype(f))
        mp["oh"] = ohv
        in_maps.append(mp)
    return in_maps, tok


def _assemble(results, tok, key="outT"):
    out = np.zeros((2, 4 * T, DM), np.float32)
    inv = np.empty_like(tok)
    inv[tok] = np.arange(T)
    for c in range(NCORES):
        b, p = c // 4, c % 4
        oT = np.asarray(results[c][key])
        xc = oT.transpose(1, 0, 2).reshape(DM, T).T
        out[b, p * T:(p + 1) * T] = xc[inv]
    return out


def kernel(**inputs):
    in_maps, tok = _prep_inputs(**inputs)
    nc = build_program()
    res = run_bass_kernel_spmd(nc, in_maps, core_ids=list(range(NCORES)))
    return _assemble(res.results, tok)
```
